# Optimizing a Trainium2 kernel written in Bass

```python
import math
import jax, jax.numpy as jnp
from jax import lax
import numpy as np

D_MODEL = 1024
BATCH = 16
SEQ = 256
DEPTH = 4
DEC_BATCH = 4
DEC_SEQ = 1024
PAST_LEN = 512

GRID_W = 64
MIX_W = D_MODEL
N_MIXERS = 3
QBLK = 128
ROPE_BASE = 10000.0
NORM_EPS = 1e-6
HQ_A = 16
HKV_A = 4
G_A = HQ_A // HKV_A
DH_A = MIX_W // HQ_A
WINDOW = 128
WBLK = WINDOW
H_B = 8
DH_B = MIX_W // (2 * H_B)
H_C = 8
DK_C = MIX_W // H_C
DV_C = MIX_W // H_C
CONV_K = 3
CHUNK = 64
IN_A = HQ_A * DH_A + 2 * HKV_A * DH_A + MIX_W
IN_B = 4 * MIX_W
IN_C = 4 * MIX_W + 4 * H_C

kernel_name = "hybrid_diffusion_prefix_trunk_step"

F32 = jnp.float32


def rmsnorm(x, w):
    xf = x.astype(F32)
    y = xf * lax.rsqrt(jnp.mean(xf * xf, axis=-1, keepdims=True) + NORM_EPS)
    return (y * w.astype(F32)).astype(x.dtype)


def l2norm(x):
    xf = x.astype(F32)
    return (xf * lax.rsqrt(jnp.sum(xf * xf, axis=-1, keepdims=True) + 1e-6)).astype(x.dtype)


def ada_modulate(x, norm_w, mod_w, mod_b, cond):
    m = jax.nn.silu(cond) @ mod_w + mod_b
    shift, scale, gate = jnp.split(m, 3, axis=-1)
    h = rmsnorm(x, norm_w) * (1 + scale[:, None, :]) + shift[:, None, :]
    return h, gate[:, None, :]


def axial_angles(n, dh):
    rows = n // GRID_W
    row = jnp.repeat(jnp.arange(rows), GRID_W).astype(F32)
    col = jnp.tile(jnp.arange(GRID_W), rows).astype(F32)
    quarter = dh // 4
    inv = ROPE_BASE ** (-jnp.arange(quarter, dtype=F32) / quarter)
    return row[:, None] * inv, col[:, None] * inv


def rope_1d(x, ang):
    d2 = x.shape[-1] // 2
    cos = jnp.cos(ang)[None, :, None, :].astype(x.dtype)
    sin = jnp.sin(ang)[None, :, None, :].astype(x.dtype)
    x1, x2 = x[..., :d2], x[..., d2:]
    return jnp.concatenate([x1 * cos - x2 * sin, x2 * cos + x1 * sin], axis=-1)


def axial_rope(x, ang_row, ang_col):
    h = x.shape[-1] // 2
    return jnp.concatenate([rope_1d(x[..., :h], ang_row), rope_1d(x[..., h:], ang_col)], axis=-1)


def query_blocks(fn, q):
    b, n = q.shape[:2]
    qb = jnp.moveaxis(q.reshape((b, n // QBLK, QBLK) + q.shape[2:]), 1, 0)
    o = jnp.moveaxis(lax.map(fn, qb), 0, 1)
    return o.reshape((b, n) + o.shape[3:])


def sink_softmax(s, sink):
    m = jnp.maximum(jnp.max(s, axis=-1, keepdims=True), sink)
    e = jnp.exp(s - m)
    return e / (jnp.sum(e, axis=-1, keepdims=True) + jnp.exp(sink - m))


def sink_attention(q, k, v, sink):
    scale = DH_A ** -0.5
    sk = sink.astype(F32).reshape(1, HKV_A, G_A, 1, 1)

    def block(qb):
        b, nq = qb.shape[:2]
        qg = qb.reshape(b, nq, HKV_A, G_A, DH_A)
        s = jnp.einsum('bqhgd,bkhd->bhgqk', qg, k).astype(F32) * scale
        p = sink_softmax(s, sk).astype(v.dtype)
        return jnp.einsum('bhgqk,bkhd->bqhgd', p, v).reshape(b, nq, HQ_A, DH_A)

    return query_blocks(block, q)


def banded_window_attention(q, k, v, kc, vc, sink):
    b, n = q.shape[:2]
    nb = n // WBLK
    scale = DH_A ** -0.5
    qb = q.reshape(b, nb, WBLK, HKV_A, G_A, DH_A)
    pad = ((0, 0), (WBLK, WBLK), (0, 0), (0, 0))
    kp = jnp.pad(k, pad).reshape(b, nb + 2, WBLK, HKV_A, DH_A)
    vp = jnp.pad(v, pad).reshape(b, nb + 2, WBLK, HKV_A, DH_A)
    kw = jnp.concatenate([kp[:, :-2], kp[:, 1:-1], kp[:, 2:]], axis=2)
    vw = jnp.concatenate([vp[:, :-2], vp[:, 1:-1], vp[:, 2:]], axis=2)
    blk = jnp.arange(nb)[:, None, None]
    qi = blk * WBLK + jnp.arange(WBLK)[None, :, None]
    kj = (blk - 1) * WBLK + jnp.arange(3 * WBLK)[None, None, :]
    valid = (jnp.abs(kj - qi) <= WINDOW) & (kj >= 0) & (kj < n)
    s_loc = jnp.einsum('bnqhgd,bnkhd->bnhgqk', qb, kw).astype(F32) * scale
    s_loc = jnp.where(valid[None, :, None, None, :, :], s_loc, -jnp.inf)
    s_ctx = jnp.einsum('bnqhgd,bchd->bnhgqc', qb, kc).astype(F32) * scale
    s = jnp.concatenate([s_loc, s_ctx], axis=-1)
    p = sink_softmax(s, sink.astype(F32).reshape(1, 1, HKV_A, G_A, 1, 1)).astype(v.dtype)
    o = (jnp.einsum('bnhgqk,bnkhd->bnqhgd', p[..., :3 * WBLK], vw)
         + jnp.einsum('bnhgqc,bchd->bnqhgd', p[..., 3 * WBLK:], vc))
    return o.reshape(b, n, HQ_A, DH_A)


def mixer_a_project(h, in_w):
    b, n, _ = h.shape
    pr = h @ in_w
    nq, nkv = HQ_A * DH_A, HKV_A * DH_A
    q = pr[..., :nq].reshape(b, n, HQ_A, DH_A)
    k = pr[..., nq:nq + nkv].reshape(b, n, HKV_A, DH_A)
    v = pr[..., nq + nkv:nq + 2 * nkv].reshape(b, n, HKV_A, DH_A)
    z = pr[..., nq + 2 * nkv:]
    return q, k, v, z


def mixer_a_context(h, p):
    b, n, _ = h.shape
    q, k, v, z = mixer_a_project(h, p["in_w"])
    o = sink_attention(q, k, v, p["sink"])
    out = (o.reshape(b, n, MIX_W) * jax.nn.silu(z)) @ p["out_w"]
    return out, k, v


def mixer_a_latent(h, p, kc, vc):
    b, n, _ = h.shape
    q, k, v, z = mixer_a_project(h, p["in_w"])
    ar, ac = axial_angles(n, DH_A)
    q = axial_rope(q, ar, ac)
    k = axial_rope(k, ar, ac)
    o = banded_window_attention(q, k, v, kc, vc, p["sink"])
    return (o.reshape(b, n, MIX_W) * jax.nn.silu(z)) @ p["out_w"]


def mixer_b_project(h, in_w):
    b, n, _ = h.shape
    pr = h @ in_w
    q = pr[..., :MIX_W].reshape(b, n, H_B, 2, DH_B)
    k = pr[..., MIX_W:2 * MIX_W].reshape(b, n, H_B, 2, DH_B)
    v = pr[..., 2 * MIX_W:3 * MIX_W].reshape(b, n, H_B, 2 * DH_B)
    z = pr[..., 3 * MIX_W:]
    return q, k, v, z


def diff_lambda(p, lam_init):
    dot_exp = lambda a, c: jnp.exp(jnp.sum(a.astype(F32) * c.astype(F32)))
    return dot_exp(p["lq1"], p["lk1"]) - dot_exp(p["lq2"], p["lk2"]) + lam_init


def diff_attention(q, k, v, lam):
    scale = DH_B ** -0.5

    def block(qb):
        s = jnp.einsum('bqhcd,bkhcd->bhcqk', qb, k).astype(F32) * scale
        pr = jax.nn.softmax(s, axis=-1)
        a = (pr[:, :, 0] - lam * pr[:, :, 1]).astype(v.dtype)
        return jnp.einsum('bhqk,bkhe->bqhe', a, v)

    return query_blocks(block, q)


def mixer_b_out(o, z, p, lam_init):
    b, n = o.shape[:2]
    o = rmsnorm(o, p["subln_w"]) * (1.0 - lam_init)
    return (o.reshape(b, n, MIX_W) * jax.nn.silu(z)) @ p["out_w"]


def mixer_b_context(h, p, lam_init):
    q, k, v, z = mixer_b_project(h, p["in_w"])
    o = diff_attention(q, k, v, diff_lambda(p, lam_init))
    return mixer_b_out(o, z, p, lam_init), k, v


def mixer_b_latent(h, p, lam_init, kc, vc):
    b, n, _ = h.shape
    q, k, v, z = mixer_b_project(h, p["in_w"])
    ar, ac = axial_angles(n, DH_B)
    rot = lambda t: axial_rope(t.reshape(b, n, 2 * H_B, DH_B), ar, ac).reshape(b, n, H_B, 2, DH_B)
    q, k = rot(q), rot(k)
    k_all = jnp.concatenate([k, kc], axis=1)
    v_all = jnp.concatenate([v, vc], axis=1)
    o = diff_attention(q, k_all, v_all, diff_lambda(p, lam_init))
    return mixer_b_out(o, z, p, lam_init)


def depthwise_conv_centred(x, w):
    ch = x.shape[-1]
    return lax.conv_general_dilated(
        x, w[:, None, :], window_strides=(1,), padding=((CONV_K // 2, CONV_K // 2),),
        dimension_numbers=('NWC', 'WIO', 'NWC'), feature_group_count=ch)


def gated_delta_chunked(q, k, v, g, beta, s0):
    dt = v.dtype
    q, k, v, g, beta, s0 = (t.astype(F32) for t in (q, k, v, g, beta, s0))
    b, n, h, dk = q.shape
    dv = v.shape[-1]
    nc = n // CHUNK

    def to_chunks(t):
        t = t.reshape((b, nc, CHUNK, h) + t.shape[3:])
        return jnp.moveaxis(jnp.moveaxis(t, 1, 0), 2, 3)

    qc, kc, vc, gc, bc = (to_chunks(t) for t in (q, k, v, g, beta))
    gcum = jnp.cumsum(gc, axis=-1)
    idx = jnp.arange(CHUNK)
    incl = idx[:, None] >= idx[None, :]
    strict = idx[:, None] > idx[None, :]
    decay = jnp.exp(jnp.where(incl, gcum[..., :, None] - gcum[..., None, :], -jnp.inf))
    kb = kc * bc[..., None]
    lmat = jnp.where(strict, jnp.einsum('...id,...jd->...ij', kb, kc) * decay, 0.0)
    eye = jnp.eye(CHUNK, dtype=F32)
    rhs = jnp.concatenate([vc * bc[..., None], kb * jnp.exp(gcum)[..., None]], axis=-1)
    sol = lax.linalg.triangular_solve(eye + lmat, rhs, left_side=True, lower=True)
    u, w = sol[..., :dv], sol[..., dv:]

    def step(state, inp):
        qi, ki, ui, wi, gi, di = inp
        intra = jnp.where(incl, jnp.einsum('bhid,bhjd->bhij', qi, ki) * di, 0.0)
        vnew = ui - jnp.einsum('bhck,bhkv->bhcv', wi, state)
        o = (jnp.einsum('bhck,bhkv->bhcv', qi * jnp.exp(gi)[..., None], state)
             + jnp.einsum('bhij,bhjv->bhiv', intra, vnew))
        glast = gi[..., -1:]
        state = (state * jnp.exp(glast)[..., None]
                 + jnp.einsum('bhck,bhcv->bhkv', ki * jnp.exp(glast - gi)[..., None], vnew))
        return state, o

    s_fin, o = lax.scan(step, s0, (qc, kc, u, w, gcum, decay))
    o = jnp.moveaxis(jnp.moveaxis(o, 3, 2), 0, 1).reshape(b, n, h, dv)
    return o.astype(dt), s_fin.astype(dt)


def mixer_c(h, p, s0_f, s0_b):
    b, n, _ = h.shape
    pr = h @ p["in_w"]
    qkv = jax.nn.silu(depthwise_conv_centred(pr[..., :3 * MIX_W], p["conv_w"]))
    z = pr[..., 3 * MIX_W:4 * MIX_W]
    bgate = pr[..., 4 * MIX_W:4 * MIX_W + 2 * H_C].reshape(b, n, 2, H_C)
    agate = pr[..., 4 * MIX_W + 2 * H_C:].reshape(b, n, 2, H_C)
    q = l2norm(qkv[..., :MIX_W].reshape(b, n, H_C, DK_C)) * (DK_C ** -0.5)
    k = l2norm(qkv[..., MIX_W:2 * MIX_W].reshape(b, n, H_C, DK_C))
    v = qkv[..., 2 * MIX_W:].reshape(b, n, H_C, DV_C)
    beta = jax.nn.sigmoid(bgate.astype(F32))
    g = -jnp.exp(p["a_log"].astype(F32)) * jax.nn.softplus(agate.astype(F32) + p["dt_bias"].astype(F32))
    o_f, s_f = gated_delta_chunked(q, k, v, g[:, :, 0], beta[:, :, 0], s0_f)
    rev = lambda t: jnp.flip(t, axis=1)
    o_b, s_b = gated_delta_chunked(rev(q), rev(k), rev(v), rev(g[:, :, 1]), rev(beta[:, :, 1]), s0_b)
    o = rmsnorm(o_f + rev(o_b), p["onorm_w"])
    out = (o.reshape(b, n, MIX_W) * jax.nn.silu(z)) @ p["out_w"]
    return out, jnp.stack([s_f, s_b], axis=1)


def lambda_init_for(layer):
    return 0.8 - 0.6 * math.exp(-0.3 * layer)


def setup_inputs(seed: int = 0) -> dict:
    key = jax.random.key(seed)
    keys = iter(jax.random.split(key, 64))
    nrm = lambda shape, s: jax.random.normal(next(keys), shape, F32) * s
    gain = lambda m: 1.0 + nrm((m,), 0.02)
    d = D_MODEL
    inp = {}
    inp["x_prompt"] = nrm((BATCH, SEQ, d), 1.0)
    inp["x_sample"] = nrm((DEC_BATCH, DEC_SEQ, d), 1.0)
    inp["cache_l0_k"] = nrm((DEC_BATCH, PAST_LEN, HKV_A, DH_A), 1.0)
    inp["cache_l0_v"] = nrm((DEC_BATCH, PAST_LEN, HKV_A, DH_A), 1.0)
    inp["cache_l1_k"] = nrm((DEC_BATCH, PAST_LEN, H_B, 2, DH_B), 1.0)
    inp["cache_l1_v"] = nrm((DEC_BATCH, PAST_LEN, H_B, 2 * DH_B), 1.0)
    inp["state_l2"] = nrm((DEC_BATCH, 2, H_C, DK_C, DV_C), 0.3)
    inp["cache_l3_k"] = nrm((DEC_BATCH, PAST_LEN, HKV_A, DH_A), 1.0)
    inp["cache_l3_v"] = nrm((DEC_BATCH, PAST_LEN, HKV_A, DH_A), 1.0)
    inp["c"] = nrm((DEC_BATCH, d), 1.0)
    inp["c_ctx"] = nrm((d,), 1.0)

    def common(i, in_dim):
        inp[f"l{i}_norm_w"] = gain(d)
        inp[f"l{i}_mod_w"] = nrm((d, 3 * d), 0.5 * d ** -0.5)
        inp[f"l{i}_mod_b"] = nrm((3 * d,), 0.01)
        inp[f"l{i}_in_w"] = nrm((d, in_dim), d ** -0.5)
        inp[f"l{i}_out_w"] = nrm((MIX_W, d), MIX_W ** -0.5)

    common(0, IN_A)
    inp["l0_sink"] = nrm((HQ_A,), 0.5)
    common(1, IN_B)
    inp["l1_lambda_q1"] = nrm((DH_B,), 0.1)
    inp["l1_lambda_k1"] = nrm((DH_B,), 0.1)
    inp["l1_lambda_q2"] = nrm((DH_B,), 0.1)
    inp["l1_lambda_k2"] = nrm((DH_B,), 0.1)
    inp["l1_subln_w"] = gain(2 * DH_B)
    common(2, IN_C)
    inp["l2_conv_w"] = nrm((CONV_K, 3 * MIX_W), CONV_K ** -0.5)
    inp["l2_a_log"] = jnp.log(jax.random.uniform(next(keys), (2, H_C), F32, 1.0, 16.0))
    dtv = jnp.exp(jax.random.uniform(next(keys), (2, H_C), F32, math.log(1e-3), math.log(1e-1)))
    inp["l2_dt_bias"] = dtv + jnp.log(-jnp.expm1(-dtv))
    inp["l2_onorm_w"] = gain(DV_C)
    common(3, IN_A)
    inp["l3_sink"] = nrm((HQ_A,), 0.5)
    inp["final_norm_w"] = gain(d)
    return inp


def reference(x_prompt, x_sample, cache_l0_k, cache_l0_v, cache_l1_k, cache_l1_v, state_l2,
              cache_l3_k, cache_l3_v, c, c_ctx,
              l0_norm_w, l0_mod_w, l0_mod_b, l0_in_w, l0_out_w, l0_sink,
              l1_norm_w, l1_mod_w, l1_mod_b, l1_in_w, l1_out_w,
              l1_lambda_q1, l1_lambda_k1, l1_lambda_q2, l1_lambda_k2, l1_subln_w,
              l2_norm_w, l2_mod_w, l2_mod_b, l2_in_w, l2_out_w,
              l2_conv_w, l2_a_log, l2_dt_bias, l2_onorm_w,
              l3_norm_w, l3_mod_w, l3_mod_b, l3_in_w, l3_out_w, l3_sink,
              final_norm_w):
    layers = [
        dict(norm_w=l0_norm_w, mod_w=l0_mod_w, mod_b=l0_mod_b, in_w=l0_in_w, out_w=l0_out_w, sink=l0_sink),
        dict(norm_w=l1_norm_w, mod_w=l1_mod_w, mod_b=l1_mod_b, in_w=l1_in_w, out_w=l1_out_w,
             lq1=l1_lambda_q1, lk1=l1_lambda_k1, lq2=l1_lambda_q2, lk2=l1_lambda_k2, subln_w=l1_subln_w),
        dict(norm_w=l2_norm_w, mod_w=l2_mod_w, mod_b=l2_mod_b, in_w=l2_in_w, out_w=l2_out_w,
             conv_w=l2_conv_w, a_log=l2_a_log, dt_bias=l2_dt_bias, onorm_w=l2_onorm_w),
        dict(norm_w=l3_norm_w, mod_w=l3_mod_w, mod_b=l3_mod_b, in_w=l3_in_w, out_w=l3_out_w, sink=l3_sink),
    ]
    caches = [(cache_l0_k, cache_l0_v), (cache_l1_k, cache_l1_v), (state_l2,), (cache_l3_k, cache_l3_v)]

    xc, xl = x_prompt, x_sample
    new_state = []
    for i in range(DEPTH):
        p = layers[i]
        hc, gate_c = ada_modulate(xc, p["norm_w"], p["mod_w"], p["mod_b"], c_ctx[None, :])
        hl, gate_l = ada_modulate(xl, p["norm_w"], p["mod_w"], p["mod_b"], c)
        kind = i % N_MIXERS
        if kind == 0:
            oc, kc_new, vc_new = mixer_a_context(hc, p)
            ol = mixer_a_latent(hl, p, caches[i][0], caches[i][1])
            new_state += [kc_new, vc_new]
        elif kind == 1:
            lam_init = lambda_init_for(i)
            oc, kc_new, vc_new = mixer_b_context(hc, p, lam_init)
            ol = mixer_b_latent(hl, p, lam_init, caches[i][0], caches[i][1])
            new_state += [kc_new, vc_new]
        else:
            zeros = jnp.zeros((xc.shape[0], H_C, DK_C, DV_C), xc.dtype)
            oc, st_new = mixer_c(hc, p, zeros, zeros)
            ol, _ = mixer_c(hl, p, caches[i][0][:, 0], caches[i][0][:, 1])
            new_state.append(st_new)
        xc = xc + gate_c * oc
        xl = xl + gate_l * ol

    y_prompt = rmsnorm(xc, final_norm_w)
    y_sample = rmsnorm(xl, final_norm_w)
    new_l0_k, new_l0_v, new_l1_k, new_l1_v, new_l2_state, new_l3_k, new_l3_v = new_state
    return (y_prompt, y_sample, new_l0_k, new_l0_v, new_l1_k, new_l1_v, new_l2_state, new_l3_k, new_l3_v)
```

```python
import numpy as np
from contextlib import ExitStack
import concourse.bass as bass
import concourse.mybir as mybir
from concourse.bass_utils import run_bass_kernel_spmd

F32 = mybir.dt.float32
BF16 = mybir.dt.bfloat16
AF = mybir.ActivationFunctionType
ALU = mybir.AluOpType
AX = mybir.AxisListType

ENGS = ("pe", "act", "dve", "pool", "sp")
NEG = -30000.0


class T:
    def __init__(self, prog, name, shape, dtype, gran, psum=False):
        self.name = name
        self.shape = shape
        self.F = int(np.prod(shape[1:]))
        self.gran = gran
        self.nreg = (self.F + gran - 1) // gran
        self.w = [None] * self.nreg
        self.r = [[] for _ in range(self.nreg)]
        self.psum = psum
        if psum:
            self.t = prog.es.enter_context(prog.nc.psum_tensor("pp_" + name, list(shape), dtype))
        else:
            self.t = prog.es.enter_context(prog.nc.sbuf_tensor("sb_" + name, list(shape), dtype))

    def regs(self, a, b):
        return range(a // self.gran, (b - 1) // self.gran + 1)


class Acc:
    def __init__(self, t, ranges):
        self.t = t
        self.ranges = ranges


def acc(t, a=None, b=None):
    if a is None:
        return Acc(t, [(0, t.F)])
    return Acc(t, [(a, b)])


def acc3(t, kts, lo, hi):
    inner = t.shape[-1] if len(t.shape) == 3 else None
    return Acc(t, [(k * inner + lo, k * inner + hi) for k in kts])


class Prog:
    def __init__(self, nc, n_dma_sems=8):
        self.nc = nc
        self.es = ExitStack()
        self.ops = {e: [] for e in ENGS}
        self.sem = {}
        for e in ENGS:
            self.sem[e] = self.es.enter_context(nc.semaphore("s_" + e))
        self.cnt = {e: 0 for e in ENGS}
        self.waited = {e: {} for e in ENGS}
        self.dsem = [self.es.enter_context(nc.semaphore("d%d" % i)) for i in range(n_dma_sems)]
        self.dval = [0] * n_dma_sems
        self.dnext = 0
        self.final_events = []

    def tensor(self, name, shape, dtype, gran=None, psum=False):
        F = int(np.prod(shape[1:]))
        return T(self, name, shape, dtype, gran or F, psum)

    def _deps(self, reads, writes):
        deps = set()
        for a in reads:
            for (lo, hi) in a.ranges:
                for g in a.t.regs(lo, hi):
                    if a.t.w[g] is not None:
                        deps.add(a.t.w[g])
        for a in writes:
            for (lo, hi) in a.ranges:
                for g in a.t.regs(lo, hi):
                    if a.t.w[g] is not None:
                        deps.add(a.t.w[g])
                    for ev in a.t.r[g]:
                        deps.add(ev)
        return deps

    def _mark(self, reads, writes, ev):
        for a in reads:
            for (lo, hi) in a.ranges:
                for g in a.t.regs(lo, hi):
                    a.t.r[g].append(ev)
        for a in writes:
            for (lo, hi) in a.ranges:
                for g in a.t.regs(lo, hi):
                    a.t.w[g] = ev
                    a.t.r[g] = []

    def _waits(self, eng, deps, skip_same=False):
        best = {}
        for (k, v) in deps:
            if skip_same and k == eng:
                continue
            if v > best.get(k, 0):
                best[k] = v
        out = []
        for k, v in best.items():
            if self.waited[eng].get(k, 0) >= v:
                continue
            self.waited[eng][k] = v
            out.append((k, v))
        return out

    def _semh(self, k):
        if isinstance(k, str):
            return self.sem[k]
        if isinstance(k, tuple):
            return self._swh[k]
        return self.dsem[k]

    def op(self, eng, fn, reads=(), writes=(), skip_same=False):
        pr = [a for a in reads if a.t.psum]
        if pr:
            reads = [a for a in reads if not a.t.psum]
            writes = list(writes) + pr
        deps = self._deps(reads, writes)
        waits = self._waits(eng, deps, skip_same)
        self.cnt[eng] += 1
        ev = (eng, self.cnt[eng])
        self._mark(reads, writes, ev)
        semh = self.sem[eng]
        wl = [(self._semh(k), v) for (k, v) in waits]

        def emit(h):
            for (s, v) in wl:
                h.wait_ge(s, v)
            fn(h).then_inc(semh, 1)

        self.ops[eng].append(emit)
        return ev

    def dma(self, eng, out_ap, in_ap, reads=(), writes=(), is_output=False, slot=None):
        if eng == "pool":
            return self.dma_sw(out_ap, in_ap, reads, writes, slot)
        deps = self._deps(reads, writes)
        i = self.dnext
        self.dnext = (self.dnext + 1) % len(self.dsem)
        if self.dval[i] > 0:
            deps.add((i, self.dval[i]))
        waits = self._waits(eng, deps)
        self.dval[i] += 16
        ev = (i, self.dval[i])
        self._mark(reads, writes, ev)
        semh = self.dsem[i]
        wl = [(self._semh(k), v) for (k, v) in waits]

        def emit(h):
            for (s, v) in wl:
                h.wait_ge(s, v)
            h.dma_start(out=out_ap, in_=in_ap).then_inc(semh, 16)

        self.ops[eng].append(emit)
        if is_output:
            self.final_events.append(ev)
        return ev

    def dma_sw(self, out_ap, in_ap, reads, writes, slot):
        eng = "pool"
        if not hasattr(self, "_swh"):
            self._swh = {}
        n = len(self._swh)
        semh = self.es.enter_context(self.nc.semaphore("w%d" % n))
        key = ("sw", n)
        self._swh[key] = semh
        deps = self._deps(reads, writes)
        waits = self._waits(eng, deps)
        ev = (key, 16)
        self._mark(reads, writes, ev)
        wl = [(self._semh(k), v) for (k, v) in waits]

        def emit(h):
            for (s, v) in wl:
                h.wait_ge(s, v)
            h.dma_start(out=out_ap, in_=in_ap).then_inc(semh, 16)

        self.ops[eng].append(emit)
        return ev

    def finish(self):
        waits = self._waits("sp", set(self.final_events))
        wl = [(self._semh(k), v) for (k, v) in waits]

        def emit(h):
            for (s, v) in wl:
                h.wait_ge(s, v)

        self.ops["sp"].append(emit)
        nc = self.nc
        ops = self.ops
        with nc.Block() as block:
            @block.tensor
            def _(h):
                for f in ops["pe"]:
                    f(h)

            @block.scalar
            def _(h):
                for f in ops["act"]:
                    f(h)

            @block.vector
            def _(h):
                for f in ops["dve"]:
                    f(h)

            @block.gpsimd
            def _(h):
                for f in ops["pool"]:
                    f(h)

            @block.sync
            def _(h):
                for f in ops["sp"]:
                    f(h)
        self.es.close()


D = 1024
NT = 1024
KT = 8
IN_COLS = {0: 2560, 1: 4096, 2: 4128, 3: 2560}
KIND = {0: "a", 1: "b", 2: "c", 3: "a"}


def lambda_init_for(layer):
    import math
    return 0.8 - 0.6 * math.exp(-0.3 * layer)


class Builder:
    def __init__(self, layers=(0, 1, 2, 3), debug=False):
        self.layers = layers
        self.debug = debug
        self.nc = nc = bass.Bass("TRN2", target_bir_lowering=False)
        self.p = p = Prog(nc)
        self.din = {}
        self.dout = {}
        self.rr = {}
        self._alloc()

    def inp(self, name, shape, dtype=F32):
        if name not in self.din:
            self.din[name] = self.nc.dram_tensor(name, list(shape), dtype, kind="ExternalInput").ap()
        return self.din[name]

    def outp(self, name, shape, dtype=F32):
        if name not in self.dout:
            self.dout[name] = self.nc.dram_tensor(name, list(shape), dtype, kind="ExternalOutput").ap()
        return self.dout[name]

    def nxt(self, key, n):
        v = self.rr.get(key, 0)
        self.rr[key] = (v + 1) % n
        return v

    def _alloc(self):
        p = self.p
        self.xT = p.tensor("xT", [128, 8, 1024], F32, gran=128)
        self.hT = p.tensor("hT", [128, 8, 1024], BF16, gran=128)
        self.qT = p.tensor("qT", [128, 8, 1024], BF16, gran=128)
        self.kT = p.tensor("kT", [128, 8, 1024], BF16, gran=128)
        self.kcT = p.tensor("kcT", [128, 8, 512], BF16, gran=128)
        self.VB = p.tensor("VB", [128, 12, 1024], BF16, gran=128)
        self.zT = p.tensor("zT", [128, 8, 1024], BF16, gran=128)
        self.oT = self.hT
        self.NW = 2
        self.W = [p.tensor("W%d" % i, [128, 8, 1024], BF16) for i in range(self.NW)]
        self.rstd = p.tensor("rstd", [128, 1024], F32, gran=512)
        self.ropeC = p.tensor("ropeC", [128, 1024], F32, gran=128)
        self.ropeS = p.tensor("ropeS", [128, 1024], F32, gran=128)
        self.Rm = p.tensor("Rm", [128, 128], BF16)
        self.ones = p.tensor("ones", [128, 128], BF16)
        self.NPT = 4
        self.PT = [p.tensor("PT%d" % i, [128, 512], BF16, gran=128) for i in range(self.NPT)]
        self.NST = 3
        self.ST = [p.tensor("ST%d" % i, [128, 512], F32, gran=128) for i in range(self.NST)]
        self.NTMP = 4
        self.TMP = [p.tensor("TMP%d" % i, [128, 512], F32, gran=128) for i in range(self.NTMP)]
        self.maskP = p.tensor("maskP", [128, 8, 128], BF16)
        self.maskN = p.tensor("maskN", [128, 8, 128], BF16)
        self.ctxb = p.tensor("ctxb", [128, 1], F32)
        self.zero1 = p.tensor("zero1", [128, 1], F32)
        self.eps1 = p.tensor("eps1", [128, 1], F32)
        self.cond = p.tensor("cond", [128, 8], F32)
        self.scond = p.tensor("scond", [128, 8], BF16)
        self.vec = p.tensor("vec", [128, 64], F32)
        self.mT = p.tensor("mT", [128, 24], F32)
        self.gain = p.tensor("gain", [128, 8], F32)
        self.sink = p.tensor("sink", [128, 16], F32)
        self.esink = p.tensor("esink", [128, 16], F32)
        self.finw = p.tensor("finw", [128, 8], F32)
        self.PS = [p.tensor("ps%d" % i, [128, 512], F32, psum=True) for i in range(8)]
        self.wcount = 0

    def ps_proj(self):
        return self.PS[self.nxt("pj", 2)]

    def ps_rope(self):
        return self.PS[2]

    def ps_s(self):
        return self.PS[3 + self.nxt("s", 3)]

    def ps_o(self):
        return self.PS[6 + self.nxt("o", 2)]

    def plan_weights(self):
        plan = []
        for li in self.layers:
            for c0 in range(0, 3072, 1024):
                plan.append(("l%d_mod_w" % li, 3072, c0, 1024))
            nc_ = IN_COLS[li]
            for c0 in range(0, nc_, 1024):
                plan.append(("l%d_in_w" % li, nc_, c0, min(1024, nc_ - c0)))
            plan.append(("l%d_out_w" % li, 1024, 0, 1024))
        self.wplan = plan
        self.wissued = 0
        self.wused = 0

    def _issue_block(self):
        p = self.p
        wname, ncols, c0, ccount = self.wplan[self.wissued]
        w = self.inp(wname, [1024, ncols])
        buf = self.W[self.wissued % self.NW]
        self.wissued += 1
        src = w.rearrange("(kt q) c -> q kt c", q=128)[:, :, c0:c0 + ccount]
        p.dma("pool", buf.t[:, :, 0:ccount], src, writes=[acc(buf)])

    def next_block(self, wname, c0):
        i = self.wused
        assert self.wplan[i][0] == wname and self.wplan[i][2] == c0, (self.wplan[i], wname, c0)
        while self.wissued <= min(i + 1, len(self.wplan) - 1):
            self._issue_block()
        self.wused += 1
        return self.W[i % self.NW]

    def setup(self):
        p = self.p
        xin = self.inp("xT", [128, 8, 1024])
        for kt in range(8):
            p.dma("sp", self.xT.t[:, kt, :], xin[:, kt, :], writes=[acc3(self.xT, [kt], 0, 1024)])
        p.dma("sp", self.cond.t[:, :], self.inp("cond", [128, 8]), writes=[acc(self.cond)])
        p.dma("sp", self.ropeC.t[:, :], self.inp("ropeC", [128, 1024]), writes=[acc(self.ropeC)])
        p.dma("sp", self.ropeS.t[:, :], self.inp("ropeS", [128, 1024]), writes=[acc(self.ropeS)])
        p.dma("sp", self.ctxb.t[:, :], self.inp("ctxb", [128, 1]), writes=[acc(self.ctxb)])
        p.dma("sp", self.finw.t[:, :], self.inp("finw", [128, 8]), writes=[acc(self.finw)])
        p.dma("pool", self.Rm.t[:, :], self.inp("Rm", [128, 128]), writes=[acc(self.Rm)])
        p.dma("pool", self.maskP.t[:, :, :], self.inp("maskP", [128, 8, 128]), writes=[acc(self.maskP)])
        p.dma("pool", self.maskN.t[:, :, :], self.inp("maskN", [128, 8, 128]), writes=[acc(self.maskN)])
        p.op("dve", lambda h: h.memset(self.ones.t[:, :], 1.0), writes=[acc(self.ones)])
        p.op("dve", lambda h: h.memset(self.zero1.t[:, :], 0.0), writes=[acc(self.zero1)])
        p.op("dve", lambda h: h.memset(self.eps1.t[:, :], 1e-6), writes=[acc(self.eps1)])
        p.op("act", lambda h: h.activation(self.scond.t[:, :], self.cond.t[:, :], AF.Silu),
             reads=[acc(self.cond)], writes=[acc(self.scond)])

    def compute_rstd(self):
        p = self.p
        sq = self.hT
        for c in range(2):
            lo, hi = c * 512, (c + 1) * 512
            for kt in range(8):
                p.op("act", lambda h, kt=kt, lo=lo, hi=hi: h.activation(sq.t[:, kt, lo:hi], self.xT.t[:, kt, lo:hi], AF.Square),
                     reads=[acc3(self.xT, [kt], lo, hi)], writes=[acc3(sq, [kt], lo, hi)])
            ps = self.ps_proj()
            for kt in range(8):
                p.op("pe", lambda h, kt=kt, lo=lo, hi=hi, ps=ps: h.matmul(ps.t[:, :], self.ones.t[:, :], sq.t[:, kt, lo:hi],
                                                                     start=(kt == 0), stop=(kt == 7)),
                     reads=[acc(self.ones), acc3(sq, [kt], lo, hi)], writes=[acc(ps)], skip_same=True)
            p.op("act", lambda h, ps=ps, lo=lo, hi=hi: h.activation(self.rstd.t[:, lo:hi], ps.t[:, :], AF.Sqrt,
                                                               bias=self.eps1.t[:, 0:1], scale=1.0 / 1024.0),
                 reads=[acc(ps), acc(self.eps1)], writes=[acc(self.rstd, lo, hi)])
            p.op("dve", lambda h, lo=lo, hi=hi: h.reciprocal(self.rstd.t[:, lo:hi], self.rstd.t[:, lo:hi]),
                 reads=[acc(self.rstd, lo, hi)], writes=[acc(self.rstd, lo, hi)])

    def modulate(self, li):
        p = self.p
        vec = self.vec
        p.dma("sp", vec.t[:, 0:8], self.inp("l%d_norm_w" % li, [128, 8]), writes=[acc(vec)])
        p.dma("sp", vec.t[:, 8:32], self.inp("l%d_mod_b" % li, [128, 24]), writes=[acc(vec)])
        self.compute_rstd()
        ps = self.PS[2]
        for b in range(3):
            buf = self.next_block("l%d_mod_w" % li, b * 1024)
            for cl in range(8):
                c = b * 8 + cl
                for kt in range(8):
                    p.op("pe", lambda h, buf=buf, cl=cl, c=c, kt=kt: h.matmul(
                        ps.t[:, c:c + 1], buf.t[:, kt, cl * 128:(cl + 1) * 128], self.scond.t[:, kt:kt + 1],
                        start=(kt == 0), stop=(kt == 7)),
                        reads=[acc(buf), acc(self.scond)], writes=[acc(ps)], skip_same=True)
        p.op("dve", lambda h: h.tensor_tensor(self.mT.t[:, :], ps.t[:, 0:24], vec.t[:, 8:32], ALU.add),
             reads=[acc(ps), acc(vec)], writes=[acc(self.mT)])
        p.op("dve", lambda h: h.scalar_tensor_tensor(self.gain.t[:, :], self.mT.t[:, 8:16], 1.0, vec.t[:, 0:8],
                                                     ALU.add, ALU.mult),
             reads=[acc(self.mT), acc(vec)], writes=[acc(self.gain)])
        for kt in range(8):
            for c in range(2):
                lo, hi = c * 512, (c + 1) * 512
                tmp = self.TMP[self.nxt("tmp", self.NTMP)]
                p.op("dve", lambda h, kt=kt, lo=lo, hi=hi, tmp=tmp: h.scalar_tensor_tensor(
                    tmp.t[:, :], self.xT.t[:, kt, lo:hi], self.gain.t[:, kt:kt + 1], self.rstd.t[:, lo:hi],
                    ALU.mult, ALU.mult),
                    reads=[acc3(self.xT, [kt], lo, hi), acc(self.gain), acc(self.rstd, lo, hi)], writes=[acc(tmp)])
                p.op("act", lambda h, kt=kt, lo=lo, hi=hi, tmp=tmp: h.activation(
                    self.hT.t[:, kt, lo:hi], tmp.t[:, :], AF.Identity, bias=self.mT.t[:, kt:kt + 1], scale=1.0),
                    reads=[acc(tmp), acc(self.mT)], writes=[acc3(self.hT, [kt], lo, hi)])

    def proj_tile(self, buf, cl, c, src=None):
        p = self.p
        src = src or self.hT
        ps = self.ps_proj()
        lo, hi = c * 512, (c + 1) * 512
        for kt in range(8):
            p.op("pe", lambda h, kt=kt: h.matmul(ps.t[:, :], buf.t[:, kt, cl * 128:(cl + 1) * 128], src.t[:, kt, lo:hi],
                                                 start=(kt == 0), stop=(kt == 7)),
                 reads=[acc(buf), acc3(src, [kt], lo, hi)], writes=[acc(ps)], skip_same=True)
        return ps

    def rope_evac(self, ps, dst, dt_, c):
        p = self.p
        lo, hi = c * 512, (c + 1) * 512
        qb = self.PT[self.nxt("pt", self.NPT)]
        p.op("act", lambda h: h.copy(qb.t[:, :], ps.t[:, :]), reads=[acc(ps)], writes=[acc(qb)])
        pr = self.ps_rope()
        p.op("pe", lambda h: h.matmul(pr.t[:, :], self.Rm.t[:, :], qb.t[:, :], start=True, stop=True),
             reads=[acc(self.Rm), acc(qb)], writes=[acc(pr)], skip_same=True)
        t1 = self.TMP[self.nxt("tmp", self.NTMP)]
        t2 = self.TMP[self.nxt("tmp", self.NTMP)]
        p.op("dve", lambda h: h.tensor_tensor(t1.t[:, :], ps.t[:, :], self.ropeC.t[:, lo:hi], ALU.mult),
             reads=[acc(ps), acc(self.ropeC)], writes=[acc(t1)])
        p.op("dve", lambda h: h.tensor_tensor(t2.t[:, :], pr.t[:, :], self.ropeS.t[:, lo:hi], ALU.mult),
             reads=[acc(pr), acc(self.ropeS)], writes=[acc(t2)])
        p.op("pool", lambda h: h.tensor_tensor(dst.t[:, dt_, lo:hi], t1.t[:, :], t2.t[:, :], ALU.add),
             reads=[acc(t1), acc(t2)], writes=[acc3(dst, [dt_], lo, hi)])

    def out_proj(self, li):
        p = self.p
        for b in range(1):
            buf = self.next_block("l%d_out_w" % li, 0)
            for cl in range(8):
                mt = cl
                for c in range(2):
                    lo, hi = c * 512, (c + 1) * 512
                    ps = self.proj_tile(buf, cl, c, src=self.oT)
                    p.op("dve", lambda h, ps=ps, mt=mt, lo=lo, hi=hi: h.scalar_tensor_tensor(
                        self.xT.t[:, mt, lo:hi], ps.t[:, :], self.mT.t[:, 16 + mt:17 + mt], self.xT.t[:, mt, lo:hi],
                        ALU.mult, ALU.add),
                        reads=[acc(ps), acc(self.mT), acc3(self.xT, [mt], lo, hi)], writes=[acc3(self.xT, [mt], lo, hi)])

    def final(self):
        p = self.p
        self.compute_rstd()
        yT = self.outp("yT", [128, 8, 1024])
        for kt in range(8):
            for c in range(2):
                lo, hi = c * 512, (c + 1) * 512
                st = self.ST[self.nxt("st", self.NST)]
                p.op("dve", lambda h, kt=kt, lo=lo, hi=hi, st=st: h.scalar_tensor_tensor(
                    st.t[:, :], self.xT.t[:, kt, lo:hi], self.finw.t[:, kt:kt + 1], self.rstd.t[:, lo:hi],
                    ALU.mult, ALU.mult),
                    reads=[acc3(self.xT, [kt], lo, hi), acc(self.finw), acc(self.rstd, lo, hi)], writes=[acc(st)])
                p.dma("sp", yT[:, kt, lo:hi], st.t[:, :], reads=[acc(st)], is_output=True)

    def dump_x(self, name):
        out = self.outp(name, [128, 8, 1024])
        for kt in range(8):
            self.p.dma("sp", out[:, kt, :], self.xT.t[:, kt, :], reads=[acc3(self.xT, [kt], 0, 1024)], is_output=True)


def layer_a(self, li):
    p = self.p
    self.modulate(li)
    p.dma("sp", self.sink.t[:, :], self.inp("l%d_sink" % li, [128, 16]), writes=[acc(self.sink)])
    p.op("act", lambda h: h.activation(self.esink.t[:, :], self.sink.t[:, :], AF.Exp),
         reads=[acc(self.sink)], writes=[acc(self.esink)])
    kc = self.inp("l%d_kcT" % li, [128, 2, 512])
    p.dma("pool", self.kcT.t[:, 0:2, :], kc, writes=[acc3(self.kcT, [0, 1], 0, 512)], slot="kcT")
    vc = self.inp("l%d_vc" % li, [512, 256])
    VB = self.VB
    for kt in range(12):
        v4 = VB.t[:, kt, 0:512].rearrange("q (g e) -> q g e", g=4)
        p.op("pool", lambda h, v4=v4: h.memset(v4[:, :, 64:128], 1.0), writes=[acc3(VB, [kt], 0, 512)])
    for t in range(4):
        v4 = VB.t[:, 8 + t, 0:512].rearrange("q (g e) -> q g e", g=4)
        p.dma("pool", v4[:, :, 0:64], vc[t * 128:(t + 1) * 128, :].rearrange("q (g e) -> q g e", g=4),
              writes=[acc3(VB, [8 + t], 0, 512)], slot="VBc%d" % t)
    wn = "l%d_in_w" % li
    nkT = self.outp("l%d_nkT" % li, [128, 2, 1024])
    nv = self.outp("l%d_nv" % li, [1024, 256])
    for b in range(3):
        buf = self.next_block(wn, b * 1024)
        if b == 0:
            for cl in range(8):
                for c in range(2):
                    ps = self.proj_tile(buf, cl, c)
                    self.rope_evac(ps, self.qT, cl, c)
        elif b == 1:
            for cl in range(2):
                for c in range(2):
                    lo, hi = c * 512, (c + 1) * 512
                    ps = self.proj_tile(buf, cl, c)
                    st = self.ST[self.nxt("st", self.NST)]
                    p.op("act", lambda h, ps=ps, st=st: h.copy(st.t[:, :], ps.t[:, :]), reads=[acc(ps)], writes=[acc(st)])
                    p.dma("sp", nkT[:, cl, lo:hi], st.t[:, :], reads=[acc(st)], is_output=True)
                    self.rope_evac(ps, self.kT, cl, c)
            for tt in range(8):
                ps = self.ps_proj()
                for kt in range(8):
                    p.op("pe", lambda h, kt=kt, tt=tt, ps=ps, buf=buf: h.matmul(
                        ps.t[:, 0:256], self.hT.t[:, kt, tt * 128:(tt + 1) * 128], buf.t[:, kt, 256:512],
                        start=(kt == 0), stop=(kt == 7)),
                        reads=[acc(buf), acc3(self.hT, [kt], tt * 128, (tt + 1) * 128)], writes=[acc(ps)], skip_same=True)
                st = self.ST[self.nxt("st", self.NST)]
                p.op("dve", lambda h, ps=ps, st=st: h.tensor_copy(st.t[:, 0:256], ps.t[:, 0:256]), reads=[acc(ps)], writes=[acc(st)])
                p.dma("sp", nv[tt * 128:(tt + 1) * 128, :], st.t[:, 0:256], reads=[acc(st)], is_output=True)
                v4 = VB.t[:, tt, 0:512].rearrange("q (g e) -> q g e", g=4)
                p.op("act", lambda h, ps=ps, v4=v4: h.copy(v4[:, :, 0:64], ps.t[:, 0:256].rearrange("q (g e) -> q g e", g=4)),
                     reads=[acc(ps)], writes=[acc3(VB, [tt], 0, 512)])
            zlist = [(4 + i, i) for i in range(4)]
        if b >= 1:
            if b == 2:
                zlist = [(i, 4 + i) for i in range(4)]
            for (cl, zt) in zlist:
                for c in range(2):
                    lo, hi = c * 512, (c + 1) * 512
                    ps = self.proj_tile(buf, cl, c)
                    p.op("act", lambda h, ps=ps, zt=zt, lo=lo, hi=hi: h.activation(self.zT.t[:, zt, lo:hi], ps.t[:, :], AF.Silu),
                         reads=[acc(ps)], writes=[acc3(self.zT, [zt], lo, hi)])
    for Tk in range(2):
        for u in range(2):
            hkv = 2 * Tk + u
            pl, ph = u * 64, (u + 1) * 64
            for j in range(8):
                blocks = []
                if j > 0:
                    blocks.append(("P", j - 1))
                blocks.append(("L", j))
                if j < 7:
                    blocks.append(("N", j + 1))
                for t in range(4):
                    blocks.append(("C", t))
                po = self.ps_o()
                qlo, qhi = j * 128, (j + 1) * 128
                nb = len(blocks)
                for bi, (kind, kb) in enumerate(blocks):
                    sb = self.ps_s()
                    if kind == "C":
                        lhsT = self.kcT.t[pl:ph, Tk, kb * 128:(kb + 1) * 128]
                        rl = acc3(self.kcT, [Tk], kb * 128, (kb + 1) * 128)
                        vt = 8 + kb
                        bias = self.ctxb
                    else:
                        lhsT = self.kT.t[pl:ph, Tk, kb * 128:(kb + 1) * 128]
                        rl = acc3(self.kT, [Tk], kb * 128, (kb + 1) * 128)
                        vt = kb
                        bias = self.zero1
                    rhs = self.qT.t[pl:ph, 4 * Tk:4 * Tk + 4, qlo:qhi]
                    p.op("pe", lambda h, sb=sb, lhsT=lhsT, rhs=rhs: h.matmul(sb.t[:, :].rearrange("q (g e) -> q g e", g=4), lhsT, rhs, start=True, stop=True),
                         reads=[rl, acc3(self.qT, range(4 * Tk, 4 * Tk + 4), qlo, qhi)], writes=[acc(sb)], skip_same=True)
                    pt = self.PT[self.nxt("pt", self.NPT)]
                    p.op("act", lambda h, sb=sb, pt=pt, bias=bias: h.activation(pt.t[:, :], sb.t[:, :], AF.Exp, bias=bias.t[:, 0:1], scale=0.125),
                         reads=[acc(sb), acc(bias)], writes=[acc(pt)])
                    if kind in ("P", "N"):
                        mk = self.maskP if kind == "P" else self.maskN
                        p.op("pool", lambda h, pt=pt, mk=mk, j=j: h.tensor_tensor(
                            pt.t[:, :].rearrange("q (g e) -> q g e", g=4), pt.t[:, :].rearrange("q (g e) -> q g e", g=4),
                            mk.t[:, j, :].unsqueeze(1).broadcast_to([128, 4, 128]), ALU.mult),
                            reads=[acc(pt), acc(mk)], writes=[acc(pt)])
                    p.op("pe", lambda h, po=po, vt=vt, pt=pt, bi=bi, nb=nb, hkv=hkv: h.matmul(
                        po.t[:, :], VB.t[:, vt, hkv * 128:(hkv + 1) * 128], pt.t[:, :], start=(bi == 0), stop=(bi == nb - 1)),
                        reads=[acc3(VB, [vt], hkv * 128, (hkv + 1) * 128), acc(pt)], writes=[acc(po)], skip_same=True)
                tmp = self.TMP[self.nxt("tmp", self.NTMP)]
                p.op("dve", lambda h, po=po, tmp=tmp, hkv=hkv: h.tensor_tensor(
                    tmp.t[64:128, :].rearrange("q (g e) -> q g e", g=4), po.t[64:128, :].rearrange("q (g e) -> q g e", g=4),
                    self.esink.t[64:128, hkv * 4:hkv * 4 + 4].unsqueeze(2).broadcast_to([64, 4, 128]), ALU.add),
                    reads=[acc(po), acc(self.esink)], writes=[acc(tmp)])
                p.op("dve", lambda h, tmp=tmp: h.reciprocal(tmp.t[64:128, :], tmp.t[64:128, :]), reads=[acc(tmp)], writes=[acc(tmp)])
                p.op("dve", lambda h, po=po, tmp=tmp, pl=pl, ph=ph, Tk=Tk, qlo=qlo, qhi=qhi: h.tensor_tensor(
                    self.oT.t[pl:ph, 4 * Tk:4 * Tk + 4, qlo:qhi], po.t[0:64, :].rearrange("q (g e) -> q g e", g=4),
                    tmp.t[64:128, :].rearrange("q (g e) -> q g e", g=4), ALU.mult),
                    reads=[acc(po), acc(tmp)], writes=[acc3(self.oT, range(4 * Tk, 4 * Tk + 4), qlo, qhi)])
    for t in range(8):
        for c in range(2):
            lo, hi = c * 512, (c + 1) * 512
            p.op("pool", lambda h, t=t, lo=lo, hi=hi: h.tensor_tensor(self.oT.t[:, t, lo:hi], self.oT.t[:, t, lo:hi], self.zT.t[:, t, lo:hi], ALU.mult),
                 reads=[acc3(self.oT, [t], lo, hi), acc3(self.zT, [t], lo, hi)], writes=[acc3(self.oT, [t], lo, hi)])
    self.out_proj(li)


Builder.layer_a = layer_a


def build(layers=(0, 1, 2, 3), debug=False):
    b = Builder(layers, debug)
    b.plan_weights()
    b.setup()
    for li in layers:
        k = KIND[li]
        if k == "a":
            b.layer_a(li)
        elif k == "b":
            b.layer_b(li)
        else:
            b.layer_c(li)
        if debug:
            b.dump_x("dbg_x%d" % li)
    b.final()
    b.p.finish()
    return b


def fm(v, n):
    return np.ascontiguousarray(np.asarray(v, np.float32).reshape(n, 128).T)


def to_fm3(a):
    a = np.asarray(a, np.float32)
    n = a.shape[1] // 128
    return np.ascontiguousarray(a.T.reshape(n, 128, a.shape[0]).transpose(1, 0, 2))


def from_fm3(a):
    return np.ascontiguousarray(a.transpose(2, 1, 0).reshape(a.shape[2], -1))


def perm_a():
    perm = np.zeros(1024, np.int64)
    for tq in range(8):
        Tk, g = tq // 4, tq % 4
        for u in range(2):
            hq = 4 * (2 * Tk + u) + g
            perm[tq * 128 + u * 64: tq * 128 + u * 64 + 64] = hq * 64 + np.arange(64)
    return perm


def rope_tables(sample):
    C = np.ones((128, 1024), np.float32)
    S = np.zeros((128, 1024), np.float32)
    if sample:
        tok = np.arange(1024)
        row = (tok // 64).astype(np.float32)
        col = (tok % 64).astype(np.float32)
        inv = (np.float32(10000.0) ** (-np.arange(16, dtype=np.float32) / np.float32(16))).astype(np.float32)
        for pp in range(128):
            d = pp % 64
            pos = row if d < 32 else col
            ang = (pos * inv[d % 16]).astype(np.float32)
            C[pp] = np.cos(ang)
            sgn = -1.0 if (d % 32) < 16 else 1.0
            S[pp] = sgn * np.sin(ang)
    return C, S


def rope_perm_matrix():
    Rm = np.zeros((128, 128), np.float32)
    for m in range(128):
        d = m % 64
        base = m - d
        partner = d + 16 if (d % 32) < 16 else d - 16
        Rm[base + partner, m] = 1.0
    return Rm


def win_masks(sample):
    mp = np.zeros((128, 8, 128), np.float32)
    mn = np.zeros((128, 8, 128), np.float32)
    k = np.arange(128)[:, None]
    q = np.arange(128)[None, :]
    for j in range(8):
        if sample:
            mp[:, j, :] = (k >= q)
            mn[:, j, :] = (k <= q)
        else:
            mp[:, j, :] = 1.0 if (j % 2 == 1) else 0.0
            mn[:, j, :] = 1.0 if (j % 2 == 0) else 0.0
    return mp, mn


_CACHE = {}
LAYERS = (0, 1, 2, 3)
DEBUG = False


def kernel(**inp):
    inp = {k: np.asarray(v) for k, v in inp.items()}
    layers = LAYERS
    key = (tuple(layers), DEBUG)
    if key not in _CACHE:
        _CACHE[key] = build(layers, DEBUG)
    b = _CACHE[key]
    pa = perm_a()
    shared = {}
    for li in layers:
        shared["l%d_mod_w" % li] = np.ascontiguousarray(inp["l%d_mod_w" % li], np.float32)
        w = inp["l%d_in_w" % li]
        ow = inp["l%d_out_w" % li]
        if KIND[li] == "a":
            w = np.concatenate([w[:, pa], w[:, 1024:1536], w[:, 1536 + pa]], axis=1)
            ow = ow[pa, :]
        shared["l%d_in_w" % li] = np.ascontiguousarray(w, np.float32)
        shared["l%d_out_w" % li] = np.ascontiguousarray(ow, np.float32)
        shared["l%d_norm_w" % li] = fm(inp["l%d_norm_w" % li], 8)
        shared["l%d_mod_b" % li] = fm(inp["l%d_mod_b" % li], 24)
        if KIND[li] == "a":
            shared["l%d_sink" % li] = np.ascontiguousarray(np.broadcast_to(inp["l%d_sink" % li].astype(np.float32)[None, :], (128, 16)))
    if 1 in layers:
        lamcat = np.concatenate([inp["l1_lambda_q1"], inp["l1_lambda_k1"], inp["l1_lambda_q2"], inp["l1_lambda_k2"]]).astype(np.float32)
        shared["l1_lam"] = np.ascontiguousarray(np.broadcast_to(lamcat[None, :], (128, 256)))
        shared["l1_subw"] = fm(inp["l1_subln_w"], 1)
    if 2 in layers:
        pp_ = np.arange(128)[:, None]
        ff_ = np.arange(128)[None, :]
        LOW, UPP, LOWI, UPPI = (pp_ > ff_), (pp_ < ff_), (pp_ >= ff_), (pp_ <= ff_)
        consts = [NEG * (1 - UPPI), NEG * (1 - LOWI), NEG * (1 - UPP), NEG * (1 - LOW),
                  -NEG * (1 - LOW), -NEG * (1 - UPP), 1.0 * UPPI, 1.0 * LOWI]
        shared["l2_consts"] = np.ascontiguousarray(np.concatenate([c_.astype(np.float32) for c_ in consts], axis=1))
        shared["l2_id32"] = np.eye(128, dtype=np.float32)
        cwv = inp["l2_conv_w"].astype(np.float32)
        shared["l2_cw"] = np.ascontiguousarray(cwv.reshape(3, 24, 128).transpose(2, 1, 0).reshape(128, 72))
    shared["finw"] = fm(inp["final_norm_w"], 8)
    shared["Rm"] = rope_perm_matrix()
    in_maps = []
    for core in range(8):
        sample = core < 4
        m = dict(shared)
        if sample:
            bb = core
            x = inp["x_sample"][bb]
            cond = inp["c"][bb]
        else:
            bb = core % 4
            s0 = 4 * (core - 4)
            x = inp["x_prompt"][s0:s0 + 4].reshape(1024, 1024)
            cond = inp["c_ctx"]
        m["xT"] = to_fm3(x)
        m["cond"] = fm(cond, 8)
        C, S = rope_tables(sample)
        m["ropeC"], m["ropeS"] = C, S
        mp, mn = win_masks(sample)
        m["maskP"], m["maskN"] = mp, mn
        m["ctxb"] = np.full((128, 1), 0.0 if sample else NEG, np.float32)
        for li in layers:
            if KIND[li] == "a":
                ck = inp["cache_l%d_k" % li][bb]
                cv = inp["cache_l%d_v" % li][bb]
                kcT = ck.transpose(1, 2, 0).reshape(2, 128, 512).transpose(1, 0, 2)
                m["l%d_kcT" % li] = np.ascontiguousarray(kcT, np.float32)
                m["l%d_vc" % li] = np.ascontiguousarray(cv.reshape(512, 256), np.float32)
        if 2 in layers:
            sc_ = np.zeros((128, 36), np.float32)
            sc_[:, 0] = 1.0 if sample else 0.0
            sc_[:, 1] = 0.0 if sample else 1.0
            sc_[:, 2] = inp["l2_onorm_w"].astype(np.float32)
            sc_[:, 3] = 1.0
            sc_[:, 4:20] = inp["l2_a_log"].astype(np.float32).reshape(1, 16)
            sc_[:, 20:36] = inp["l2_dt_bias"].astype(np.float32).reshape(1, 16)
            m["l2_scal"] = sc_
            m["l2_s0"] = np.ascontiguousarray(inp["state_l2"][bb], np.float32)
        if 1 in layers:
            m["l1_kcT"] = to_fm3(inp["cache_l1_k"][bb].reshape(512, 1024))
            m["l1_vc"] = np.ascontiguousarray(inp["cache_l1_v"][bb].reshape(512, 1024), np.float32)
            bt = np.zeros((128, 48), np.float32)
            if not sample:
                for qc in range(4):
                    for kb in range(12):
                        if not (kb < 8 and kb // 2 == qc):
                            bt[:, qc * 12 + kb] = NEG
            m["l1_bias"] = bt
        in_maps.append({k: m[k] for k in b.din})
    res = run_bass_kernel_spmd(b.nc, in_maps, core_ids=list(range(8)))
    R = res.results
    kernel.last = R
    y_sample = np.stack([from_fm3(R[c]["yT"]) for c in range(4)])
    y_prompt = np.concatenate([from_fm3(R[c]["yT"]).reshape(4, 256, 1024) for c in range(4, 8)])
    outs = {}
    for li in (0, 3):
        if li in layers:
            outs["nk%d" % li] = np.concatenate([from_fm3(R[c]["l%d_nkT" % li]).reshape(4, 256, 4, 64) for c in range(4, 8)])
            outs["nv%d" % li] = np.concatenate([R[c]["l%d_nv" % li].reshape(4, 256, 4, 64) for c in range(4, 8)])
        else:
            outs["nk%d" % li] = np.zeros((16, 256, 4, 64), np.float32)
            outs["nv%d" % li] = np.zeros((16, 256, 4, 64), np.float32)
    if 1 in layers:
        nk1 = np.concatenate([from_fm3(R[c]["l1_nkT"]).reshape(4, 256, 8, 2, 64) for c in range(4, 8)])
        nv1 = np.concatenate([R[c]["l1_nv"].reshape(4, 256, 8, 128) for c in range(4, 8)])
    else:
        nk1 = np.zeros((16, 256, 8, 2, 64), np.float32)
        nv1 = np.zeros((16, 256, 8, 128), np.float32)
    if 2 in layers:
        st = np.concatenate([R[c]["l2_nst"].transpose(1, 0, 2, 3, 4) for c in range(4, 8)])
    else:
        st = np.zeros((16, 2, 8, 128, 128), np.float32)
    return (y_prompt.astype(np.float32), y_sample.astype(np.float32), outs["nk0"], outs["nv0"], nk1, nv1, st,
            outs["nk3"], outs["nv3"])


def layer_b(self, li):
    p = self.p
    lam_init = lambda_init_for(li)
    if not hasattr(self, "lamv"):
        self.lamv = p.tensor("lamv", [128, 256], F32)
        self.lsm = p.tensor("lsm", [128, 8], F32)
        self.l1bias = p.tensor("l1bias", [128, 48], F32)
    lamv, lsm, l1bias = self.lamv, self.lsm, self.l1bias
    self.modulate(li)
    p.dma("sp", lamv.t[:, :], self.inp("l1_lam", [128, 256]), writes=[acc(lamv)])
    p.dma("sp", l1bias.t[:, :], self.inp("l1_bias", [128, 48]), writes=[acc(l1bias)])
    p.dma("sp", lsm.t[:, 7:8], self.inp("l1_subw", [128, 1]), writes=[acc(lsm)])
    p.op("dve", lambda h: h.tensor_tensor(lamv.t[:, 0:64], lamv.t[:, 0:64], lamv.t[:, 64:128], ALU.mult), reads=[acc(lamv)], writes=[acc(lamv)])
    p.op("dve", lambda h: h.tensor_tensor(lamv.t[:, 128:192], lamv.t[:, 128:192], lamv.t[:, 192:256], ALU.mult), reads=[acc(lamv)], writes=[acc(lamv)])
    p.op("dve", lambda h: h.reduce_sum(lsm.t[:, 0:1], lamv.t[:, 0:64], AX.X), reads=[acc(lamv), acc(lsm)], writes=[acc(lsm)])
    p.op("dve", lambda h: h.reduce_sum(lsm.t[:, 1:2], lamv.t[:, 128:192], AX.X), reads=[acc(lamv), acc(lsm)], writes=[acc(lsm)])
    p.op("act", lambda h: h.activation(lsm.t[:, 2:4], lsm.t[:, 0:2], AF.Exp), reads=[acc(lsm)], writes=[acc(lsm)])
    p.op("dve", lambda h: h.tensor_tensor(lsm.t[:, 4:5], lsm.t[:, 3:4], lsm.t[:, 2:3], ALU.subtract), reads=[acc(lsm)], writes=[acc(lsm)])
    p.op("dve", lambda h: h.tensor_scalar(lsm.t[:, 4:5], lsm.t[:, 4:5], -lam_init, None, ALU.add), reads=[acc(lsm)], writes=[acc(lsm)])
    p.op("dve", lambda h: h.tensor_scalar(lsm.t[:, 5:6], lsm.t[:, 7:8], 1.0 - lam_init, None, ALU.mult), reads=[acc(lsm)], writes=[acc(lsm)])
    kc = self.inp("l1_kcT", [128, 8, 512])
    p.dma("pool", self.kcT.t[:, :, :], kc, writes=[acc(self.kcT)])
    vc = self.inp("l1_vc", [512, 1024])
    VB = self.VB
    for t in range(4):
        p.dma("pool", VB.t[:, 8 + t, :], vc[t * 128:(t + 1) * 128, :], writes=[acc3(VB, [8 + t], 0, 1024)])
    wn = "l1_in_w"
    nkT = self.outp("l1_nkT", [128, 8, 1024])
    nv = self.outp("l1_nv", [1024, 1024])
    for b in range(4):
        buf = self.next_block(wn, b * 1024)
        if b == 0:
            for cl in range(8):
                for c in range(2):
                    ps = self.proj_tile(buf, cl, c)
                    self.rope_evac(ps, self.qT, cl, c)
        elif b == 1:
            for cl in range(8):
                for c in range(2):
                    lo, hi = c * 512, (c + 1) * 512
                    ps = self.proj_tile(buf, cl, c)
                    st = self.ST[self.nxt("st", self.NST)]
                    p.op("act", lambda h, ps=ps, st=st: h.copy(st.t[:, :], ps.t[:, :]), reads=[acc(ps)], writes=[acc(st)])
                    p.dma("sp", nkT[:, cl, lo:hi], st.t[:, :], reads=[acc(st)], is_output=True)
                    self.rope_evac(ps, self.kT, cl, c)
        elif b == 2:
            for tt in range(8):
                for g in range(2):
                    ps = self.ps_proj()
                    for kt in range(8):
                        p.op("pe", lambda h, kt=kt, tt=tt, ps=ps, buf=buf, g=g: h.matmul(
                            ps.t[:, :], self.hT.t[:, kt, tt * 128:(tt + 1) * 128], buf.t[:, kt, g * 512:(g + 1) * 512],
                            start=(kt == 0), stop=(kt == 7)),
                            reads=[acc(buf), acc3(self.hT, [kt], tt * 128, (tt + 1) * 128)], writes=[acc(ps)], skip_same=True)
                    st = self.ST[self.nxt("st", self.NST)]
                    p.op("dve", lambda h, ps=ps, st=st: h.tensor_copy(st.t[:, :], ps.t[:, :]), reads=[acc(ps)], writes=[acc(st)])
                    p.dma("sp", nv[tt * 128:(tt + 1) * 128, g * 512:(g + 1) * 512], st.t[:, :], reads=[acc(st)], is_output=True)
                    p.op("act", lambda h, ps=ps, tt=tt, g=g: h.copy(VB.t[:, tt, g * 512:(g + 1) * 512], ps.t[:, :]),
                         reads=[acc(ps)], writes=[acc3(VB, [tt], g * 512, (g + 1) * 512)])
        else:
            for cl in range(8):
                for c in range(2):
                    lo, hi = c * 512, (c + 1) * 512
                    ps = self.proj_tile(buf, cl, c)
                    p.op("act", lambda h, ps=ps, cl=cl, lo=lo, hi=hi: h.activation(self.zT.t[:, cl, lo:hi], ps.t[:, :], AF.Silu),
                         reads=[acc(ps)], writes=[acc3(self.zT, [cl], lo, hi)])
    for hh in range(8):
        for qc in range(4):
            qlo, qhi = qc * 256, (qc + 1) * 256
            ocs = []
            for c in range(2):
                pl, ph = c * 64, (c + 1) * 64
                po, pd = (self.PS[6], self.PS[7]) if c == 0 else (self.PS[0], self.PS[1])
                for kb in range(12):
                    sb = self.ps_s()
                    if kb >= 8:
                        lhsT = self.kcT.t[pl:ph, hh, (kb - 8) * 128:(kb - 7) * 128]
                        rl = acc3(self.kcT, [hh], (kb - 8) * 128, (kb - 7) * 128)
                    else:
                        lhsT = self.kT.t[pl:ph, hh, kb * 128:(kb + 1) * 128]
                        rl = acc3(self.kT, [hh], kb * 128, (kb + 1) * 128)
                    rhs = self.qT.t[pl:ph, hh, qlo:qhi]
                    p.op("pe", lambda h, sb=sb, lhsT=lhsT, rhs=rhs: h.matmul(sb.t[:, 0:256], lhsT, rhs, start=True, stop=True),
                         reads=[rl, acc3(self.qT, [hh], qlo, qhi)], writes=[acc(sb)], skip_same=True)
                    pt = self.PT[self.nxt("pt", self.NPT)]
                    bi = qc * 12 + kb
                    p.op("act", lambda h, sb=sb, pt=pt, bi=bi: h.activation(pt.t[:, 0:256], sb.t[:, 0:256], AF.Exp, bias=l1bias.t[:, bi:bi + 1], scale=0.125),
                         reads=[acc(sb), acc(l1bias)], writes=[acc(pt)])
                    p.op("pe", lambda h, po=po, kb=kb, pt=pt, hh=hh: h.matmul(
                        po.t[:, 0:256], VB.t[:, kb, hh * 128:(hh + 1) * 128], pt.t[:, 0:256], start=(kb == 0), stop=(kb == 11)),
                        reads=[acc3(VB, [kb], hh * 128, (hh + 1) * 128), acc(pt)], writes=[acc(po)], skip_same=True)
                    p.op("pe", lambda h, pd=pd, kb=kb, pt=pt: h.matmul(
                        pd.t[:, 0:256], self.ones.t[:, :], pt.t[:, 0:256], start=(kb == 0), stop=(kb == 11)),
                        reads=[acc(self.ones), acc(pt)], writes=[acc(pd)], skip_same=True)
                rd = self.TMP[self.nxt("tmp", self.NTMP)]
                p.op("dve", lambda h, rd=rd, pd=pd: h.reciprocal(rd.t[:, 0:256], pd.t[:, 0:256]), reads=[acc(pd)], writes=[acc(rd)])
                oc = self.TMP[self.nxt("tmp", self.NTMP)]
                p.op("dve", lambda h, oc=oc, po=po, rd=rd: h.tensor_tensor(oc.t[:, 0:256], po.t[:, 0:256], rd.t[:, 0:256], ALU.mult),
                     reads=[acc(po), acc(rd)], writes=[acc(oc)])
                ocs.append(oc)
            o = self.TMP[self.nxt("tmp", self.NTMP)]
            p.op("dve", lambda h, o=o, o0=ocs[0], o1=ocs[1]: h.scalar_tensor_tensor(o.t[:, 0:256], o1.t[:, 0:256], lsm.t[:, 4:5], o0.t[:, 0:256], ALU.mult, ALU.add),
                 reads=[acc(ocs[0]), acc(ocs[1]), acc(lsm)], writes=[acc(o)])
            sq = self.PT[self.nxt("pt", self.NPT)]
            p.op("act", lambda h, sq=sq, o=o: h.activation(sq.t[:, 0:256], o.t[:, 0:256], AF.Square), reads=[acc(o)], writes=[acc(sq)])
            pr = self.PS[2]
            p.op("pe", lambda h, sq=sq: h.matmul(pr.t[:, 0:256], self.ones.t[:, :], sq.t[:, 0:256], start=True, stop=True),
                 reads=[acc(self.ones), acc(sq)], writes=[acc(pr)], skip_same=True)
            rs = self.TMP[self.nxt("tmp", self.NTMP)]
            p.op("act", lambda h, rs=rs: h.activation(rs.t[:, 0:256], pr.t[:, 0:256], AF.Sqrt, bias=self.eps1.t[:, 0:1], scale=1.0 / 128.0),
                 reads=[acc(pr), acc(self.eps1)], writes=[acc(rs)])
            p.op("dve", lambda h, rs=rs: h.reciprocal(rs.t[:, 0:256], rs.t[:, 0:256]), reads=[acc(rs)], writes=[acc(rs)])
            p.op("dve", lambda h, o=o, rs=rs: h.scalar_tensor_tensor(o.t[:, 0:256], o.t[:, 0:256], lsm.t[:, 5:6], rs.t[:, 0:256], ALU.mult, ALU.mult),
                 reads=[acc(o), acc(rs), acc(lsm)], writes=[acc(o)])
            p.op("pool", lambda h, o=o, hh=hh, qlo=qlo, qhi=qhi: h.tensor_tensor(self.oT.t[:, hh, qlo:qhi], o.t[:, 0:256], self.zT.t[:, hh, qlo:qhi], ALU.mult),
                 reads=[acc(o), acc3(self.zT, [hh], qlo, qhi)], writes=[acc3(self.oT, [hh], qlo, qhi)])
    self.out_proj(li)


Builder.layer_b = layer_b


class Slot:
    def __init__(self, t, a, n=128):
        self.t = t
        self.a = a
        self.n = n
        if len(t.shape) == 3:
            self.ap = t.t[:, :, :].rearrange("q a b -> q (a b)")[:, a:a + n]
        else:
            self.ap = t.t[:, a:a + n]
        self.acc = acc(t, a, a + n)

    def cols(self, lo, hi):
        return Slot(self.t, self.a + lo, hi - lo)


def layer_c(self, li):
    p = self.p
    VB = self.VB
    if not hasattr(self, "S32"):
        self.S32 = p.tensor("S32", [128, 4, 128], F32, gran=128)
        self.S16 = p.tensor("S16", [128, 4, 128], BF16, gran=128)
        self.id32 = p.tensor("id32", [128, 128], F32)
        self.id16 = p.tensor("id16", [128, 128], BF16)
        self.ones32 = p.tensor("ones32", [128, 128], F32)
        self.cw = p.tensor("cw", [128, 160], F32)
        self.l2s = p.tensor("l2s", [128, 64], F32)
    S32, S16, id32, id16, ones32, cw, l2s = self.S32, self.S16, self.id32, self.id16, self.ones32, self.cw, self.l2s
    GS = self.ropeC
    MK = self.ropeS
    KEEP, BND, ONW, ONE = 0, 1, 2, 3
    G_BETA, G_GC, G_NGC, G_EGC, G_KDEC, G_BG, G_EGT = range(7)

    def gs(idx, t, c0, c1):
        a = idx * 128 + t * 16
        return Slot(GS, a + c0, c1 - c0)

    def mk(idx):
        return Slot(MK, idx * 128)

    self.modulate(li)
    p.dma("sp", MK.t[:, :], self.inp("l2_consts", [128, 1024]), writes=[acc(MK)])
    p.dma("sp", id32.t[:, :], self.inp("l2_id32", [128, 128]), writes=[acc(id32)])
    p.op("dve", lambda h: h.tensor_copy(id16.t[:, :], id32.t[:, :]), reads=[acc(id32)], writes=[acc(id16)])
    p.op("dve", lambda h: h.memset(ones32.t[:, :], 1.0), writes=[acc(ones32)])
    p.dma("sp", cw.t[:, 0:72], self.inp("l2_cw", [128, 72]), writes=[acc(cw)])
    p.dma("sp", l2s.t[:, 0:36], self.inp("l2_scal", [128, 36]), writes=[acc(l2s)])
    p.op("dve", lambda h: h.tensor_scalar(cw.t[:, 72:144], cw.t[:, 0:72], l2s.t[:, BND:BND + 1], -1.0, ALU.mult, ALU.mult),
         reads=[acc(cw), acc(l2s)], writes=[acc(cw)])
    p.op("act", lambda h: h.activation(l2s.t[:, 36:52], l2s.t[:, 4:20], AF.Exp), reads=[acc(l2s)], writes=[acc(l2s)])
    p.op("dve", lambda h: h.tensor_scalar(l2s.t[:, 36:52], l2s.t[:, 36:52], -1.0, None, ALU.mult), reads=[acc(l2s)], writes=[acc(l2s)])

    rr = {"s16": 0, "s32": 0, "ps": 0}

    def s16():
        i = rr["s16"]
        rr["s16"] = (i + 1) % (self.NPT * 4)
        return Slot(self.PT[i // 4], (i % 4) * 128)

    pool32 = self.TMP + self.ST

    def s32():
        i = rr["s32"]
        rr["s32"] = (i + 1) % (len(pool32) * 4)
        return Slot(pool32[i // 4], (i % 4) * 128)

    def pst(n=128):
        i = rr["ps"]
        rr["ps"] = (i + 1) % 8
        return Slot(self.PS[i], 0, n)

    def mm(out, lhsT, rhs, start=True, stop=True, extra_r=()):
        p.op("pe", lambda h: h.matmul(out[0], lhsT[0], rhs[0], start=start, stop=stop),
             reads=[lhsT[1], rhs[1]] + list(extra_r), writes=[out[1]], skip_same=True)

    def sl(s):
        return (s.ap, s.acc)

    wn = "l2_in_w"
    QSC = 128.0 ** -0.5

    def ktok(tt):
        if tt < 4:
            return VB, (8 + tt) * 1024
        return self.kcT, (tt - 4) * 1024

    kflat = self.kcT.t[:, :, :].rearrange("q a b -> q (a b)")
    vflat = VB.t[:, :, :].rearrange("q a b -> q (a b)")

    def ktok_slot(tt, hh):
        if tt < 4:
            a = (8 + tt) * 1024 + hh * 128
            s_ = Slot.__new__(Slot)
            s_.t, s_.a, s_.n = VB, a, 128
            s_.ap = vflat[:, a:a + 128]
            s_.acc = acc(VB, a, a + 128)
            return s_
        a = (tt - 4) * 1024 + hh * 128
        s_ = Slot.__new__(Slot)
        s_.t, s_.a, s_.n = self.kcT, a, 128
        s_.ap = kflat[:, a:a + 128]
        s_.acc = acc(self.kcT, a, a + 128)
        return s_

    def vtok_slot(tt, hh):
        a = tt * 1024 + hh * 128
        s_ = Slot.__new__(Slot)
        s_.t, s_.a, s_.n = VB, a, 128
        s_.ap = vflat[:, a:a + 128]
        s_.acc = acc(VB, a, a + 128)
        return s_

    for b in range(3):
        buf = self.next_block(wn, b * 1024)
        for cl in range(8):
            ft = b * 8 + cl
            xp = [None, None]
            xs = []
            for c in range(2):
                ps = self.proj_tile(buf, cl, c)
                xt = self.TMP[self.nxt("tmp", self.NTMP)]
                p.op("act", lambda h, ps=ps, xt=xt: h.copy(xt.t[:, :], ps.t[:, :]), reads=[acc(ps)], writes=[acc(xt)])
                xs.append(xt)
            w0 = cw.t[:, ft * 3 + 0:ft * 3 + 1]
            w1 = cw.t[:, ft * 3 + 1:ft * 3 + 2]
            w2 = cw.t[:, ft * 3 + 2:ft * 3 + 3]
            nb0 = cw.t[:, 72 + ft * 3 + 0:72 + ft * 3 + 1]
            nb2 = cw.t[:, 72 + ft * 3 + 2:72 + ft * 3 + 3]
            ys = []
            for c in range(2):
                y = self.ST[self.nxt("st", self.NST)]
                x = xs[c]
                xo = xs[1 - c]
                p.op("dve", lambda h, y=y, x=x, w1=w1: h.tensor_scalar(y.t[:, :], x.t[:, :], w1, None, ALU.mult),
                     reads=[acc(x), acc(cw)], writes=[acc(y)])
                p.op("dve", lambda h, y=y, x=x, w0=w0: h.scalar_tensor_tensor(y.t[:, 1:512], x.t[:, 0:511], w0, y.t[:, 1:512], ALU.mult, ALU.add),
                     reads=[acc(x), acc(cw), acc(y)], writes=[acc(y)])
                p.op("dve", lambda h, y=y, x=x, w2=w2: h.scalar_tensor_tensor(y.t[:, 0:511], x.t[:, 1:512], w2, y.t[:, 0:511], ALU.mult, ALU.add),
                     reads=[acc(x), acc(cw), acc(y)], writes=[acc(y)])
                if c == 1:
                    p.op("dve", lambda h, y=y, xo=xo, w0=w0: h.scalar_tensor_tensor(y.t[:, 0:1], xo.t[:, 511:512], w0, y.t[:, 0:1], ALU.mult, ALU.add),
                         reads=[acc(xo), acc(cw), acc(y)], writes=[acc(y)])
                    p.op("dve", lambda h, y=y, xo=xo, nb0=nb0: h.scalar_tensor_tensor(y.t[:, 0:1], xo.t[:, 511:512], nb0, y.t[:, 0:1], ALU.mult, ALU.add),
                         reads=[acc(xo), acc(cw), acc(y)], writes=[acc(y)])
                else:
                    p.op("dve", lambda h, y=y, xo=xo, w2=w2: h.scalar_tensor_tensor(y.t[:, 511:512], xo.t[:, 0:1], w2, y.t[:, 511:512], ALU.mult, ALU.add),
                         reads=[acc(xo), acc(cw), acc(y)], writes=[acc(y)])
                    p.op("dve", lambda h, y=y, xo=xo, nb2=nb2: h.scalar_tensor_tensor(y.t[:, 511:512], xo.t[:, 0:1], nb2, y.t[:, 511:512], ALU.mult, ALU.add),
                         reads=[acc(xo), acc(cw), acc(y)], writes=[acc(y)])
                p.op("dve", lambda h, y=y, x=x, nb0=nb0: h.scalar_tensor_tensor(y.t[:, 256:257], x.t[:, 255:256], nb0, y.t[:, 256:257], ALU.mult, ALU.add),
                     reads=[acc(x), acc(cw), acc(y)], writes=[acc(y)])
                p.op("dve", lambda h, y=y, x=x, nb2=nb2: h.scalar_tensor_tensor(y.t[:, 255:256], x.t[:, 256:257], nb2, y.t[:, 255:256], ALU.mult, ALU.add),
                     reads=[acc(x), acc(cw), acc(y)], writes=[acc(y)])
                ys.append(y)
            for c in range(2):
                y = ys[c]
                lo, hi = c * 512, (c + 1) * 512
                if b == 2:
                    hh = cl
                    v16 = self.PT[self.nxt("pt", self.NPT)]
                    p.op("act", lambda h, y=y, v16=v16: h.activation(v16.t[:, :], y.t[:, :], AF.Silu), reads=[acc(y)], writes=[acc(v16)])
                    for q4 in range(4):
                        tt = c * 4 + q4
                        pt_ = pst()
                        mm(sl(pt_), (v16.t[:, q4 * 128:(q4 + 1) * 128], acc(v16, q4 * 128, (q4 + 1) * 128)), (id16.t[:, :], acc(id16)))
                        vs = vtok_slot(tt, hh)
                        p.op("act", lambda h, vs=vs, pt_=pt_: h.copy(vs.ap, pt_.ap), reads=[pt_.acc], writes=[vs.acc])
                else:
                    hh = cl
                    p.op("act", lambda h, y=y: h.activation(y.t[:, :], y.t[:, :], AF.Silu), reads=[acc(y)], writes=[acc(y)])
                    sq = self.PT[self.nxt("pt", self.NPT)]
                    p.op("act", lambda h, y=y, sq=sq: h.activation(sq.t[:, :], y.t[:, :], AF.Square), reads=[acc(y)], writes=[acc(sq)])
                    pss = self.ps_proj()
                    p.op("pe", lambda h, pss=pss, sq=sq: h.matmul(pss.t[:, :], self.ones.t[:, :], sq.t[:, :], start=True, stop=True),
                         reads=[acc(self.ones), acc(sq)], writes=[acc(pss)], skip_same=True)
                    rs = self.TMP[self.nxt("tmp", self.NTMP)]
                    p.op("act", lambda h, rs=rs, pss=pss: h.activation(rs.t[:, :], pss.t[:, :], AF.Sqrt, bias=self.eps1.t[:, 0:1], scale=1.0),
                         reads=[acc(pss), acc(self.eps1)], writes=[acc(rs)])
                    p.op("dve", lambda h, rs=rs: h.reciprocal(rs.t[:, :], rs.t[:, :]), reads=[acc(rs)], writes=[acc(rs)])
                    dst = self.qT if b == 0 else self.kT
                    sc_ = QSC if b == 0 else 1.0
                    p.op("dve", lambda h, y=y, rs=rs, dst=dst, hh=hh, lo=lo, hi=hi, sc_=sc_: h.scalar_tensor_tensor(
                        dst.t[:, hh, lo:hi], y.t[:, :], sc_, rs.t[:, :], ALU.mult, ALU.mult),
                        reads=[acc(y), acc(rs)], writes=[acc3(dst, [hh], lo, hi)])
                    if b == 1:
                        for q4 in range(4):
                            tt = c * 4 + q4
                            pt_ = pst()
                            mm(sl(pt_), (self.kT.t[:, hh, tt * 128:(tt + 1) * 128], acc3(self.kT, [hh], tt * 128, (tt + 1) * 128)), (id16.t[:, :], acc(id16)))
                            ks = ktok_slot(tt, hh)
                            p.op("act", lambda h, ks=ks, pt_=pt_: h.copy(ks.ap, pt_.ap), reads=[pt_.acc], writes=[ks.acc])
    buf = self.next_block(wn, 3072)
    for cl in range(8):
        for c in range(2):
            lo, hi = c * 512, (c + 1) * 512
            ps = self.proj_tile(buf, cl, c)
            p.op("act", lambda h, ps=ps, cl=cl, lo=lo, hi=hi: h.activation(self.zT.t[:, cl, lo:hi], ps.t[:, :], AF.Silu),
                 reads=[acc(ps)], writes=[acc3(self.zT, [cl], lo, hi)])
    buf = self.next_block(wn, 4096)
    alog_nA = l2s.t[:, 36:52]
    dtb = l2s.t[:, 20:36]
    one1 = l2s.t[:, ONE:ONE + 1]
    for t in range(8):
        gp = pst(32)
        for kt in range(8):
            p.op("pe", lambda h, kt=kt, t=t, gp=gp, buf=buf: h.matmul(gp.ap, self.hT.t[:, kt, t * 128:(t + 1) * 128], buf.t[:, kt, 0:32],
                                                                  start=(kt == 0), stop=(kt == 7)),
                 reads=[acc(buf), acc3(self.hT, [kt], t * 128, (t + 1) * 128)], writes=[gp.acc], skip_same=True)
        beta = gs(G_BETA, t, 0, 16)
        p.op("act", lambda h, beta=beta, gp=gp: h.activation(beta.ap, gp.ap[:, 0:16], AF.Sigmoid), reads=[gp.acc], writes=[beta.acc])
        sc = s32()
        p.op("dve", lambda h, sc=sc, gp=gp: h.tensor_tensor(sc.ap[:, 0:16], gp.ap[:, 16:32], dtb, ALU.add), reads=[gp.acc, acc(l2s)], writes=[sc.acc])
        p.op("act", lambda h, sc=sc: h.activation(sc.ap[:, 16:32], sc.ap[:, 0:16], AF.Exp), reads=[sc.acc], writes=[sc.acc])
        p.op("act", lambda h, sc=sc: h.activation(sc.ap[:, 32:48], sc.ap[:, 16:32], AF.Ln, bias=one1, scale=1.0), reads=[sc.acc, acc(l2s)], writes=[sc.acc])
        p.op("dve", lambda h, sc=sc: h.tensor_tensor(sc.ap[:, 48:64], sc.ap[:, 32:48], alog_nA, ALU.mult), reads=[sc.acc, acc(l2s)], writes=[sc.acc])
        g32 = (sc.ap[:, 48:64], sc.acc)
        pg = pst(32)
        UT, LT = mk(6), mk(7)
        p.op("pe", lambda h, pg=pg, sc=sc, UT=UT: h.matmul(pg.ap[:, 0:8], UT.ap, sc.ap[:, 48:56], start=True, stop=True),
             reads=[UT.acc, sc.acc], writes=[pg.acc], skip_same=True)
        p.op("pe", lambda h, pg=pg, sc=sc, LT=LT: h.matmul(pg.ap[:, 8:16], LT.ap, sc.ap[:, 56:64], start=True, stop=True),
             reads=[LT.acc, sc.acc], writes=[pg.acc], skip_same=True)
        p.op("pe", lambda h, pg=pg, sc=sc: h.matmul(pg.ap[:, 16:32], ones32.t[:, :], sc.ap[:, 48:64], start=True, stop=True),
             reads=[acc(ones32), sc.acc], writes=[pg.acc], skip_same=True)
        gc, ngc, egc, kdec, bg, egt = (gs(i, t, 0, 16) for i in (G_GC, G_NGC, G_EGC, G_KDEC, G_BG, G_EGT))
        p.op("dve", lambda h, gc=gc, pg=pg: h.tensor_copy(gc.ap, pg.ap[:, 0:16]), reads=[pg.acc], writes=[gc.acc])
        p.op("dve", lambda h, gc=gc, ngc=ngc: h.tensor_scalar(ngc.ap, gc.ap, -1.0, None, ALU.mult), reads=[gc.acc], writes=[ngc.acc])
        p.op("act", lambda h, gc=gc, egc=egc: h.activation(egc.ap, gc.ap, AF.Exp), reads=[gc.acc], writes=[egc.acc])
        p.op("act", lambda h, egt=egt, pg=pg: h.activation(egt.ap, pg.ap[:, 16:32], AF.Exp), reads=[pg.acc], writes=[egt.acc])
        p.op("dve", lambda h, kdec=kdec, pg=pg, ngc=ngc: h.tensor_tensor(kdec.ap, pg.ap[:, 16:32], ngc.ap, ALU.add), reads=[pg.acc, ngc.acc], writes=[kdec.acc])
        p.op("act", lambda h, kdec=kdec: h.activation(kdec.ap, kdec.ap, AF.Exp), reads=[kdec.acc], writes=[kdec.acc])
        p.op("dve", lambda h, bg=bg, beta=beta, egc=egc: h.tensor_tensor(bg.ap, beta.ap, egc.ap, ALU.mult), reads=[beta.acc, egc.acc], writes=[bg.acc])

    if self.debug:
        dq = self.outp("dbg_qT", [128, 8, 1024], BF16)
        dk = self.outp("dbg_kT", [128, 8, 1024], BF16)
        dv = self.outp("dbg_VB", [128, 12, 1024], BF16)
        dkc = self.outp("dbg_kcT", [128, 8, 512], BF16)
        dg = self.outp("dbg_GS", [128, 1024], F32)
        dz = self.outp("dbg_zT", [128, 8, 1024], BF16)
        p.dma("sp", dq, self.qT.t[:, :, :], reads=[acc(self.qT)], is_output=True)
        p.dma("sp", dk, self.kT.t[:, :, :], reads=[acc(self.kT)], is_output=True)
        p.dma("sp", dv, VB.t[:, :, :], reads=[acc(VB)], is_output=True)
        p.dma("sp", dkc, self.kcT.t[:, :, :], reads=[acc(self.kcT)], is_output=True)
        p.dma("sp", dg, GS.t[:, :], reads=[acc(GS)], is_output=True)
        p.dma("sp", dz, self.zT.t[:, :, :], reads=[acc(self.zT)], is_output=True)
    s0 = self.inp("l2_s0", [2, 8, 128, 128])
    nst = self.outp("l2_nst", [2, 4, 8, 128, 128])
    keep = l2s.t[:, KEEP:KEEP + 1]
    onw = l2s.t[:, ONW:ONW + 1]

    def col(idx, t, c):
        s_ = gs(idx, t, c, c + 1)
        return s_

    def instance(t, d, hh, ci, first):
        c = d * 8 + hh
        tl, th = t * 128, (t + 1) * 128
        KT_ = (self.kT.t[:, hh, tl:th], acc3(self.kT, [hh], tl, th))
        QT_ = (self.qT.t[:, hh, tl:th], acc3(self.qT, [hh], tl, th))
        ktk = ktok_slot(t, hh)
        vtk = vtok_slot(t, hh)
        gcC, ngcC, egcC, kdecC, bgC, egtC, betaC = (col(i, t, c) for i in (G_GC, G_NGC, G_EGC, G_KDEC, G_BG, G_EGT, G_BETA))
        MTi, MTs, MAs = mk(0 + d), mk(2 + d), mk(4 + d)

        def diag(colslot):
            dg = s32()
            p.op("dve", lambda h: h.tensor_scalar(dg.ap, id32.t[:, :], colslot.ap, None, ALU.mult),
                 reads=[acc(id32), colslot.acc], writes=[dg.acc])
            return dg

        dgc = diag(gcC)
        def decay(mask, scale, biascol):
            ps_ = pst()
            mm(sl(ps_), (ones32.t[:, :], acc(ones32)), sl(dgc), start=True, stop=False)
            mm(sl(ps_), (id32.t[:, :], acc(id32)), sl(mask), start=False, stop=True)
            o_ = s16()
            p.op("act", lambda h: h.activation(o_.ap, ps_.ap, AF.Exp, bias=biascol.ap, scale=scale),
                 reads=[ps_.acc, biascol.acc], writes=[o_.acc])
            return o_

        DTs = decay(MTs, 1.0, ngcC)
        DAs = decay(MAs, -1.0, gcC)
        def rowscaled(colslot, src):
            dg = diag(colslot)
            ps_ = pst()
            mm(sl(ps_), (ones32.t[:, :], acc(ones32)), sl(dg))
            o_ = s16()
            p.op("dve", lambda h: h.tensor_tensor(o_.ap, src[0], ps_.ap, ALU.mult), reads=[src[1], ps_.acc], writes=[o_.acc])
            return o_

        KbT = rowscaled(betaC, KT_)

        def masked(lhsT, rhs, dm):
            ps_ = pst()
            mm(sl(ps_), lhsT, rhs)
            o_ = s16()
            p.op("dve", lambda h: h.tensor_tensor(o_.ap, ps_.ap, dm.ap, ALU.mult), reads=[ps_.acc, dm.acc], writes=[o_.acc])
            return o_

        def masked32(lhsT, rhs, dm):
            ps_ = pst()
            mm(sl(ps_), lhsT, rhs)
            o_ = s32()
            p.op("dve", lambda h: h.tensor_tensor(o_.ap, ps_.ap, dm.ap, ALU.mult), reads=[ps_.acc, dm.acc], writes=[o_.acc])
            return o_

        A = masked32(sl(KbT), KT_, DAs)
        P = masked32(KT_, sl(KbT), DTs)
        Tt32 = s32()
        p.op("dve", lambda h: h.tensor_tensor(Tt32.ap, id32.t[:, :], P.ap, ALU.subtract), reads=[acc(id32), P.acc], writes=[Tt32.acc])
        Am, Pm = A, P
        for m in (1, 2, 4, 8, 16, 32):
            psA = pst()
            mm(sl(psA), sl(Pm), sl(Am))
            A2 = s32()
            p.op("act", lambda h, A2=A2, psA=psA: h.copy(A2.ap, psA.ap), reads=[psA.acc], writes=[A2.acc])
            if m < 32:
                psP = pst()
                mm(sl(psP), sl(Am), sl(Pm))
                P2 = s32()
                p.op("dve", lambda h, P2=P2, psP=psP: h.tensor_copy(P2.ap, psP.ap), reads=[psP.acc], writes=[P2.acc])
            psT = pst()
            mm(sl(psT), sl(A2), sl(Tt32))
            p.op("dve", lambda h, psT=psT: h.tensor_tensor(Tt32.ap, Tt32.ap, psT.ap, ALU.add), reads=[Tt32.acc, psT.acc], writes=[Tt32.acc])
            Am = A2
            if m < 32:
                Pm = P2
        Tt16 = s16()
        p.op("act", lambda h: h.copy(Tt16.ap, Tt32.ap), reads=[Tt32.acc], writes=[Tt16.acc])
        DTi = decay(MTi, 1.0, ngcC)
        QgT = rowscaled(egcC, QT_)
        intraT = masked(KT_, QT_, DTi)
        Vb = s16()
        p.op("pool", lambda h: h.tensor_scalar(Vb.ap, vtk.ap, betaC.ap, None, ALU.mult), reads=[vtk.acc, betaC.acc], writes=[Vb.acc])
        Kbg = s16()
        p.op("pool", lambda h: h.tensor_scalar(Kbg.ap, ktk.ap, bgC.ap, None, ALU.mult), reads=[ktk.acc, bgC.acc], writes=[Kbg.acc])
        Kdec = s16()
        p.op("pool", lambda h: h.tensor_scalar(Kdec.ap, ktk.ap, kdecC.ap, None, ALU.mult), reads=[ktk.acc, kdecC.acc], writes=[Kdec.acc])
        psU = pst()
        mm(sl(psU), sl(Tt16), sl(Vb))
        U = s32()
        p.op("act", lambda h: h.copy(U.ap, psU.ap), reads=[psU.acc], writes=[U.acc])
        psW = pst()
        mm(sl(psW), sl(Kbg), sl(Tt16))
        WT = s16()
        p.op("dve", lambda h: h.tensor_copy(WT.ap, psW.ap), reads=[psW.acc], writes=[WT.acc])
        S16s = Slot(S16, ci * 128)
        S32s = Slot(S32, ci * 128)
        S16ap = (S16.t[:, ci, :], S16s.acc)
        S32ap = S32.t[:, ci, :]
        psWS = pst()
        mm(sl(psWS), sl(WT), S16ap)
        Vn = s16()
        p.op("dve", lambda h: h.tensor_tensor(Vn.ap, U.ap, psWS.ap, ALU.subtract), reads=[U.acc, psWS.acc], writes=[Vn.acc])
        psO = pst()
        mm(sl(psO), S16ap, sl(QgT), start=True, stop=False)
        mm(sl(psO), sl(Vn), sl(intraT), start=False, stop=True)
        psS = pst()
        mm(sl(psS), sl(Kdec), sl(Vn))
        p.op("dve", lambda h: h.scalar_tensor_tensor(S32ap, S32ap, egtC.ap, psS.ap, ALU.mult, ALU.add),
             reads=[S32s.acc, egtC.acc, psS.acc], writes=[S32s.acc])
        if self.debug and (t, hh) in ((0, 0), (7, 0)):
            for nm, sl_ in (("A", A), ("P", P), ("DTs", DTs), ("DAs", DAs), ("KbT", KbT), ("Tt16", Tt16), ("Tt32", Tt32), ("U", U),
                            ("WT", WT), ("Vn", Vn), ("intraT", intraT), ("QgT", QgT), ("Kdec", Kdec), ("DTi", DTi), ("Vb", Vb), ("Kbg", Kbg)):
                dd = self.outp("dbgi%d%d_%s" % (d, t, nm), [128, 128], sl_.t.t.dtype)
                p.dma("sp", dd, sl_.ap, reads=[sl_.acc], is_output=True)
            dd = self.outp("dbgi%d%d_S" % (d, t), [128, 128], F32)
            p.dma("sp", dd, S32ap, reads=[S32s.acc], is_output=True)
        oslot_ap = self.oT.t[:, hh, tl:th]
        oacc = acc3(self.oT, [hh], tl, th)
        if first:
            p.op("act", lambda h: h.copy(oslot_ap, psO.ap), reads=[psO.acc], writes=[oacc])
        else:
            ot = s32()
            p.op("dve", lambda h: h.tensor_tensor(ot.ap, psO.ap, oslot_ap, ALU.add), reads=[psO.acc, oacc], writes=[ot.acc])
            sq = s16()
            p.op("act", lambda h: h.activation(sq.ap, ot.ap, AF.Square), reads=[ot.acc], writes=[sq.acc])
            pss = pst()
            mm(sl(pss), (self.ones.t[:, :], acc(self.ones)), sl(sq))
            rs = s32()
            p.op("act", lambda h: h.activation(rs.ap, pss.ap, AF.Sqrt, bias=self.eps1.t[:, 0:1], scale=1.0 / 128.0),
                 reads=[pss.acc, acc(self.eps1)], writes=[rs.acc])
            p.op("dve", lambda h: h.reciprocal(rs.ap, rs.ap), reads=[rs.acc], writes=[rs.acc])
            p.op("dve", lambda h: h.scalar_tensor_tensor(ot.ap, ot.ap, onw, rs.ap, ALU.mult, ALU.mult),
                 reads=[ot.acc, rs.acc, acc(l2s)], writes=[ot.acc])
            zacc = acc3(self.zT, [hh], tl, th)
            p.op("pool", lambda h: h.tensor_tensor(oslot_ap, ot.ap, self.zT.t[:, hh, tl:th], ALU.mult),
                 reads=[ot.acc, zacc], writes=[oacc])

    for hp in range(4):
        chains = [(d, 2 * hp + e) for e in range(2) for d in range(2)]
        for ci, (d, hh) in enumerate(chains):
            S32s = Slot(S32, ci * 128)
            S16s = Slot(S16, ci * 128)
            p.dma("sp", S32.t[:, ci, :], s0[d, hh, :, :], writes=[S32s.acc])
            p.op("dve", lambda h, ci=ci: h.tensor_scalar(S32.t[:, ci, :], S32.t[:, ci, :], keep, None, ALU.mult),
                 reads=[S32s.acc, acc(l2s)], writes=[S32s.acc])
            p.op("act", lambda h, ci=ci: h.copy(S16.t[:, ci, :], S32.t[:, ci, :]), reads=[S32s.acc], writes=[S16s.acc])
        for n in range(8):
            for ci, (d, hh) in enumerate(chains):
                t = n if d == 0 else 7 - n
                instance(t, d, hh, ci, n < 4)
                S32s = Slot(S32, ci * 128)
                S16s = Slot(S16, ci * 128)
                if n % 2 == 1:
                    p.dma("sp", nst[d, t // 2, hh, :, :], S32.t[:, ci, :], reads=[S32s.acc], is_output=True)
                    if n < 7:
                        p.op("dve", lambda h, ci=ci: h.tensor_scalar(S32.t[:, ci, :], S32.t[:, ci, :], keep, None, ALU.mult),
                             reads=[S32s.acc, acc(l2s)], writes=[S32s.acc])
                if n < 7:
                    p.op("act", lambda h, ci=ci: h.copy(S16.t[:, ci, :], S32.t[:, ci, :]), reads=[S32s.acc], writes=[S16s.acc])
    if self.debug:
        do = self.outp("dbg_oT", [128, 8, 1024], BF16)
        p.dma("sp", do, self.oT.t[:, :, :], reads=[acc(self.oT)], is_output=True)
    p.dma("sp", self.ropeC.t[:, :], self.inp("ropeC", [128, 1024]), writes=[acc(self.ropeC)])
    p.dma("sp", self.ropeS.t[:, :], self.inp("ropeS", [128, 1024]), writes=[acc(self.ropeS)])
    self.out_proj(li)


Builder.layer_c = layer_c
```

```python
import numpy as np
from contextlib import ExitStack
import concourse.bass as bass
import concourse.mybir as mybir
from concourse.bass_utils import run_bass_kernel_spmd

F32 = mybir.dt.float32
BF16 = mybir.dt.bfloat16
AF = mybir.ActivationFunctionType
ALU = mybir.AluOpType
AX = mybir.AxisListType

ENGS = ("pe", "act", "dve", "pool", "sp")
NEG = -30000.0


class T:
    def __init__(self, prog, name, shape, dtype, gran, psum=False):
        self.name = name
        self.shape = shape
        self.F = int(np.prod(shape[1:]))
        self.gran = gran
        self.nreg = (self.F + gran - 1) // gran
        self.w = [None] * self.nreg
        self.r = [[] for _ in range(self.nreg)]
        self.psum = psum
        if psum:
            self.t = prog.es.enter_context(prog.nc.psum_tensor("pp_" + name, list(shape), dtype))
        else:
            self.t = prog.es.enter_context(prog.nc.sbuf_tensor("sb_" + name, list(shape), dtype))

    def regs(self, a, b):
        return range(a // self.gran, (b - 1) // self.gran + 1)


class Acc:
    def __init__(self, t, ranges):
        self.t = t
        self.ranges = ranges


def acc(t, a=None, b=None):
    if a is None:
        return Acc(t, [(0, t.F)])
    return Acc(t, [(a, b)])


def acc3(t, kts, lo, hi):
    inner = t.shape[-1] if len(t.shape) == 3 else None
    return Acc(t, [(k * inner + lo, k * inner + hi) for k in kts])


class Prog:
    def __init__(self, nc, n_dma_sems=8):
        self.nc = nc
        self.es = ExitStack()
        self.ops = {e: [] for e in ENGS}
        self.sem = {}
        for e in ENGS:
            self.sem[e] = self.es.enter_context(nc.semaphore("s_" + e))
        self.cnt = {e: 0 for e in ENGS}
        self.waited = {e: {} for e in ENGS}
        self.dsem = [self.es.enter_context(nc.semaphore("d%d" % i)) for i in range(n_dma_sems)]
        self.dval = [0] * n_dma_sems
        self.dnext = 0
        self.final_events = []

    def tensor(self, name, shape, dtype, gran=None, psum=False):
        F = int(np.prod(shape[1:]))
        return T(self, name, shape, dtype, gran or F, psum)

    def _deps(self, reads, writes):
        deps = set()
        for a in reads:
            for (lo, hi) in a.ranges:
                for g in a.t.regs(lo, hi):
                    if a.t.w[g] is not None:
                        deps.add(a.t.w[g])
        for a in writes:
            for (lo, hi) in a.ranges:
                for g in a.t.regs(lo, hi):
                    if a.t.w[g] is not None:
                        deps.add(a.t.w[g])
                    for ev in a.t.r[g]:
                        deps.add(ev)
        return deps

    def _mark(self, reads, writes, ev):
        for a in reads:
            for (lo, hi) in a.ranges:
                for g in a.t.regs(lo, hi):
                    a.t.r[g].append(ev)
        for a in writes:
            for (lo, hi) in a.ranges:
                for g in a.t.regs(lo, hi):
                    a.t.w[g] = ev
                    a.t.r[g] = []

    def _waits(self, eng, deps, skip_same=False):
        best = {}
        for (k, v) in deps:
            if skip_same and k == eng:
                continue
            if v > best.get(k, 0):
                best[k] = v
        out = []
        for k, v in best.items():
            if self.waited[eng].get(k, 0) >= v:
                continue
            self.waited[eng][k] = v
            out.append((k, v))
        return out

    def _semh(self, k):
        if isinstance(k, str):
            return self.sem[k]
        if isinstance(k, tuple):
            return self._swh[k]
        return self.dsem[k]

    def op(self, eng, fn, reads=(), writes=(), skip_same=False):
        pr = [a for a in reads if a.t.psum]
        if pr:
            reads = [a for a in reads if not a.t.psum]
            writes = list(writes) + pr
        deps = self._deps(reads, writes)
        waits = self._waits(eng, deps, skip_same)
        self.cnt[eng] += 1
        ev = (eng, self.cnt[eng])
        self._mark(reads, writes, ev)
        semh = self.sem[eng]
        wl = [(self._semh(k), v) for (k, v) in waits]

        def emit(h):
            for (s, v) in wl:
                h.wait_ge(s, v)
            fn(h).then_inc(semh, 1)

        self.ops[eng].append(emit)
        return ev

    def dma(self, eng, out_ap, in_ap, reads=(), writes=(), is_output=False, slot=None):
        if eng == "pool":
            return self.dma_sw(out_ap, in_ap, reads, writes, slot)
        deps = self._deps(reads, writes)
        i = self.dnext
        self.dnext = (self.dnext + 1) % len(self.dsem)
        if self.dval[i] > 0:
            deps.add((i, self.dval[i]))
        waits = self._waits(eng, deps)
        self.dval[i] += 16
        ev = (i, self.dval[i])
        self._mark(reads, writes, ev)
        semh = self.dsem[i]
        wl = [(self._semh(k), v) for (k, v) in waits]

        def emit(h):
            for (s, v) in wl:
                h.wait_ge(s, v)
            h.dma_start(out=out_ap, in_=in_ap).then_inc(semh, 16)

        self.ops[eng].append(emit)
        if is_output:
            self.final_events.append(ev)
        return ev

    def dma_sw(self, out_ap, in_ap, reads, writes, slot):
        eng = "pool"
        if not hasattr(self, "_swh"):
            self._swh = {}
        n = len(self._swh)
        semh = self.es.enter_context(self.nc.semaphore("w%d" % n))
        key = ("sw", n)
        self._swh[key] = semh
        deps = self._deps(reads, writes)
        waits = self._waits(eng, deps)
        ev = (key, 16)
        self._mark(reads, writes, ev)
        wl = [(self._semh(k), v) for (k, v) in waits]

        def emit(h):
            for (s, v) in wl:
                h.wait_ge(s, v)
            h.dma_start(out=out_ap, in_=in_ap).then_inc(semh, 16)

        self.ops[eng].append(emit)
        return ev

    def finish(self):
        waits = self._waits("sp", set(self.final_events))
        wl = [(self._semh(k), v) for (k, v) in waits]

        def emit(h):
            for (s, v) in wl:
                h.wait_ge(s, v)

        self.ops["sp"].append(emit)
        nc = self.nc
        ops = self.ops
        with nc.Block() as block:
            @block.tensor
            def _(h):
                for f in ops["pe"]:
                    f(h)

            @block.scalar
            def _(h):
                for f in ops["act"]:
                    f(h)

            @block.vector
            def _(h):
                for f in ops["dve"]:
                    f(h)

            @block.gpsimd
            def _(h):
                for f in ops["pool"]:
                    f(h)

            @block.sync
            def _(h):
                for f in ops["sp"]:
                    f(h)
        self.es.close()


D = 1024
NT = 1024
KT = 8
IN_COLS = {0: 2560, 1: 4096, 2: 4128, 3: 2560}
KIND = {0: "a", 1: "b", 2: "c", 3: "a"}


def lambda_init_for(layer):
    import math
    return 0.8 - 0.6 * math.exp(-0.3 * layer)


class Builder:
    def __init__(self, layers=(0, 1, 2, 3), debug=False):
        self.layers = layers
        self.debug = debug
        self.nc = nc = bass.Bass("TRN2", target_bir_lowering=False)
        self.p = p = Prog(nc)
        self.din = {}
        self.dout = {}
        self.rr = {}
        self._alloc()

    def inp(self, name, shape, dtype=F32):
        if name not in self.din:
            self.din[name] = self.nc.dram_tensor(name, list(shape), dtype, kind="ExternalInput").ap()
        return self.din[name]

    def outp(self, name, shape, dtype=F32):
        if name not in self.dout:
            self.dout[name] = self.nc.dram_tensor(name, list(shape), dtype, kind="ExternalOutput").ap()
        return self.dout[name]

    def nxt(self, key, n):
        v = self.rr.get(key, 0)
        self.rr[key] = (v + 1) % n
        return v

    def _alloc(self):
        p = self.p
        self.xT = p.tensor("xT", [128, 8, 1024], F32, gran=128)
        self.hT = p.tensor("hT", [128, 8, 1024], BF16, gran=128)
        self.qT = p.tensor("qT", [128, 8, 1024], BF16, gran=128)
        self.kT = p.tensor("kT", [128, 8, 1024], BF16, gran=128)
        self.kcT = p.tensor("kcT", [128, 8, 512], BF16, gran=128)
        self.VB = p.tensor("VB", [128, 12, 1024], BF16, gran=128)
        self.zT = p.tensor("zT", [128, 8, 1024], BF16, gran=128)
        self.oT = self.hT
        self.NW = 2
        self.W = [p.tensor("W%d" % i, [128, 8, 1024], BF16) for i in range(self.NW)]
        self.rstd = p.tensor("rstd", [128, 1024], F32, gran=512)
        self.ropeC = p.tensor("ropeC", [128, 1024], F32, gran=128)
        self.ropeS = p.tensor("ropeS", [128, 1024], F32, gran=128)
        self.Rm = p.tensor("Rm", [128, 128], BF16)
        self.ones = p.tensor("ones", [128, 128], BF16)
        self.NPT = 4
        self.PT = [p.tensor("PT%d" % i, [128, 512], BF16, gran=128) for i in range(self.NPT)]
        self.NST = 3
        self.ST = [p.tensor("ST%d" % i, [128, 512], F32, gran=128) for i in range(self.NST)]
        self.NTMP = 4
        self.TMP = [p.tensor("TMP%d" % i, [128, 512], F32, gran=128) for i in range(self.NTMP)]
        self.maskP = p.tensor("maskP", [128, 8, 128], BF16)
        self.maskN = p.tensor("maskN", [128, 8, 128], BF16)
        self.ctxb = p.tensor("ctxb", [128, 1], F32)
        self.zero1 = p.tensor("zero1", [128, 1], F32)
        self.eps1 = p.tensor("eps1", [128, 1], F32)
        self.cond = p.tensor("cond", [128, 8], F32)
        self.scond = p.tensor("scond", [128, 8], BF16)
        self.vec = p.tensor("vec", [128, 64], F32)
        self.mT = p.tensor("mT", [128, 24], F32)
        self.gain = p.tensor("gain", [128, 8], F32)
        self.sink = p.tensor("sink", [128, 16], F32)
        self.esink = p.tensor("esink", [128, 16], F32)
        self.finw = p.tensor("finw", [128, 8], F32)
        self.PS = [p.tensor("ps%d" % i, [128, 512], F32, psum=True) for i in range(8)]
        self.wcount = 0

    def ps_proj(self):
        return self.PS[self.nxt("pj", 2)]

    def ps_rope(self):
        return self.PS[2]

    def ps_s(self):
        return self.PS[3 + self.nxt("s", 3)]

    def ps_o(self):
        return self.PS[6 + self.nxt("o", 2)]

    def plan_weights(self):
        plan = []
        for li in self.layers:
            for c0 in range(0, 3072, 1024):
                plan.append(("l%d_mod_w" % li, 3072, c0, 1024))
            nc_ = IN_COLS[li]
            for c0 in range(0, nc_, 1024):
                plan.append(("l%d_in_w" % li, nc_, c0, min(1024, nc_ - c0)))
            plan.append(("l%d_out_w" % li, 1024, 0, 1024))
        self.wplan = plan
        self.wissued = 0
        self.wused = 0

    def _issue_block(self):
        p = self.p
        wname, ncols, c0, ccount = self.wplan[self.wissued]
        w = self.inp(wname, [1024, ncols])
        buf = self.W[self.wissued % self.NW]
        self.wissued += 1
        src = w.rearrange("(kt q) c -> q kt c", q=128)[:, :, c0:c0 + ccount]
        p.dma("pool", buf.t[:, :, 0:ccount], src, writes=[acc(buf)])

    def next_block(self, wname, c0):
        i = self.wused
        assert self.wplan[i][0] == wname and self.wplan[i][2] == c0, (self.wplan[i], wname, c0)
        while self.wissued <= min(i + 1, len(self.wplan) - 1):
            self._issue_block()
        self.wused += 1
        return self.W[i % self.NW]

    def setup(self):
        p = self.p
        xin = self.inp("xT", [128, 8, 1024])
        for kt in range(8):
            p.dma("sp", self.xT.t[:, kt, :], xin[:, kt, :], writes=[acc3(self.xT, [kt], 0, 1024)])
        p.dma("sp", self.cond.t[:, :], self.inp("cond", [128, 8]), writes=[acc(self.cond)])
        p.dma("sp", self.ropeC.t[:, :], self.inp("ropeC", [128, 1024]), writes=[acc(self.ropeC)])
        p.dma("sp", self.ropeS.t[:, :], self.inp("ropeS", [128, 1024]), writes=[acc(self.ropeS)])
        p.dma("sp", self.ctxb.t[:, :], self.inp("ctxb", [128, 1]), writes=[acc(self.ctxb)])
        p.dma("sp", self.finw.t[:, :], self.inp("finw", [128, 8]), writes=[acc(self.finw)])
        p.dma("pool", self.Rm.t[:, :], self.inp("Rm", [128, 128]), writes=[acc(self.Rm)])
        p.dma("pool", self.maskP.t[:, :, :], self.inp("maskP", [128, 8, 128]), writes=[acc(self.maskP)])
        p.dma("pool", self.maskN.t[:, :, :], self.inp("maskN", [128, 8, 128]), writes=[acc(self.maskN)])
        p.op("dve", lambda h: h.memset(self.ones.t[:, :], 1.0), writes=[acc(self.ones)])
        p.op("dve", lambda h: h.memset(self.zero1.t[:, :], 0.0), writes=[acc(self.zero1)])
        p.op("dve", lambda h: h.memset(self.eps1.t[:, :], 1e-6), writes=[acc(self.eps1)])
        p.op("act", lambda h: h.activation(self.scond.t[:, :], self.cond.t[:, :], AF.Silu),
             reads=[acc(self.cond)], writes=[acc(self.scond)])

    def compute_rstd(self):
        p = self.p
        sq = self.hT
        for c in range(2):
            lo, hi = c * 512, (c + 1) * 512
            for kt in range(8):
                p.op("act", lambda h, kt=kt, lo=lo, hi=hi: h.activation(sq.t[:, kt, lo:hi], self.xT.t[:, kt, lo:hi], AF.Square),
                     reads=[acc3(self.xT, [kt], lo, hi)], writes=[acc3(sq, [kt], lo, hi)])
            ps = self.ps_proj()
            for kt in range(8):
                p.op("pe", lambda h, kt=kt, lo=lo, hi=hi, ps=ps: h.matmul(ps.t[:, :], self.ones.t[:, :], sq.t[:, kt, lo:hi],
                                                                     start=(kt == 0), stop=(kt == 7)),
                     reads=[acc(self.ones), acc3(sq, [kt], lo, hi)], writes=[acc(ps)], skip_same=True)
            p.op("act", lambda h, ps=ps, lo=lo, hi=hi: h.activation(self.rstd.t[:, lo:hi], ps.t[:, :], AF.Sqrt,
                                                               bias=self.eps1.t[:, 0:1], scale=1.0 / 1024.0),
                 reads=[acc(ps), acc(self.eps1)], writes=[acc(self.rstd, lo, hi)])
            p.op("dve", lambda h, lo=lo, hi=hi: h.reciprocal(self.rstd.t[:, lo:hi], self.rstd.t[:, lo:hi]),
                 reads=[acc(self.rstd, lo, hi)], writes=[acc(self.rstd, lo, hi)])

    def modulate(self, li):
        p = self.p
        vec = self.vec
        p.dma("sp", vec.t[:, 0:8], self.inp("l%d_norm_w" % li, [128, 8]), writes=[acc(vec)])
        p.dma("sp", vec.t[:, 8:32], self.inp("l%d_mod_b" % li, [128, 24]), writes=[acc(vec)])
        self.compute_rstd()
        ps = self.PS[2]
        for b in range(3):
            buf = self.next_block("l%d_mod_w" % li, b * 1024)
            for cl in range(8):
                c = b * 8 + cl
                for kt in range(8):
                    p.op("pe", lambda h, buf=buf, cl=cl, c=c, kt=kt: h.matmul(
                        ps.t[:, c:c + 1], buf.t[:, kt, cl * 128:(cl + 1) * 128], self.scond.t[:, kt:kt + 1],
                        start=(kt == 0), stop=(kt == 7)),
                        reads=[acc(buf), acc(self.scond)], writes=[acc(ps)], skip_same=True)
        p.op("dve", lambda h: h.tensor_tensor(self.mT.t[:, :], ps.t[:, 0:24], vec.t[:, 8:32], ALU.add),
             reads=[acc(ps), acc(vec)], writes=[acc(self.mT)])
        p.op("dve", lambda h: h.scalar_tensor_tensor(self.gain.t[:, :], self.mT.t[:, 8:16], 1.0, vec.t[:, 0:8],
                                                     ALU.add, ALU.mult),
             reads=[acc(self.mT), acc(vec)], writes=[acc(self.gain)])
        for kt in range(8):
            for c in range(2):
                lo, hi = c * 512, (c + 1) * 512
                tmp = self.TMP[self.nxt("tmp", self.NTMP)]
                p.op("dve", lambda h, kt=kt, lo=lo, hi=hi, tmp=tmp: h.scalar_tensor_tensor(
                    tmp.t[:, :], self.xT.t[:, kt, lo:hi], self.gain.t[:, kt:kt + 1], self.rstd.t[:, lo:hi],
                    ALU.mult, ALU.mult),
                    reads=[acc3(self.xT, [kt], lo, hi), acc(self.gain), acc(self.rstd, lo, hi)], writes=[acc(tmp)])
                p.op("act", lambda h, kt=kt, lo=lo, hi=hi, tmp=tmp: h.activation(
                    self.hT.t[:, kt, lo:hi], tmp.t[:, :], AF.Identity, bias=self.mT.t[:, kt:kt + 1], scale=1.0),
                    reads=[acc(tmp), acc(self.mT)], writes=[acc3(self.hT, [kt], lo, hi)])

    def proj_tile(self, buf, cl, c, src=None):
        p = self.p
        src = src or self.hT
        ps = self.ps_proj()
        lo, hi = c * 512, (c + 1) * 512
        for kt in range(8):
            p.op("pe", lambda h, kt=kt: h.matmul(ps.t[:, :], buf.t[:, kt, cl * 128:(cl + 1) * 128], src.t[:, kt, lo:hi],
                                                 start=(kt == 0), stop=(kt == 7)),
                 reads=[acc(buf), acc3(src, [kt], lo, hi)], writes=[acc(ps)], skip_same=True)
        return ps

    def rope_evac(self, ps, dst, dt_, c):
        p = self.p
        lo, hi = c * 512, (c + 1) * 512
        qb = self.PT[self.nxt("pt", self.NPT)]
        p.op("act", lambda h: h.copy(qb.t[:, :], ps.t[:, :]), reads=[acc(ps)], writes=[acc(qb)])
        pr = self.ps_rope()
        p.op("pe", lambda h: h.matmul(pr.t[:, :], self.Rm.t[:, :], qb.t[:, :], start=True, stop=True),
             reads=[acc(self.Rm), acc(qb)], writes=[acc(pr)], skip_same=True)
        t1 = self.TMP[self.nxt("tmp", self.NTMP)]
        t2 = self.TMP[self.nxt("tmp", self.NTMP)]
        p.op("dve", lambda h: h.tensor_tensor(t1.t[:, :], ps.t[:, :], self.ropeC.t[:, lo:hi], ALU.mult),
             reads=[acc(ps), acc(self.ropeC)], writes=[acc(t1)])
        p.op("dve", lambda h: h.tensor_tensor(t2.t[:, :], pr.t[:, :], self.ropeS.t[:, lo:hi], ALU.mult),
             reads=[acc(pr), acc(self.ropeS)], writes=[acc(t2)])
        p.op("pool", lambda h: h.tensor_tensor(dst.t[:, dt_, lo:hi], t1.t[:, :], t2.t[:, :], ALU.add),
             reads=[acc(t1), acc(t2)], writes=[acc3(dst, [dt_], lo, hi)])

    def out_proj(self, li):
        p = self.p
        for b in range(1):
            buf = self.next_block("l%d_out_w" % li, 0)
            for cl in range(8):
                mt = cl
                for c in range(2):
                    lo, hi = c * 512, (c + 1) * 512
                    ps = self.proj_tile(buf, cl, c, src=self.oT)
                    p.op("dve", lambda h, ps=ps, mt=mt, lo=lo, hi=hi: h.scalar_tensor_tensor(
                        self.xT.t[:, mt, lo:hi], ps.t[:, :], self.mT.t[:, 16 + mt:17 + mt], self.xT.t[:, mt, lo:hi],
                        ALU.mult, ALU.add),
                        reads=[acc(ps), acc(self.mT), acc3(self.xT, [mt], lo, hi)], writes=[acc3(self.xT, [mt], lo, hi)])

    def final(self):
        p = self.p
        self.compute_rstd()
        yT = self.outp("yT", [128, 8, 1024])
        for kt in range(8):
            for c in range(2):
                lo, hi = c * 512, (c + 1) * 512
                st = self.ST[self.nxt("st", self.NST)]
                p.op("dve", lambda h, kt=kt, lo=lo, hi=hi, st=st: h.scalar_tensor_tensor(
                    st.t[:, :], self.xT.t[:, kt, lo:hi], self.finw.t[:, kt:kt + 1], self.rstd.t[:, lo:hi],
                    ALU.mult, ALU.mult),
                    reads=[acc3(self.xT, [kt], lo, hi), acc(self.finw), acc(self.rstd, lo, hi)], writes=[acc(st)])
                p.dma("sp", yT[:, kt, lo:hi], st.t[:, :], reads=[acc(st)], is_output=True)

    def dump_x(self, name):
        out = self.outp(name, [128, 8, 1024])
        for kt in range(8):
            self.p.dma("sp", out[:, kt, :], self.xT.t[:, kt, :], reads=[acc3(self.xT, [kt], 0, 1024)], is_output=True)


def layer_a(self, li):
    p = self.p
    self.modulate(li)
    p.dma("sp", self.sink.t[:, :], self.inp("l%d_sink" % li, [128, 16]), writes=[acc(self.sink)])
    p.op("act", lambda h: h.activation(self.esink.t[:, :], self.sink.t[:, :], AF.Exp),
         reads=[acc(self.sink)], writes=[acc(self.esink)])
    kc = self.inp("l%d_kcT" % li, [128, 2, 512])
    p.dma("pool", self.kcT.t[:, 0:2, :], kc, writes=[acc3(self.kcT, [0, 1], 0, 512)], slot="kcT")
    vc = self.inp("l%d_vc" % li, [512, 256])
    VB = self.VB
    for kt in range(12):
        v4 = VB.t[:, kt, 0:512].rearrange("q (g e) -> q g e", g=4)
        p.op("pool", lambda h, v4=v4: h.memset(v4[:, :, 64:128], 1.0), writes=[acc3(VB, [kt], 0, 512)])
    for t in range(4):
        v4 = VB.t[:, 8 + t, 0:512].rearrange("q (g e) -> q g e", g=4)
        p.dma("pool", v4[:, :, 0:64], vc[t * 128:(t + 1) * 128, :].rearrange("q (g e) -> q g e", g=4),
              writes=[acc3(VB, [8 + t], 0, 512)], slot="VBc%d" % t)
    wn = "l%d_in_w" % li
    nkT = self.outp("l%d_nkT" % li, [128, 2, 1024])
    nv = self.outp("l%d_nv" % li, [1024, 256])
    for b in range(3):
        buf = self.next_block(wn, b * 1024)
        if b == 0:
            for cl in range(8):
                for c in range(2):
                    ps = self.proj_tile(buf, cl, c)
                    self.rope_evac(ps, self.qT, cl, c)
        elif b == 1:
            for cl in range(2):
                for c in range(2):
                    lo, hi = c * 512, (c + 1) * 512
                    ps = self.proj_tile(buf, cl, c)
                    st = self.ST[self.nxt("st", self.NST)]
                    p.op("act", lambda h, ps=ps, st=st: h.copy(st.t[:, :], ps.t[:, :]), reads=[acc(ps)], writes=[acc(st)])
                    p.dma("sp", nkT[:, cl, lo:hi], st.t[:, :], reads=[acc(st)], is_output=True)
                    self.rope_evac(ps, self.kT, cl, c)
            for tt in range(8):
                ps = self.ps_proj()
                for kt in range(8):
                    p.op("pe", lambda h, kt=kt, tt=tt, ps=ps, buf=buf: h.matmul(
                        ps.t[:, 0:256], self.hT.t[:, kt, tt * 128:(tt + 1) * 128], buf.t[:, kt, 256:512],
                        start=(kt == 0), stop=(kt == 7)),
                        reads=[acc(buf), acc3(self.hT, [kt], tt * 128, (tt + 1) * 128)], writes=[acc(ps)], skip_same=True)
                st = self.ST[self.nxt("st", self.NST)]
                p.op("dve", lambda h, ps=ps, st=st: h.tensor_copy(st.t[:, 0:256], ps.t[:, 0:256]), reads=[acc(ps)], writes=[acc(st)])
                p.dma("sp", nv[tt * 128:(tt + 1) * 128, :], st.t[:, 0:256], reads=[acc(st)], is_output=True)
                v4 = VB.t[:, tt, 0:512].rearrange("q (g e) -> q g e", g=4)
                p.op("act", lambda h, ps=ps, v4=v4: h.copy(v4[:, :, 0:64], ps.t[:, 0:256].rearrange("q (g e) -> q g e", g=4)),
                     reads=[acc(ps)], writes=[acc3(VB, [tt], 0, 512)])
            zlist = [(4 + i, i) for i in range(4)]
        if b >= 1:
            if b == 2:
                zlist = [(i, 4 + i) for i in range(4)]
            for (cl, zt) in zlist:
                for c in range(2):
                    lo, hi = c * 512, (c + 1) * 512
                    ps = self.proj_tile(buf, cl, c)
                    p.op("act", lambda h, ps=ps, zt=zt, lo=lo, hi=hi: h.activation(self.zT.t[:, zt, lo:hi], ps.t[:, :], AF.Silu),
                         reads=[acc(ps)], writes=[acc3(self.zT, [zt], lo, hi)])
    items = []
    for Tk in range(2):
        for u in range(2):
            for j in range(8):
                blocks = []
                if j > 0:
                    blocks.append(("P", j - 1))
                blocks.append(("L", j))
                if j < 7:
                    blocks.append(("N", j + 1))
                for t in range(4):
                    blocks.append(("C", t))
                for bi, (kind, kb) in enumerate(blocks):
                    items.append(dict(Tk=Tk, u=u, j=j, kind=kind, kb=kb, bi=bi, nb=len(blocks)))

    def emit_s(it):
        Tk, u, j, kind, kb = it["Tk"], it["u"], it["j"], it["kind"], it["kb"]
        pl, ph = u * 64, (u + 1) * 64
        qlo, qhi = j * 128, (j + 1) * 128
        sb = self.ps_s()
        if kind == "C":
            lhsT = self.kcT.t[pl:ph, Tk, kb * 128:(kb + 1) * 128]
            rl = acc3(self.kcT, [Tk], kb * 128, (kb + 1) * 128)
            it["vt"] = 8 + kb
            bias = self.ctxb
        else:
            lhsT = self.kT.t[pl:ph, Tk, kb * 128:(kb + 1) * 128]
            rl = acc3(self.kT, [Tk], kb * 128, (kb + 1) * 128)
            it["vt"] = kb
            bias = self.zero1
        rhs = self.qT.t[pl:ph, 4 * Tk:4 * Tk + 4, qlo:qhi]
        p.op("pe", lambda h: h.matmul(sb.t[:, :].rearrange("q (g e) -> q g e", g=4), lhsT, rhs, start=True, stop=True),
             reads=[rl, acc3(self.qT, range(4 * Tk, 4 * Tk + 4), qlo, qhi)], writes=[acc(sb)], skip_same=True)
        pt = self.PT[self.nxt("pt", self.NPT)]
        it["pt"] = pt
        p.op("act", lambda h: h.activation(pt.t[:, :], sb.t[:, :], AF.Exp, bias=bias.t[:, 0:1], scale=0.125),
             reads=[acc(sb), acc(bias)], writes=[acc(pt)])
        if kind in ("P", "N"):
            mk = self.maskP if kind == "P" else self.maskN
            p.op("pool", lambda h: h.tensor_tensor(
                pt.t[:, :].rearrange("q (g e) -> q g e", g=4), pt.t[:, :].rearrange("q (g e) -> q g e", g=4),
                mk.t[:, j, :].unsqueeze(1).broadcast_to([128, 4, 128]), ALU.mult),
                reads=[acc(pt), acc(mk)], writes=[acc(pt)])

    cur_po = [None]

    def emit_pv(it):
        Tk, u, j, bi, nb = it["Tk"], it["u"], it["j"], it["bi"], it["nb"]
        hkv = 2 * Tk + u
        pl, ph = u * 64, (u + 1) * 64
        qlo, qhi = j * 128, (j + 1) * 128
        if bi == 0:
            cur_po[0] = self.ps_o()
        po = cur_po[0]
        vt, pt = it["vt"], it["pt"]
        p.op("pe", lambda h: h.matmul(po.t[:, :], VB.t[:, vt, hkv * 128:(hkv + 1) * 128], pt.t[:, :], start=(bi == 0), stop=(bi == nb - 1)),
             reads=[acc3(VB, [vt], hkv * 128, (hkv + 1) * 128), acc(pt)], writes=[acc(po)], skip_same=True)
        if bi == nb - 1:
            tmp = self.TMP[self.nxt("tmp", self.NTMP)]
            p.op("dve", lambda h: h.tensor_tensor(
                tmp.t[64:128, :].rearrange("q (g e) -> q g e", g=4), po.t[64:128, :].rearrange("q (g e) -> q g e", g=4),
                self.esink.t[64:128, hkv * 4:hkv * 4 + 4].unsqueeze(2).broadcast_to([64, 4, 128]), ALU.add),
                reads=[acc(po), acc(self.esink)], writes=[acc(tmp)])
            p.op("dve", lambda h: h.reciprocal(tmp.t[64:128, :], tmp.t[64:128, :]), reads=[acc(tmp)], writes=[acc(tmp)])
            p.op("dve", lambda h: h.tensor_tensor(
                self.oT.t[pl:ph, 4 * Tk:4 * Tk + 4, qlo:qhi], po.t[0:64, :].rearrange("q (g e) -> q g e", g=4),
                tmp.t[64:128, :].rearrange("q (g e) -> q g e", g=4), ALU.mult),
                reads=[acc(po), acc(tmp)], writes=[acc3(self.oT, range(4 * Tk, 4 * Tk + 4), qlo, qhi)])

    LA = 2
    for i in range(len(items) + LA):
        if i < len(items):
            emit_s(items[i])
        if i >= LA:
            emit_pv(items[i - LA])
    for t in range(8):
        for c in range(2):
            lo, hi = c * 512, (c + 1) * 512
            p.op("pool", lambda h, t=t, lo=lo, hi=hi: h.tensor_tensor(self.oT.t[:, t, lo:hi], self.oT.t[:, t, lo:hi], self.zT.t[:, t, lo:hi], ALU.mult),
                 reads=[acc3(self.oT, [t], lo, hi), acc3(self.zT, [t], lo, hi)], writes=[acc3(self.oT, [t], lo, hi)])
    self.out_proj(li)


Builder.layer_a = layer_a


def build(layers=(0, 1, 2, 3), debug=False):
    b = Builder(layers, debug)
    b.plan_weights()
    b.setup()
    for li in layers:
        k = KIND[li]
        if k == "a":
            b.layer_a(li)
        elif k == "b":
            b.layer_b(li)
        else:
            b.layer_c(li)
        if debug:
            b.dump_x("dbg_x%d" % li)
    b.final()
    print("sbuf bytes remaining", b.nc.sbuf_bytes_remaining)
    b.p.finish()
    return b


def fm(v, n):
    return np.ascontiguousarray(np.asarray(v, np.float32).reshape(n, 128).T)


def to_fm3(a):
    a = np.asarray(a, np.float32)
    n = a.shape[1] // 128
    return np.ascontiguousarray(a.T.reshape(n, 128, a.shape[0]).transpose(1, 0, 2))


def from_fm3(a):
    return np.ascontiguousarray(a.transpose(2, 1, 0).reshape(a.shape[2], -1))


def perm_a():
    perm = np.zeros(1024, np.int64)
    for tq in range(8):
        Tk, g = tq // 4, tq % 4
        for u in range(2):
            hq = 4 * (2 * Tk + u) + g
            perm[tq * 128 + u * 64: tq * 128 + u * 64 + 64] = hq * 64 + np.arange(64)
    return perm


def rope_tables(sample):
    C = np.ones((128, 1024), np.float32)
    S = np.zeros((128, 1024), np.float32)
    if sample:
        tok = np.arange(1024)
        row = (tok // 64).astype(np.float32)
        col = (tok % 64).astype(np.float32)
        inv = (np.float32(10000.0) ** (-np.arange(16, dtype=np.float32) / np.float32(16))).astype(np.float32)
        for pp in range(128):
            d = pp % 64
            pos = row if d < 32 else col
            ang = (pos * inv[d % 16]).astype(np.float32)
            C[pp] = np.cos(ang)
            sgn = -1.0 if (d % 32) < 16 else 1.0
            S[pp] = sgn * np.sin(ang)
    return C, S


def rope_perm_matrix():
    Rm = np.zeros((128, 128), np.float32)
    for m in range(128):
        d = m % 64
        base = m - d
        partner = d + 16 if (d % 32) < 16 else d - 16
        Rm[base + partner, m] = 1.0
    return Rm


def win_masks(sample):
    mp = np.zeros((128, 8, 128), np.float32)
    mn = np.zeros((128, 8, 128), np.float32)
    k = np.arange(128)[:, None]
    q = np.arange(128)[None, :]
    for j in range(8):
        if sample:
            mp[:, j, :] = (k >= q)
            mn[:, j, :] = (k <= q)
        else:
            mp[:, j, :] = 1.0 if (j % 2 == 1) else 0.0
            mn[:, j, :] = 1.0 if (j % 2 == 0) else 0.0
    return mp, mn


_CACHE = {}
LAYERS = (0, 1, 2, 3)
DEBUG = False


def kernel(**inp):
    inp = {k: np.asarray(v) for k, v in inp.items()}
    layers = LAYERS
    key = (tuple(layers), DEBUG)
    if key not in _CACHE:
        _CACHE[key] = build(layers, DEBUG)
    b = _CACHE[key]
    pa = perm_a()
    shared = {}
    for li in layers:
        shared["l%d_mod_w" % li] = np.ascontiguousarray(inp["l%d_mod_w" % li], np.float32)
        w = inp["l%d_in_w" % li]
        ow = inp["l%d_out_w" % li]
        if KIND[li] == "a":
            w = np.concatenate([w[:, pa], w[:, 1024:1536], w[:, 1536 + pa]], axis=1)
            ow = ow[pa, :]
        shared["l%d_in_w" % li] = np.ascontiguousarray(w, np.float32)
        shared["l%d_out_w" % li] = np.ascontiguousarray(ow, np.float32)
        shared["l%d_norm_w" % li] = fm(inp["l%d_norm_w" % li], 8)
        shared["l%d_mod_b" % li] = fm(inp["l%d_mod_b" % li], 24)
        if KIND[li] == "a":
            shared["l%d_sink" % li] = np.ascontiguousarray(np.broadcast_to(inp["l%d_sink" % li].astype(np.float32)[None, :], (128, 16)))
    if 1 in layers:
        lamcat = np.concatenate([inp["l1_lambda_q1"], inp["l1_lambda_k1"], inp["l1_lambda_q2"], inp["l1_lambda_k2"]]).astype(np.float32)
        shared["l1_lam"] = np.ascontiguousarray(np.broadcast_to(lamcat[None, :], (128, 256)))
        shared["l1_subw"] = fm(inp["l1_subln_w"], 1)
    if 2 in layers:
        pp_ = np.arange(128)[:, None]
        ff_ = np.arange(128)[None, :]
        LOW, UPP, LOWI, UPPI = (pp_ > ff_), (pp_ < ff_), (pp_ >= ff_), (pp_ <= ff_)
        consts = [NEG * (1 - UPPI), NEG * (1 - LOWI), NEG * (1 - UPP), NEG * (1 - LOW),
                  -NEG * (1 - LOW), -NEG * (1 - UPP), 1.0 * UPPI, 1.0 * LOWI]
        shared["l2_consts"] = np.ascontiguousarray(np.concatenate([c_.astype(np.float32) for c_ in consts], axis=1))
        shared["l2_id32"] = np.eye(128, dtype=np.float32)
        cwv = inp["l2_conv_w"].astype(np.float32)
        shared["l2_cw"] = np.ascontiguousarray(cwv.reshape(3, 24, 128).transpose(2, 1, 0).reshape(128, 72))
    shared["finw"] = fm(inp["final_norm_w"], 8)
    shared["Rm"] = rope_perm_matrix()
    in_maps = []
    for core in range(8):
        sample = core < 4
        m = dict(shared)
        if sample:
            bb = core
            x = inp["x_sample"][bb]
            cond = inp["c"][bb]
        else:
            bb = core % 4
            s0 = 4 * (core - 4)
            x = inp["x_prompt"][s0:s0 + 4].reshape(1024, 1024)
            cond = inp["c_ctx"]
        m["xT"] = to_fm3(x)
        m["cond"] = fm(cond, 8)
        C, S = rope_tables(sample)
        m["ropeC"], m["ropeS"] = C, S
        mp, mn = win_masks(sample)
        m["maskP"], m["maskN"] = mp, mn
        m["ctxb"] = np.full((128, 1), 0.0 if sample else NEG, np.float32)
        for li in layers:
            if KIND[li] == "a":
                ck = inp["cache_l%d_k" % li][bb]
                cv = inp["cache_l%d_v" % li][bb]
                kcT = ck.transpose(1, 2, 0).reshape(2, 128, 512).transpose(1, 0, 2)
                m["l%d_kcT" % li] = np.ascontiguousarray(kcT, np.float32)
                m["l%d_vc" % li] = np.ascontiguousarray(cv.reshape(512, 256), np.float32)
        if 2 in layers:
            sc_ = np.zeros((128, 36), np.float32)
            sc_[:, 0] = 1.0 if sample else 0.0
            sc_[:, 1] = 0.0 if sample else 1.0
            sc_[:, 2] = inp["l2_onorm_w"].astype(np.float32)
            sc_[:, 3] = 1.0
            sc_[:, 4:20] = inp["l2_a_log"].astype(np.float32).reshape(1, 16)
            sc_[:, 20:36] = inp["l2_dt_bias"].astype(np.float32).reshape(1, 16)
            m["l2_scal"] = sc_
            m["l2_s0"] = np.ascontiguousarray(inp["state_l2"][bb], np.float32)
        if 1 in layers:
            m["l1_kcT"] = to_fm3(inp["cache_l1_k"][bb].reshape(512, 1024))
            m["l1_vc"] = np.ascontiguousarray(inp["cache_l1_v"][bb].reshape(512, 1024), np.float32)
            bt = np.zeros((128, 48), np.float32)
            if not sample:
                for qc in range(4):
                    for kb in range(12):
                        if not (kb < 8 and kb // 2 == qc):
                            bt[:, qc * 12 + kb] = NEG
            m["l1_bias"] = bt
        in_maps.append({k: m[k] for k in b.din})
    res = run_bass_kernel_spmd(b.nc, in_maps, core_ids=list(range(8)))
    R = res.results
    kernel.last = R
    y_sample = np.stack([from_fm3(R[c]["yT"]) for c in range(4)])
    y_prompt = np.concatenate([from_fm3(R[c]["yT"]).reshape(4, 256, 1024) for c in range(4, 8)])
    outs = {}
    for li in (0, 3):
        if li in layers:
            outs["nk%d" % li] = np.concatenate([from_fm3(R[c]["l%d_nkT" % li]).reshape(4, 256, 4, 64) for c in range(4, 8)])
            outs["nv%d" % li] = np.concatenate([R[c]["l%d_nv" % li].reshape(4, 256, 4, 64) for c in range(4, 8)])
        else:
            outs["nk%d" % li] = np.zeros((16, 256, 4, 64), np.float32)
            outs["nv%d" % li] = np.zeros((16, 256, 4, 64), np.float32)
    if 1 in layers:
        nk1 = np.concatenate([from_fm3(R[c]["l1_nkT"]).reshape(4, 256, 8, 2, 64) for c in range(4, 8)])
        nv1 = np.concatenate([R[c]["l1_nv"].reshape(4, 256, 8, 128) for c in range(4, 8)])
    else:
        nk1 = np.zeros((16, 256, 8, 2, 64), np.float32)
        nv1 = np.zeros((16, 256, 8, 128), np.float32)
    if 2 in layers:
        st = np.concatenate([R[c]["l2_nst"].transpose(1, 0, 2, 3, 4) for c in range(4, 8)])
    else:
        st = np.zeros((16, 2, 8, 128, 128), np.float32)
    return (y_prompt.astype(np.float32), y_sample.astype(np.float32), outs["nk0"], outs["nv0"], nk1, nv1, st,
            outs["nk3"], outs["nv3"])


def layer_b(self, li):
    p = self.p
    lam_init = lambda_init_for(li)
    if not hasattr(self, "lamv"):
        self.lamv = p.tensor("lamv", [128, 256], F32)
        self.lsm = p.tensor("lsm", [128, 8], F32)
        self.l1bias = p.tensor("l1bias", [128, 48], F32)
    lamv, lsm, l1bias = self.lamv, self.lsm, self.l1bias
    self.modulate(li)
    p.dma("sp", lamv.t[:, :], self.inp("l1_lam", [128, 256]), writes=[acc(lamv)])
    p.dma("sp", l1bias.t[:, :], self.inp("l1_bias", [128, 48]), writes=[acc(l1bias)])
    p.dma("sp", lsm.t[:, 7:8], self.inp("l1_subw", [128, 1]), writes=[acc(lsm)])
    p.op("dve", lambda h: h.tensor_tensor(lamv.t[:, 0:64], lamv.t[:, 0:64], lamv.t[:, 64:128], ALU.mult), reads=[acc(lamv)], writes=[acc(lamv)])
    p.op("dve", lambda h: h.tensor_tensor(lamv.t[:, 128:192], lamv.t[:, 128:192], lamv.t[:, 192:256], ALU.mult), reads=[acc(lamv)], writes=[acc(lamv)])
    p.op("dve", lambda h: h.reduce_sum(lsm.t[:, 0:1], lamv.t[:, 0:64], AX.X), reads=[acc(lamv), acc(lsm)], writes=[acc(lsm)])
    p.op("dve", lambda h: h.reduce_sum(lsm.t[:, 1:2], lamv.t[:, 128:192], AX.X), reads=[acc(lamv), acc(lsm)], writes=[acc(lsm)])
    p.op("act", lambda h: h.activation(lsm.t[:, 2:4], lsm.t[:, 0:2], AF.Exp), reads=[acc(lsm)], writes=[acc(lsm)])
    p.op("dve", lambda h: h.tensor_tensor(lsm.t[:, 4:5], lsm.t[:, 3:4], lsm.t[:, 2:3], ALU.subtract), reads=[acc(lsm)], writes=[acc(lsm)])
    p.op("dve", lambda h: h.tensor_scalar(lsm.t[:, 4:5], lsm.t[:, 4:5], -lam_init, None, ALU.add), reads=[acc(lsm)], writes=[acc(lsm)])
    p.op("dve", lambda h: h.tensor_scalar(lsm.t[:, 5:6], lsm.t[:, 7:8], 1.0 - lam_init, None, ALU.mult), reads=[acc(lsm)], writes=[acc(lsm)])
    kc = self.inp("l1_kcT", [128, 8, 512])
    p.dma("pool", self.kcT.t[:, :, :], kc, writes=[acc(self.kcT)])
    vc = self.inp("l1_vc", [512, 1024])
    VB = self.VB
    for t in range(4):
        p.dma("pool", VB.t[:, 8 + t, :], vc[t * 128:(t + 1) * 128, :], writes=[acc3(VB, [8 + t], 0, 1024)])
    wn = "l1_in_w"
    nkT = self.outp("l1_nkT", [128, 8, 1024])
    nv = self.outp("l1_nv", [1024, 1024])
    for b in range(4):
        buf = self.next_block(wn, b * 1024)
        if b == 0:
            for cl in range(8):
                for c in range(2):
                    ps = self.proj_tile(buf, cl, c)
                    self.rope_evac(ps, self.qT, cl, c)
        elif b == 1:
            for cl in range(8):
                for c in range(2):
                    lo, hi = c * 512, (c + 1) * 512
                    ps = self.proj_tile(buf, cl, c)
                    st = self.ST[self.nxt("st", self.NST)]
                    p.op("act", lambda h, ps=ps, st=st: h.copy(st.t[:, :], ps.t[:, :]), reads=[acc(ps)], writes=[acc(st)])
                    p.dma("sp", nkT[:, cl, lo:hi], st.t[:, :], reads=[acc(st)], is_output=True)
                    self.rope_evac(ps, self.kT, cl, c)
        elif b == 2:
            for tt in range(8):
                for g in range(2):
                    ps = self.ps_proj()
                    for kt in range(8):
                        p.op("pe", lambda h, kt=kt, tt=tt, ps=ps, buf=buf, g=g: h.matmul(
                            ps.t[:, :], self.hT.t[:, kt, tt * 128:(tt + 1) * 128], buf.t[:, kt, g * 512:(g + 1) * 512],
                            start=(kt == 0), stop=(kt == 7)),
                            reads=[acc(buf), acc3(self.hT, [kt], tt * 128, (tt + 1) * 128)], writes=[acc(ps)], skip_same=True)
                    st = self.ST[self.nxt("st", self.NST)]
                    p.op("dve", lambda h, ps=ps, st=st: h.tensor_copy(st.t[:, :], ps.t[:, :]), reads=[acc(ps)], writes=[acc(st)])
                    p.dma("sp", nv[tt * 128:(tt + 1) * 128, g * 512:(g + 1) * 512], st.t[:, :], reads=[acc(st)], is_output=True)
                    p.op("act", lambda h, ps=ps, tt=tt, g=g: h.copy(VB.t[:, tt, g * 512:(g + 1) * 512], ps.t[:, :]),
                         reads=[acc(ps)], writes=[acc3(VB, [tt], g * 512, (g + 1) * 512)])
        else:
            for cl in range(8):
                for c in range(2):
                    lo, hi = c * 512, (c + 1) * 512
                    ps = self.proj_tile(buf, cl, c)
                    p.op("act", lambda h, ps=ps, cl=cl, lo=lo, hi=hi: h.activation(self.zT.t[:, cl, lo:hi], ps.t[:, :], AF.Silu),
                         reads=[acc(ps)], writes=[acc3(self.zT, [cl], lo, hi)])
    items = []
    for hh in range(8):
        for qc in range(4):
            for c in range(2):
                for kb in range(12):
                    items.append(dict(hh=hh, qc=qc, c=c, kb=kb))

    def emit_s(it):
        hh, qc, c, kb = it["hh"], it["qc"], it["c"], it["kb"]
        qlo, qhi = qc * 256, (qc + 1) * 256
        pl, ph = c * 64, (c + 1) * 64
        sb = self.ps_s()
        if kb >= 8:
            lhsT = self.kcT.t[pl:ph, hh, (kb - 8) * 128:(kb - 7) * 128]
            rl = acc3(self.kcT, [hh], (kb - 8) * 128, (kb - 7) * 128)
        else:
            lhsT = self.kT.t[pl:ph, hh, kb * 128:(kb + 1) * 128]
            rl = acc3(self.kT, [hh], kb * 128, (kb + 1) * 128)
        rhs = self.qT.t[pl:ph, hh, qlo:qhi]
        p.op("pe", lambda h: h.matmul(sb.t[:, 0:256], lhsT, rhs, start=True, stop=True),
             reads=[rl, acc3(self.qT, [hh], qlo, qhi)], writes=[acc(sb)], skip_same=True)
        pt = self.PT[self.nxt("pt", self.NPT)]
        it["pt"] = pt
        bi = qc * 12 + kb
        p.op("act", lambda h: h.activation(pt.t[:, 0:256], sb.t[:, 0:256], AF.Exp, bias=l1bias.t[:, bi:bi + 1], scale=0.125),
             reads=[acc(sb), acc(l1bias)], writes=[acc(pt)])

    ocs = []

    def emit_pv(it):
        hh, qc, c, kb, pt = it["hh"], it["qc"], it["c"], it["kb"], it["pt"]
        qlo, qhi = qc * 256, (qc + 1) * 256
        po, pd = (self.PS[6], self.PS[7]) if c == 0 else (self.PS[0], self.PS[1])
        p.op("pe", lambda h: h.matmul(po.t[:, 0:256], VB.t[:, kb, hh * 128:(hh + 1) * 128], pt.t[:, 0:256], start=(kb == 0), stop=(kb == 11)),
             reads=[acc3(VB, [kb], hh * 128, (hh + 1) * 128), acc(pt)], writes=[acc(po)], skip_same=True)
        p.op("pe", lambda h: h.matmul(pd.t[:, 0:256], self.ones.t[:, :], pt.t[:, 0:256], start=(kb == 0), stop=(kb == 11)),
             reads=[acc(self.ones), acc(pt)], writes=[acc(pd)], skip_same=True)
        if kb < 11:
            return
        rd = self.TMP[self.nxt("tmp", self.NTMP)]
        p.op("dve", lambda h: h.reciprocal(rd.t[:, 0:256], pd.t[:, 0:256]), reads=[acc(pd)], writes=[acc(rd, 0, 256)])
        oc = self.TMP[self.nxt("tmp", self.NTMP)]
        p.op("dve", lambda h: h.tensor_tensor(oc.t[:, 0:256], po.t[:, 0:256], rd.t[:, 0:256], ALU.mult),
             reads=[acc(po), acc(rd, 0, 256)], writes=[acc(oc, 0, 256)])
        ocs.append(oc)
        if c == 0:
            return
        o0, o1 = ocs[0], ocs[1]
        del ocs[:]
        o = self.TMP[self.nxt("tmp", self.NTMP)]
        p.op("dve", lambda h: h.scalar_tensor_tensor(o.t[:, 0:256], o1.t[:, 0:256], lsm.t[:, 4:5], o0.t[:, 0:256], ALU.mult, ALU.add),
             reads=[acc(o0, 0, 256), acc(o1, 0, 256), acc(lsm)], writes=[acc(o, 0, 256)])
        sq = self.PT[self.nxt("pt", self.NPT)]
        p.op("act", lambda h: h.activation(sq.t[:, 0:256], o.t[:, 0:256], AF.Square), reads=[acc(o, 0, 256)], writes=[acc(sq)])
        pr = self.PS[2]
        p.op("pe", lambda h: h.matmul(pr.t[:, 0:256], self.ones.t[:, :], sq.t[:, 0:256], start=True, stop=True),
             reads=[acc(self.ones), acc(sq)], writes=[acc(pr)], skip_same=True)
        rs = self.TMP[self.nxt("tmp", self.NTMP)]
        p.op("act", lambda h: h.activation(rs.t[:, 0:256], pr.t[:, 0:256], AF.Sqrt, bias=self.eps1.t[:, 0:1], scale=1.0 / 128.0),
             reads=[acc(pr), acc(self.eps1)], writes=[acc(rs, 0, 256)])
        p.op("dve", lambda h: h.reciprocal(rs.t[:, 0:256], rs.t[:, 0:256]), reads=[acc(rs, 0, 256)], writes=[acc(rs, 0, 256)])
        p.op("dve", lambda h: h.scalar_tensor_tensor(o.t[:, 0:256], o.t[:, 0:256], lsm.t[:, 5:6], rs.t[:, 0:256], ALU.mult, ALU.mult),
             reads=[acc(o, 0, 256), acc(rs, 0, 256), acc(lsm)], writes=[acc(o, 0, 256)])
        p.op("pool", lambda h: h.tensor_tensor(self.oT.t[:, hh, qlo:qhi], o.t[:, 0:256], self.zT.t[:, hh, qlo:qhi], ALU.mult),
             reads=[acc(o, 0, 256), acc3(self.zT, [hh], qlo, qhi)], writes=[acc3(self.oT, [hh], qlo, qhi)])

    LA = 2
    for i in range(len(items) + LA):
        if i < len(items):
            emit_s(items[i])
        if i >= LA:
            emit_pv(items[i - LA])
    self.out_proj(li)


Builder.layer_b = layer_b


class Slot:
    def __init__(self, t, a, n=128):
        self.t = t
        self.a = a
        self.n = n
        if len(t.shape) == 3:
            self.ap = t.t[:, :, :].rearrange("q a b -> q (a b)")[:, a:a + n]
        else:
            self.ap = t.t[:, a:a + n]
        self.acc = acc(t, a, a + n)

    def cols(self, lo, hi):
        return Slot(self.t, self.a + lo, hi - lo)


def layer_c(self, li):
    p = self.p
    VB = self.VB
    if not hasattr(self, "S32"):
        self.S32 = p.tensor("S32", [128, 4, 128], F32, gran=128)
        self.S16 = p.tensor("S16", [128, 4, 128], BF16, gran=128)
        self.id32 = p.tensor("id32", [128, 128], F32)
        self.id16 = p.tensor("id16", [128, 128], BF16)
        self.ones32 = p.tensor("ones32", [128, 128], F32)
        self.cw = p.tensor("cw", [128, 160], F32)
        self.l2s = p.tensor("l2s", [128, 64], F32)
    S32, S16, id32, id16, ones32, cw, l2s = self.S32, self.S16, self.id32, self.id16, self.ones32, self.cw, self.l2s
    GS = self.ropeC
    MK = self.ropeS
    KEEP, BND, ONW, ONE = 0, 1, 2, 3
    G_BETA, G_GC, G_NGC, G_EGC, G_KDEC, G_BG, G_EGT = range(7)

    def gs(idx, t, c0, c1):
        a = idx * 128 + t * 16
        return Slot(GS, a + c0, c1 - c0)

    def mk(idx):
        return Slot(MK, idx * 128)

    self.modulate(li)
    p.dma("sp", MK.t[:, :], self.inp("l2_consts", [128, 1024]), writes=[acc(MK)])
    p.dma("sp", id32.t[:, :], self.inp("l2_id32", [128, 128]), writes=[acc(id32)])
    p.op("dve", lambda h: h.tensor_copy(id16.t[:, :], id32.t[:, :]), reads=[acc(id32)], writes=[acc(id16)])
    p.op("dve", lambda h: h.memset(ones32.t[:, :], 1.0), writes=[acc(ones32)])
    p.dma("sp", cw.t[:, 0:72], self.inp("l2_cw", [128, 72]), writes=[acc(cw)])
    p.dma("sp", l2s.t[:, 0:36], self.inp("l2_scal", [128, 36]), writes=[acc(l2s)])
    p.op("dve", lambda h: h.tensor_scalar(cw.t[:, 72:144], cw.t[:, 0:72], l2s.t[:, BND:BND + 1], -1.0, ALU.mult, ALU.mult),
         reads=[acc(cw), acc(l2s)], writes=[acc(cw)])
    p.op("act", lambda h: h.activation(l2s.t[:, 36:52], l2s.t[:, 4:20], AF.Exp), reads=[acc(l2s)], writes=[acc(l2s)])
    p.op("dve", lambda h: h.tensor_scalar(l2s.t[:, 36:52], l2s.t[:, 36:52], -1.0, None, ALU.mult), reads=[acc(l2s)], writes=[acc(l2s)])

    rr = {"s16": 0, "s32": 0, "ps": 0}

    def s16():
        i = rr["s16"]
        rr["s16"] = (i + 1) % (self.NPT * 4)
        return Slot(self.PT[i // 4], (i % 4) * 128)

    pool32 = self.TMP + self.ST

    def s32():
        i = rr["s32"]
        rr["s32"] = (i + 1) % (len(pool32) * 4)
        return Slot(pool32[i // 4], (i % 4) * 128)

    def pst(n=128):
        i = rr["ps"]
        rr["ps"] = (i + 1) % 8
        return Slot(self.PS[i], 0, n)

    def mm(out, lhsT, rhs, start=True, stop=True, extra_r=()):
        p.op("pe", lambda h: h.matmul(out[0], lhsT[0], rhs[0], start=start, stop=stop),
             reads=[lhsT[1], rhs[1]] + list(extra_r), writes=[out[1]], skip_same=True)

    def sl(s):
        return (s.ap, s.acc)

    wn = "l2_in_w"
    QSC = 128.0 ** -0.5

    def ktok(tt):
        if tt < 4:
            return VB, (8 + tt) * 1024
        return self.kcT, (tt - 4) * 1024

    kflat = self.kcT.t[:, :, :].rearrange("q a b -> q (a b)")
    vflat = VB.t[:, :, :].rearrange("q a b -> q (a b)")

    def ktok_slot(tt, hh):
        if tt < 4:
            a = (8 + tt) * 1024 + hh * 128
            s_ = Slot.__new__(Slot)
            s_.t, s_.a, s_.n = VB, a, 128
            s_.ap = vflat[:, a:a + 128]
            s_.acc = acc(VB, a, a + 128)
            return s_
        a = (tt - 4) * 1024 + hh * 128
        s_ = Slot.__new__(Slot)
        s_.t, s_.a, s_.n = self.kcT, a, 128
        s_.ap = kflat[:, a:a + 128]
        s_.acc = acc(self.kcT, a, a + 128)
        return s_

    def vtok_slot(tt, hh):
        a = tt * 1024 + hh * 128
        s_ = Slot.__new__(Slot)
        s_.t, s_.a, s_.n = VB, a, 128
        s_.ap = vflat[:, a:a + 128]
        s_.acc = acc(VB, a, a + 128)
        return s_

    for b in range(3):
        buf = self.next_block(wn, b * 1024)
        for cl in range(8):
            ft = b * 8 + cl
            xp = [None, None]
            xs = []
            for c in range(2):
                ps = self.proj_tile(buf, cl, c)
                xt = self.TMP[self.nxt("tmp", self.NTMP)]
                p.op("act", lambda h, ps=ps, xt=xt: h.copy(xt.t[:, :], ps.t[:, :]), reads=[acc(ps)], writes=[acc(xt)])
                xs.append(xt)
            w0 = cw.t[:, ft * 3 + 0:ft * 3 + 1]
            w1 = cw.t[:, ft * 3 + 1:ft * 3 + 2]
            w2 = cw.t[:, ft * 3 + 2:ft * 3 + 3]
            nb0 = cw.t[:, 72 + ft * 3 + 0:72 + ft * 3 + 1]
            nb2 = cw.t[:, 72 + ft * 3 + 2:72 + ft * 3 + 3]
            ys = []
            for c in range(2):
                y = self.ST[self.nxt("st", self.NST)]
                x = xs[c]
                xo = xs[1 - c]
                p.op("dve", lambda h, y=y, x=x, w1=w1: h.tensor_scalar(y.t[:, :], x.t[:, :], w1, None, ALU.mult),
                     reads=[acc(x), acc(cw)], writes=[acc(y)])
                p.op("dve", lambda h, y=y, x=x, w0=w0: h.scalar_tensor_tensor(y.t[:, 1:512], x.t[:, 0:511], w0, y.t[:, 1:512], ALU.mult, ALU.add),
                     reads=[acc(x), acc(cw), acc(y)], writes=[acc(y)])
                p.op("dve", lambda h, y=y, x=x, w2=w2: h.scalar_tensor_tensor(y.t[:, 0:511], x.t[:, 1:512], w2, y.t[:, 0:511], ALU.mult, ALU.add),
                     reads=[acc(x), acc(cw), acc(y)], writes=[acc(y)])
                if c == 1:
                    p.op("dve", lambda h, y=y, xo=xo, w0=w0: h.scalar_tensor_tensor(y.t[:, 0:1], xo.t[:, 511:512], w0, y.t[:, 0:1], ALU.mult, ALU.add),
                         reads=[acc(xo), acc(cw), acc(y)], writes=[acc(y)])
                    p.op("dve", lambda h, y=y, xo=xo, nb0=nb0: h.scalar_tensor_tensor(y.t[:, 0:1], xo.t[:, 511:512], nb0, y.t[:, 0:1], ALU.mult, ALU.add),
                         reads=[acc(xo), acc(cw), acc(y)], writes=[acc(y)])
                else:
                    p.op("dve", lambda h, y=y, xo=xo, w2=w2: h.scalar_tensor_tensor(y.t[:, 511:512], xo.t[:, 0:1], w2, y.t[:, 511:512], ALU.mult, ALU.add),
                         reads=[acc(xo), acc(cw), acc(y)], writes=[acc(y)])
                    p.op("dve", lambda h, y=y, xo=xo, nb2=nb2: h.scalar_tensor_tensor(y.t[:, 511:512], xo.t[:, 0:1], nb2, y.t[:, 511:512], ALU.mult, ALU.add),
                         reads=[acc(xo), acc(cw), acc(y)], writes=[acc(y)])
                p.op("dve", lambda h, y=y, x=x, nb0=nb0: h.scalar_tensor_tensor(y.t[:, 256:257], x.t[:, 255:256], nb0, y.t[:, 256:257], ALU.mult, ALU.add),
                     reads=[acc(x), acc(cw), acc(y)], writes=[acc(y)])
                p.op("dve", lambda h, y=y, x=x, nb2=nb2: h.scalar_tensor_tensor(y.t[:, 255:256], x.t[:, 256:257], nb2, y.t[:, 255:256], ALU.mult, ALU.add),
                     reads=[acc(x), acc(cw), acc(y)], writes=[acc(y)])
                ys.append(y)
            for c in range(2):
                y = ys[c]
                lo, hi = c * 512, (c + 1) * 512
                if b == 2:
                    hh = cl
                    v16 = self.PT[self.nxt("pt", self.NPT)]
                    p.op("act", lambda h, y=y, v16=v16: h.activation(v16.t[:, :], y.t[:, :], AF.Silu), reads=[acc(y)], writes=[acc(v16)])
                    for q4 in range(4):
                        tt = c * 4 + q4
                        pt_ = pst()
                        mm(sl(pt_), (v16.t[:, q4 * 128:(q4 + 1) * 128], acc(v16, q4 * 128, (q4 + 1) * 128)), (id16.t[:, :], acc(id16)))
                        vs = vtok_slot(tt, hh)
                        p.op("act", lambda h, vs=vs, pt_=pt_: h.copy(vs.ap, pt_.ap), reads=[pt_.acc], writes=[vs.acc])
                else:
                    hh = cl
                    p.op("act", lambda h, y=y: h.activation(y.t[:, :], y.t[:, :], AF.Silu), reads=[acc(y)], writes=[acc(y)])
                    sq = self.PT[self.nxt("pt", self.NPT)]
                    p.op("act", lambda h, y=y, sq=sq: h.activation(sq.t[:, :], y.t[:, :], AF.Square), reads=[acc(y)], writes=[acc(sq)])
                    pss = self.ps_proj()
                    p.op("pe", lambda h, pss=pss, sq=sq: h.matmul(pss.t[:, :], self.ones.t[:, :], sq.t[:, :], start=True, stop=True),
                         reads=[acc(self.ones), acc(sq)], writes=[acc(pss)], skip_same=True)
                    rs = self.TMP[self.nxt("tmp", self.NTMP)]
                    p.op("act", lambda h, rs=rs, pss=pss: h.activation(rs.t[:, :], pss.t[:, :], AF.Sqrt, bias=self.eps1.t[:, 0:1], scale=1.0),
                         reads=[acc(pss), acc(self.eps1)], writes=[acc(rs)])
                    p.op("dve", lambda h, rs=rs: h.reciprocal(rs.t[:, :], rs.t[:, :]), reads=[acc(rs)], writes=[acc(rs)])
                    dst = self.qT if b == 0 else self.kT
                    sc_ = QSC if b == 0 else 1.0
                    p.op("dve", lambda h, y=y, rs=rs, dst=dst, hh=hh, lo=lo, hi=hi, sc_=sc_: h.scalar_tensor_tensor(
                        dst.t[:, hh, lo:hi], y.t[:, :], sc_, rs.t[:, :], ALU.mult, ALU.mult),
                        reads=[acc(y), acc(rs)], writes=[acc3(dst, [hh], lo, hi)])
                    if b == 1:
                        for q4 in range(4):
                            tt = c * 4 + q4
                            pt_ = pst()
                            mm(sl(pt_), (self.kT.t[:, hh, tt * 128:(tt + 1) * 128], acc3(self.kT, [hh], tt * 128, (tt + 1) * 128)), (id16.t[:, :], acc(id16)))
                            ks = ktok_slot(tt, hh)
                            p.op("act", lambda h, ks=ks, pt_=pt_: h.copy(ks.ap, pt_.ap), reads=[pt_.acc], writes=[ks.acc])
    buf = self.next_block(wn, 3072)
    for cl in range(8):
        for c in range(2):
            lo, hi = c * 512, (c + 1) * 512
            ps = self.proj_tile(buf, cl, c)
            p.op("act", lambda h, ps=ps, cl=cl, lo=lo, hi=hi: h.activation(self.zT.t[:, cl, lo:hi], ps.t[:, :], AF.Silu),
                 reads=[acc(ps)], writes=[acc3(self.zT, [cl], lo, hi)])
    buf = self.next_block(wn, 4096)
    alog_nA = l2s.t[:, 36:52]
    dtb = l2s.t[:, 20:36]
    one1 = l2s.t[:, ONE:ONE + 1]
    for t in range(8):
        gp = pst(32)
        for kt in range(8):
            p.op("pe", lambda h, kt=kt, t=t, gp=gp, buf=buf: h.matmul(gp.ap, self.hT.t[:, kt, t * 128:(t + 1) * 128], buf.t[:, kt, 0:32],
                                                                  start=(kt == 0), stop=(kt == 7)),
                 reads=[acc(buf), acc3(self.hT, [kt], t * 128, (t + 1) * 128)], writes=[gp.acc], skip_same=True)
        beta = gs(G_BETA, t, 0, 16)
        p.op("act", lambda h, beta=beta, gp=gp: h.activation(beta.ap, gp.ap[:, 0:16], AF.Sigmoid), reads=[gp.acc], writes=[beta.acc])
        sc = s32()
        p.op("dve", lambda h, sc=sc, gp=gp: h.tensor_tensor(sc.ap[:, 0:16], gp.ap[:, 16:32], dtb, ALU.add), reads=[gp.acc, acc(l2s)], writes=[sc.acc])
        p.op("act", lambda h, sc=sc: h.activation(sc.ap[:, 16:32], sc.ap[:, 0:16], AF.Exp), reads=[sc.acc], writes=[sc.acc])
        p.op("act", lambda h, sc=sc: h.activation(sc.ap[:, 32:48], sc.ap[:, 16:32], AF.Ln, bias=one1, scale=1.0), reads=[sc.acc, acc(l2s)], writes=[sc.acc])
        p.op("dve", lambda h, sc=sc: h.tensor_tensor(sc.ap[:, 48:64], sc.ap[:, 32:48], alog_nA, ALU.mult), reads=[sc.acc, acc(l2s)], writes=[sc.acc])
        g32 = (sc.ap[:, 48:64], sc.acc)
        pg = pst(32)
        UT, LT = mk(6), mk(7)
        p.op("pe", lambda h, pg=pg, sc=sc, UT=UT: h.matmul(pg.ap[:, 0:8], UT.ap, sc.ap[:, 48:56], start=True, stop=True),
             reads=[UT.acc, sc.acc], writes=[pg.acc], skip_same=True)
        p.op("pe", lambda h, pg=pg, sc=sc, LT=LT: h.matmul(pg.ap[:, 8:16], LT.ap, sc.ap[:, 56:64], start=True, stop=True),
             reads=[LT.acc, sc.acc], writes=[pg.acc], skip_same=True)
        p.op("pe", lambda h, pg=pg, sc=sc: h.matmul(pg.ap[:, 16:32], ones32.t[:, :], sc.ap[:, 48:64], start=True, stop=True),
             reads=[acc(ones32), sc.acc], writes=[pg.acc], skip_same=True)
        gc, ngc, egc, kdec, bg, egt = (gs(i, t, 0, 16) for i in (G_GC, G_NGC, G_EGC, G_KDEC, G_BG, G_EGT))
        p.op("dve", lambda h, gc=gc, pg=pg: h.tensor_copy(gc.ap, pg.ap[:, 0:16]), reads=[pg.acc], writes=[gc.acc])
        p.op("dve", lambda h, gc=gc, ngc=ngc: h.tensor_scalar(ngc.ap, gc.ap, -1.0, None, ALU.mult), reads=[gc.acc], writes=[ngc.acc])
        p.op("act", lambda h, gc=gc, egc=egc: h.activation(egc.ap, gc.ap, AF.Exp), reads=[gc.acc], writes=[egc.acc])
        p.op("act", lambda h, egt=egt, pg=pg: h.activation(egt.ap, pg.ap[:, 16:32], AF.Exp), reads=[pg.acc], writes=[egt.acc])
        p.op("dve", lambda h, kdec=kdec, pg=pg, ngc=ngc: h.tensor_tensor(kdec.ap, pg.ap[:, 16:32], ngc.ap, ALU.add), reads=[pg.acc, ngc.acc], writes=[kdec.acc])
        p.op("act", lambda h, kdec=kdec: h.activation(kdec.ap, kdec.ap, AF.Exp), reads=[kdec.acc], writes=[kdec.acc])
        p.op("dve", lambda h, bg=bg, beta=beta, egc=egc: h.tensor_tensor(bg.ap, beta.ap, egc.ap, ALU.mult), reads=[beta.acc, egc.acc], writes=[bg.acc])

    if self.debug:
        dq = self.outp("dbg_qT", [128, 8, 1024], BF16)
        dk = self.outp("dbg_kT", [128, 8, 1024], BF16)
        dv = self.outp("dbg_VB", [128, 12, 1024], BF16)
        dkc = self.outp("dbg_kcT", [128, 8, 512], BF16)
        dg = self.outp("dbg_GS", [128, 1024], F32)
        dz = self.outp("dbg_zT", [128, 8, 1024], BF16)
        p.dma("sp", dq, self.qT.t[:, :, :], reads=[acc(self.qT)], is_output=True)
        p.dma("sp", dk, self.kT.t[:, :, :], reads=[acc(self.kT)], is_output=True)
        p.dma("sp", dv, VB.t[:, :, :], reads=[acc(VB)], is_output=True)
        p.dma("sp", dkc, self.kcT.t[:, :, :], reads=[acc(self.kcT)], is_output=True)
        p.dma("sp", dg, GS.t[:, :], reads=[acc(GS)], is_output=True)
        p.dma("sp", dz, self.zT.t[:, :, :], reads=[acc(self.zT)], is_output=True)
    s0 = self.inp("l2_s0", [2, 8, 128, 128])
    nst = self.outp("l2_nst", [2, 4, 8, 128, 128])
    keep = l2s.t[:, KEEP:KEEP + 1]
    onw = l2s.t[:, ONW:ONW + 1]

    def col(idx, t, c):
        s_ = gs(idx, t, c, c + 1)
        return s_

    if not hasattr(self, "XF"):
        self.XF = p.tensor("XF", [128, 512], F32, gran=128)
        self.XB = [p.tensor("XB%d" % i, [128, 512], BF16, gran=128) for i in range(4)]
    f_t = self.TMP + self.ST + [self.XF]
    b_t = self.PT + self.XB
    FS = [[Slot(f_t[2 * ci + j // 4], (j % 4) * 128) for j in range(8)] for ci in range(4)]
    BS = [[Slot(b_t[2 * ci + j // 4], (j % 4) * 128) for j in range(8)] for ci in range(4)]
    psrr = [0, 0, 0, 0]

    def cps(ci):
        i = psrr[ci]
        psrr[ci] = 1 - i
        return Slot(self.PS[2 * ci + i], 0, 128)

    def inst_gen(t, d, hh, ci, first):
        c = d * 8 + hh
        tl, th = t * 128, (t + 1) * 128
        F, B = FS[ci], BS[ci]
        KT_ = (self.kT.t[:, hh, tl:th], acc3(self.kT, [hh], tl, th))
        QT_ = (self.qT.t[:, hh, tl:th], acc3(self.qT, [hh], tl, th))
        ktk = ktok_slot(t, hh)
        vtk = vtok_slot(t, hh)
        gcC, ngcC, egcC, kdecC, bgC, egtC, betaC = (col(i, t, c) for i in (G_GC, G_NGC, G_EGC, G_KDEC, G_BG, G_EGT, G_BETA))
        MTi, MTs, MAs = mk(0 + d), mk(2 + d), mk(4 + d)

        def diag(colslot, dg):
            p.op("dve", lambda h: h.tensor_scalar(dg.ap, id32.t[:, :], colslot.ap, None, ALU.mult),
                 reads=[acc(id32), colslot.acc], writes=[dg.acc])

        def decay(mask, scale, biascol, o_):
            ps_ = cps(ci)
            mm(sl(ps_), (ones32.t[:, :], acc(ones32)), sl(F[0]), start=True, stop=False)
            mm(sl(ps_), (id32.t[:, :], acc(id32)), sl(mask), start=False, stop=True)
            p.op("act", lambda h: h.activation(o_.ap, ps_.ap, AF.Exp, bias=biascol.ap, scale=scale),
                 reads=[ps_.acc, biascol.acc], writes=[o_.acc])

        def rowscaled(colslot, src, dg, o_):
            diag(colslot, dg)
            ps_ = cps(ci)
            mm(sl(ps_), (ones32.t[:, :], acc(ones32)), sl(dg))
            p.op("dve", lambda h: h.tensor_tensor(o_.ap, src[0], ps_.ap, ALU.mult), reads=[src[1], ps_.acc], writes=[o_.acc])

        def masked(lhsT, rhs, dm, o_):
            ps_ = cps(ci)
            mm(sl(ps_), lhsT, rhs)
            p.op("dve", lambda h: h.tensor_tensor(o_.ap, ps_.ap, dm.ap, ALU.mult), reads=[ps_.acc, dm.acc], writes=[o_.acc])

        diag(gcC, F[0])
        yield
        decay(MAs, -1.0, gcC, B[1])
        yield
        psG = cps(ci)
        mm(sl(psG), KT_, KT_)
        p.op("dve", lambda h: h.scalar_tensor_tensor(F[2].ap, psG.ap, betaC.ap, B[1].ap, ALU.mult, ALU.mult),
             reads=[psG.acc, betaC.acc, B[1].acc], writes=[F[2].acc])
        yield
        psPt = cps(ci)
        mm(sl(psPt), sl(F[2]), (id32.t[:, :], acc(id32)))
        p.op("act", lambda h: h.copy(F[3].ap, psPt.ap), reads=[psPt.acc], writes=[F[3].acc])
        yield
        Tt32 = F[6]
        p.op("dve", lambda h: h.tensor_tensor(Tt32.ap, id32.t[:, :], F[3].ap, ALU.subtract), reads=[acc(id32), F[3].acc], writes=[Tt32.acc])
        yield
        cur, nxt_ = (F[2], F[3]), (F[4], F[5])
        for m in (1, 2, 4, 8, 16, 32):
            Am, Pm = cur
            A2, P2 = nxt_
            psA = cps(ci)
            mm(sl(psA), sl(Pm), sl(Am))
            p.op("act", lambda h, A2=A2, psA=psA: h.copy(A2.ap, psA.ap), reads=[psA.acc], writes=[A2.acc])
            yield
            if m < 32:
                psP = cps(ci)
                mm(sl(psP), sl(Am), sl(Pm))
                p.op("dve", lambda h, P2=P2, psP=psP: h.tensor_copy(P2.ap, psP.ap), reads=[psP.acc], writes=[P2.acc])
                yield
            psT = cps(ci)
            mm(sl(psT), sl(A2), sl(Tt32))
            p.op("dve", lambda h, psT=psT: h.tensor_tensor(Tt32.ap, Tt32.ap, psT.ap, ALU.add), reads=[Tt32.acc, psT.acc], writes=[Tt32.acc])
            yield
            cur, nxt_ = nxt_, cur
        Tt16 = B[1]
        p.op("act", lambda h: h.copy(Tt16.ap, Tt32.ap), reads=[Tt32.acc], writes=[Tt16.acc])
        yield
        decay(MTi, 1.0, ngcC, B[0])
        yield
        p.op("dve", lambda h: h.tensor_scalar(B[3].ap, id16.t[:, :], egcC.ap, None, ALU.mult), reads=[acc(id16), egcC.acc], writes=[B[3].acc])
        psR = cps(ci)
        mm(sl(psR), (self.ones.t[:, :], acc(self.ones)), sl(B[3]))
        p.op("dve", lambda h: h.tensor_tensor(B[2].ap, QT_[0], psR.ap, ALU.mult), reads=[QT_[1], psR.acc], writes=[B[2].acc])
        yield
        masked(KT_, QT_, B[0], B[3])
        yield
        Vb, Kbg, Kdec, WT = B[4], B[5], B[6], B[7]
        p.op("pool", lambda h: h.tensor_scalar(Vb.ap, vtk.ap, betaC.ap, None, ALU.mult), reads=[vtk.acc, betaC.acc], writes=[Vb.acc])
        p.op("pool", lambda h: h.tensor_scalar(Kbg.ap, ktk.ap, bgC.ap, None, ALU.mult), reads=[ktk.acc, bgC.acc], writes=[Kbg.acc])
        p.op("pool", lambda h: h.tensor_scalar(Kdec.ap, ktk.ap, kdecC.ap, None, ALU.mult), reads=[ktk.acc, kdecC.acc], writes=[Kdec.acc])
        yield
        psU = cps(ci)
        mm(sl(psU), sl(Tt16), sl(Vb))
        U = F[7]
        p.op("act", lambda h: h.copy(U.ap, psU.ap), reads=[psU.acc], writes=[U.acc])
        yield
        psW = cps(ci)
        mm(sl(psW), sl(Kbg), sl(Tt16))
        p.op("dve", lambda h: h.tensor_copy(WT.ap, psW.ap), reads=[psW.acc], writes=[WT.acc])
        yield
        S16s = Slot(S16, ci * 128)
        S32s = Slot(S32, ci * 128)
        S16ap = (S16.t[:, ci, :], S16s.acc)
        S32ap = S32.t[:, ci, :]
        psWS = cps(ci)
        mm(sl(psWS), sl(WT), S16ap)
        Vn = B[4]
        p.op("dve", lambda h: h.tensor_tensor(Vn.ap, U.ap, psWS.ap, ALU.subtract), reads=[U.acc, psWS.acc], writes=[Vn.acc])
        yield
        psS = cps(ci)
        mm(sl(psS), sl(Kdec), sl(Vn))
        p.op("dve", lambda h: h.scalar_tensor_tensor(S32ap, S32ap, egtC.ap, psS.ap, ALU.mult, ALU.add),
             reads=[S32s.acc, egtC.acc, psS.acc], writes=[S32s.acc])
        yield
        psO = cps(ci)
        mm(sl(psO), S16ap, sl(B[2]), start=True, stop=False)
        mm(sl(psO), sl(Vn), sl(B[3]), start=False, stop=True)
        oslot_ap = self.oT.t[:, hh, tl:th]
        oacc = acc3(self.oT, [hh], tl, th)
        if first:
            p.op("act", lambda h: h.copy(oslot_ap, psO.ap), reads=[psO.acc], writes=[oacc])
            yield
        else:
            ot, rs, sq = F[2], F[3], B[5]
            p.op("dve", lambda h: h.tensor_tensor(ot.ap, psO.ap, oslot_ap, ALU.add), reads=[psO.acc, oacc], writes=[ot.acc])
            p.op("act", lambda h: h.activation(sq.ap, ot.ap, AF.Square), reads=[ot.acc], writes=[sq.acc])
            yield
            pss = cps(ci)
            mm(sl(pss), (self.ones.t[:, :], acc(self.ones)), sl(sq))
            p.op("act", lambda h: h.activation(rs.ap, pss.ap, AF.Sqrt, bias=self.eps1.t[:, 0:1], scale=1.0 / 128.0),
                 reads=[pss.acc, acc(self.eps1)], writes=[rs.acc])
            yield
            p.op("dve", lambda h: h.reciprocal(rs.ap, rs.ap), reads=[rs.acc], writes=[rs.acc])
            p.op("dve", lambda h: h.scalar_tensor_tensor(ot.ap, ot.ap, onw, rs.ap, ALU.mult, ALU.mult),
                 reads=[ot.acc, rs.acc, acc(l2s)], writes=[ot.acc])
            zacc = acc3(self.zT, [hh], tl, th)
            p.op("pool", lambda h: h.tensor_tensor(oslot_ap, ot.ap, self.zT.t[:, hh, tl:th], ALU.mult),
                 reads=[ot.acc, zacc], writes=[oacc])
            yield

    def chain_gen(ci, d, hh):
        S32s = Slot(S32, ci * 128)
        S16s = Slot(S16, ci * 128)
        p.dma("sp", S32.t[:, ci, :], s0[d, hh, :, :], writes=[S32s.acc])
        p.op("dve", lambda h: h.tensor_scalar(S32.t[:, ci, :], S32.t[:, ci, :], keep, None, ALU.mult),
             reads=[S32s.acc, acc(l2s)], writes=[S32s.acc])
        p.op("act", lambda h: h.copy(S16.t[:, ci, :], S32.t[:, ci, :]), reads=[S32s.acc], writes=[S16s.acc])
        yield
        for n in range(8):
            t = n if d == 0 else 7 - n
            yield from inst_gen(t, d, hh, ci, n < 4)
            if n % 2 == 1:
                p.dma("sp", nst[d, t // 2, hh, :, :], S32.t[:, ci, :], reads=[S32s.acc], is_output=True)
                if n < 7:
                    p.op("dve", lambda h: h.tensor_scalar(S32.t[:, ci, :], S32.t[:, ci, :], keep, None, ALU.mult),
                         reads=[S32s.acc, acc(l2s)], writes=[S32s.acc])
            if n < 7:
                p.op("act", lambda h: h.copy(S16.t[:, ci, :], S32.t[:, ci, :]), reads=[S32s.acc], writes=[S16s.acc])
            yield

    for hp in range(4):
        chains = [(d, 2 * hp + e) for e in range(2) for d in range(2)]
        gens = [chain_gen(ci, d, hh) for ci, (d, hh) in enumerate(chains)]
        alive = list(gens)
        while alive:
            for g_ in list(alive):
                try:
                    next(g_)
                except StopIteration:
                    alive.remove(g_)
    if self.debug:
        do = self.outp("dbg_oT", [128, 8, 1024], BF16)
        p.dma("sp", do, self.oT.t[:, :, :], reads=[acc(self.oT)], is_output=True)
    p.dma("sp", self.ropeC.t[:, :], self.inp("ropeC", [128, 1024]), writes=[acc(self.ropeC)])
    p.dma("sp", self.ropeS.t[:, :], self.inp("ropeS", [128, 1024]), writes=[acc(self.ropeS)])
    self.out_proj(li)


Builder.layer_c = layer_c
```

```python
import numpy as np
from contextlib import ExitStack
import concourse.bass as bass
import concourse.mybir as mybir
from concourse.bass_utils import run_bass_kernel_spmd

F32 = mybir.dt.float32
BF16 = mybir.dt.bfloat16
AF = mybir.ActivationFunctionType
ALU = mybir.AluOpType
AX = mybir.AxisListType

ENGS = ("pe", "act", "dve", "pool", "sp")
NEG = -30000.0


class T:
    def __init__(self, prog, name, shape, dtype, gran, psum=False):
        self.name = name
        self.shape = shape
        self.F = int(np.prod(shape[1:]))
        self.gran = gran
        self.nreg = (self.F + gran - 1) // gran
        self.w = [None] * self.nreg
        self.r = [[] for _ in range(self.nreg)]
        self.psum = psum
        if psum:
            self.t = prog.es.enter_context(prog.nc.psum_tensor("pp_" + name, list(shape), dtype))
        else:
            self.t = prog.es.enter_context(prog.nc.sbuf_tensor("sb_" + name, list(shape), dtype))

    def regs(self, a, b):
        return range(a // self.gran, (b - 1) // self.gran + 1)


class Acc:
    def __init__(self, t, ranges):
        self.t = t
        self.ranges = ranges


def acc(t, a=None, b=None):
    if a is None:
        return Acc(t, [(0, t.F)])
    return Acc(t, [(a, b)])


def acc3(t, kts, lo, hi):
    inner = t.shape[-1] if len(t.shape) == 3 else None
    return Acc(t, [(k * inner + lo, k * inner + hi) for k in kts])


class Prog:
    def __init__(self, nc, n_dma_sems=8):
        self.nc = nc
        self.es = ExitStack()
        self.ops = {e: [] for e in ENGS}
        self.sem = {}
        for e in ENGS:
            self.sem[e] = self.es.enter_context(nc.semaphore("s_" + e))
        self.cnt = {e: 0 for e in ENGS}
        self.waited = {e: {} for e in ENGS}
        self.dsem = [self.es.enter_context(nc.semaphore("d%d" % i)) for i in range(n_dma_sems)]
        self.dval = [0] * n_dma_sems
        self.dnext = 0
        self.final_events = []

    def tensor(self, name, shape, dtype, gran=None, psum=False):
        F = int(np.prod(shape[1:]))
        return T(self, name, shape, dtype, gran or F, psum)

    def _deps(self, reads, writes):
        deps = set()
        for a in reads:
            for (lo, hi) in a.ranges:
                for g in a.t.regs(lo, hi):
                    if a.t.w[g] is not None:
                        deps.add(a.t.w[g])
        for a in writes:
            for (lo, hi) in a.ranges:
                for g in a.t.regs(lo, hi):
                    if a.t.w[g] is not None:
                        deps.add(a.t.w[g])
                    for ev in a.t.r[g]:
                        deps.add(ev)
        return deps

    def _mark(self, reads, writes, ev):
        for a in reads:
            for (lo, hi) in a.ranges:
                for g in a.t.regs(lo, hi):
                    a.t.r[g].append(ev)
        for a in writes:
            for (lo, hi) in a.ranges:
                for g in a.t.regs(lo, hi):
                    a.t.w[g] = ev
                    a.t.r[g] = []

    def _waits(self, eng, deps, skip_same=False):
        best = {}
        for (k, v) in deps:
            if skip_same and k == eng:
                continue
            if v > best.get(k, 0):
                best[k] = v
        out = []
        for k, v in best.items():
            if self.waited[eng].get(k, 0) >= v:
                continue
            self.waited[eng][k] = v
            out.append((k, v))
        return out

    def _semh(self, k):
        if isinstance(k, str):
            return self.sem[k]
        if isinstance(k, tuple):
            return self._swh[k]
        return self.dsem[k]

    def op(self, eng, fn, reads=(), writes=(), skip_same=False):
        pr = [a for a in reads if a.t.psum]
        if pr:
            reads = [a for a in reads if not a.t.psum]
            writes = list(writes) + pr
        deps = self._deps(reads, writes)
        waits = self._waits(eng, deps, skip_same)
        self.cnt[eng] += 1
        ev = (eng, self.cnt[eng])
        self._mark(reads, writes, ev)
        semh = self.sem[eng]
        wl = [(self._semh(k), v) for (k, v) in waits]

        def emit(h):
            for (s, v) in wl:
                h.wait_ge(s, v)
            fn(h).then_inc(semh, 1)

        self.ops[eng].append(emit)
        return ev

    def dma(self, eng, out_ap, in_ap, reads=(), writes=(), is_output=False, slot=None):
        if eng == "pool":
            return self.dma_sw(out_ap, in_ap, reads, writes, slot)
        deps = self._deps(reads, writes)
        i = self.dnext
        self.dnext = (self.dnext + 1) % len(self.dsem)
        if self.dval[i] > 0:
            deps.add((i, self.dval[i]))
        waits = self._waits(eng, deps)
        self.dval[i] += 16
        ev = (i, self.dval[i])
        self._mark(reads, writes, ev)
        semh = self.dsem[i]
        wl = [(self._semh(k), v) for (k, v) in waits]

        def emit(h):
            for (s, v) in wl:
                h.wait_ge(s, v)
            h.dma_start(out=out_ap, in_=in_ap).then_inc(semh, 16)

        self.ops[eng].append(emit)
        if is_output:
            self.final_events.append(ev)
        return ev

    def dma_sw(self, out_ap, in_ap, reads, writes, slot):
        eng = "pool"
        if not hasattr(self, "_swh"):
            self._swh = {}
        n = len(self._swh)
        semh = self.es.enter_context(self.nc.semaphore("w%d" % n))
        key = ("sw", n)
        self._swh[key] = semh
        deps = self._deps(reads, writes)
        waits = self._waits(eng, deps)
        ev = (key, 16)
        self._mark(reads, writes, ev)
        wl = [(self._semh(k), v) for (k, v) in waits]

        def emit(h):
            for (s, v) in wl:
                h.wait_ge(s, v)
            h.dma_start(out=out_ap, in_=in_ap).then_inc(semh, 16)

        self.ops[eng].append(emit)
        return ev

    def finish(self):
        waits = self._waits("sp", set(self.final_events))
        wl = [(self._semh(k), v) for (k, v) in waits]

        def emit(h):
            for (s, v) in wl:
                h.wait_ge(s, v)

        self.ops["sp"].append(emit)
        nc = self.nc
        ops = self.ops
        with nc.Block() as block:
            @block.tensor
            def _(h):
                for f in ops["pe"]:
                    f(h)

            @block.scalar
            def _(h):
                for f in ops["act"]:
                    f(h)

            @block.vector
            def _(h):
                for f in ops["dve"]:
                    f(h)

            @block.gpsimd
            def _(h):
                for f in ops["pool"]:
                    f(h)

            @block.sync
            def _(h):
                for f in ops["sp"]:
                    f(h)
        self.es.close()


D = 1024
NT = 1024
KT = 8
IN_COLS = {0: 2560, 1: 4096, 2: 4128, 3: 2560}
KIND = {0: "a", 1: "b", 2: "c", 3: "a"}


def lambda_init_for(layer):
    import math
    return 0.8 - 0.6 * math.exp(-0.3 * layer)


class Builder:
    def __init__(self, layers=(0, 1, 2, 3), debug=False):
        self.layers = layers
        self.debug = debug
        self.nc = nc = bass.Bass("TRN2", target_bir_lowering=False)
        self.p = p = Prog(nc)
        self.din = {}
        self.dout = {}
        self.rr = {}
        self._alloc()

    def inp(self, name, shape, dtype=F32):
        if name not in self.din:
            self.din[name] = self.nc.dram_tensor(name, list(shape), dtype, kind="ExternalInput").ap()
        return self.din[name]

    def outp(self, name, shape, dtype=F32):
        if name not in self.dout:
            self.dout[name] = self.nc.dram_tensor(name, list(shape), dtype, kind="ExternalOutput").ap()
        return self.dout[name]

    def nxt(self, key, n):
        v = self.rr.get(key, 0)
        self.rr[key] = (v + 1) % n
        return v

    def _alloc(self):
        p = self.p
        self.xT = p.tensor("xT", [128, 8, 1024], F32, gran=128)
        self.hT = p.tensor("hT", [128, 8, 1024], BF16, gran=128)
        self.qT = p.tensor("qT", [128, 8, 1024], BF16, gran=128)
        self.kT = p.tensor("kT", [128, 8, 1024], BF16, gran=128)
        self.kcT = p.tensor("kcT", [128, 8, 512], BF16, gran=128)
        self.VB = p.tensor("VB", [128, 12, 1024], BF16, gran=128)
        self.zT = p.tensor("zT", [128, 8, 1024], BF16, gran=128)
        self.oT = self.hT
        self.NW = 2
        self.W = [p.tensor("W%d" % i, [128, 8, 1024], BF16) for i in range(self.NW)]
        self.rstd = p.tensor("rstd", [128, 1024], F32, gran=512)
        self.ropeC = p.tensor("ropeC", [128, 1024], F32, gran=128)
        self.ropeS = p.tensor("ropeS", [128, 1024], F32, gran=128)
        self.Rm = p.tensor("Rm", [128, 128], BF16)
        self.ones = p.tensor("ones", [128, 128], BF16)
        self.NPT = 4
        self.PT = [p.tensor("PT%d" % i, [128, 512], BF16, gran=128) for i in range(self.NPT)]
        self.NST = 3
        self.ST = [p.tensor("ST%d" % i, [128, 512], F32, gran=128) for i in range(self.NST)]
        self.NTMP = 4
        self.TMP = [p.tensor("TMP%d" % i, [128, 512], F32, gran=128) for i in range(self.NTMP)]
        self.maskP = p.tensor("maskP", [128, 8, 128], BF16)
        self.maskN = p.tensor("maskN", [128, 8, 128], BF16)
        self.ctxb = p.tensor("ctxb", [128, 1], F32)
        self.zero1 = p.tensor("zero1", [128, 1], F32)
        self.eps1 = p.tensor("eps1", [128, 1], F32)
        self.cond = p.tensor("cond", [128, 8], F32)
        self.scond = p.tensor("scond", [128, 8], BF16)
        self.vec = p.tensor("vec", [128, 64], F32)
        self.mT = p.tensor("mT", [128, 24], F32)
        self.gain = p.tensor("gain", [128, 8], F32)
        self.sink = p.tensor("sink", [128, 16], F32)
        self.esink = p.tensor("esink", [128, 16], F32)
        self.finw = p.tensor("finw", [128, 8], F32)
        self.PS = [p.tensor("ps%d" % i, [128, 512], F32, psum=True) for i in range(8)]
        self.wcount = 0

    def ps_proj(self):
        return self.PS[self.nxt("pj", 2)]

    def ps_rope(self):
        return self.PS[2]

    def ps_s(self):
        return self.PS[3 + self.nxt("s", 3)]

    def ps_o(self):
        return self.PS[6 + self.nxt("o", 2)]

    def plan_weights(self):
        plan = []
        for li in self.layers:
            for c0 in range(0, 3072, 1024):
                plan.append(("l%d_mod_w" % li, 3072, c0, 1024))
            nc_ = IN_COLS[li]
            for c0 in range(0, nc_, 1024):
                plan.append(("l%d_in_w" % li, nc_, c0, min(1024, nc_ - c0)))
            plan.append(("l%d_out_w" % li, 1024, 0, 1024))
        self.wplan = plan
        self.wissued = 0
        self.wused = 0

    def _issue_block(self):
        p = self.p
        wname, ncols, c0, ccount = self.wplan[self.wissued]
        w = self.inp(wname, [1024, ncols])
        buf = self.W[self.wissued % self.NW]
        self.wissued += 1
        src = w.rearrange("(kt q) c -> q kt c", q=128)[:, :, c0:c0 + ccount]
        p.dma("pool", buf.t[:, :, 0:ccount], src, writes=[acc(buf)])

    def next_block(self, wname, c0):
        i = self.wused
        assert self.wplan[i][0] == wname and self.wplan[i][2] == c0, (self.wplan[i], wname, c0)
        while self.wissued <= min(i + 1, len(self.wplan) - 1):
            self._issue_block()
        self.wused += 1
        return self.W[i % self.NW]

    def setup(self):
        p = self.p
        xin = self.inp("xT", [128, 8, 1024])
        for kt in range(8):
            p.dma("sp", self.xT.t[:, kt, :], xin[:, kt, :], writes=[acc3(self.xT, [kt], 0, 1024)])
        p.dma("sp", self.cond.t[:, :], self.inp("cond", [128, 8]), writes=[acc(self.cond)])
        p.dma("sp", self.ropeC.t[:, :], self.inp("ropeC", [128, 1024]), writes=[acc(self.ropeC)])
        p.dma("sp", self.ropeS.t[:, :], self.inp("ropeS", [128, 1024]), writes=[acc(self.ropeS)])
        p.dma("sp", self.ctxb.t[:, :], self.inp("ctxb", [128, 1]), writes=[acc(self.ctxb)])
        p.dma("sp", self.finw.t[:, :], self.inp("finw", [128, 8]), writes=[acc(self.finw)])
        p.dma("pool", self.Rm.t[:, :], self.inp("Rm", [128, 128]), writes=[acc(self.Rm)])
        p.dma("pool", self.maskP.t[:, :, :], self.inp("maskP", [128, 8, 128]), writes=[acc(self.maskP)])
        p.dma("pool", self.maskN.t[:, :, :], self.inp("maskN", [128, 8, 128]), writes=[acc(self.maskN)])
        p.op("dve", lambda h: h.memset(self.ones.t[:, :], 1.0), writes=[acc(self.ones)])
        p.op("dve", lambda h: h.memset(self.zero1.t[:, :], 0.0), writes=[acc(self.zero1)])
        p.op("dve", lambda h: h.memset(self.eps1.t[:, :], 1e-6), writes=[acc(self.eps1)])
        p.op("act", lambda h: h.activation(self.scond.t[:, :], self.cond.t[:, :], AF.Silu),
             reads=[acc(self.cond)], writes=[acc(self.scond)])

    def compute_rstd(self):
        p = self.p
        sq = self.hT
        for c in range(2):
            lo, hi = c * 512, (c + 1) * 512
            for kt in range(8):
                p.op("act", lambda h, kt=kt, lo=lo, hi=hi: h.activation(sq.t[:, kt, lo:hi], self.xT.t[:, kt, lo:hi], AF.Square),
                     reads=[acc3(self.xT, [kt], lo, hi)], writes=[acc3(sq, [kt], lo, hi)])
            ps = self.ps_proj()
            for kt in range(8):
                p.op("pe", lambda h, kt=kt, lo=lo, hi=hi, ps=ps: h.matmul(ps.t[:, :], self.ones.t[:, :], sq.t[:, kt, lo:hi],
                                                                     start=(kt == 0), stop=(kt == 7)),
                     reads=[acc(self.ones), acc3(sq, [kt], lo, hi)], writes=[acc(ps)], skip_same=True)
            p.op("act", lambda h, ps=ps, lo=lo, hi=hi: h.activation(self.rstd.t[:, lo:hi], ps.t[:, :], AF.Sqrt,
                                                               bias=self.eps1.t[:, 0:1], scale=1.0 / 1024.0),
                 reads=[acc(ps), acc(self.eps1)], writes=[acc(self.rstd, lo, hi)])
            p.op("dve", lambda h, lo=lo, hi=hi: h.reciprocal(self.rstd.t[:, lo:hi], self.rstd.t[:, lo:hi]),
                 reads=[acc(self.rstd, lo, hi)], writes=[acc(self.rstd, lo, hi)])

    def modulate(self, li):
        p = self.p
        vec = self.vec
        p.dma("sp", vec.t[:, 0:8], self.inp("l%d_norm_w" % li, [128, 8]), writes=[acc(vec)])
        p.dma("sp", vec.t[:, 8:32], self.inp("l%d_mod_b" % li, [128, 24]), writes=[acc(vec)])
        self.compute_rstd()
        ps = self.PS[2]
        for b in range(3):
            buf = self.next_block("l%d_mod_w" % li, b * 1024)
            for cl in range(8):
                c = b * 8 + cl
                for kt in range(8):
                    p.op("pe", lambda h, buf=buf, cl=cl, c=c, kt=kt: h.matmul(
                        ps.t[:, c:c + 1], buf.t[:, kt, cl * 128:(cl + 1) * 128], self.scond.t[:, kt:kt + 1],
                        start=(kt == 0), stop=(kt == 7)),
                        reads=[acc(buf), acc(self.scond)], writes=[acc(ps)], skip_same=True)
        p.op("dve", lambda h: h.tensor_tensor(self.mT.t[:, :], ps.t[:, 0:24], vec.t[:, 8:32], ALU.add),
             reads=[acc(ps), acc(vec)], writes=[acc(self.mT)])
        p.op("dve", lambda h: h.scalar_tensor_tensor(self.gain.t[:, :], self.mT.t[:, 8:16], 1.0, vec.t[:, 0:8],
                                                     ALU.add, ALU.mult),
             reads=[acc(self.mT), acc(vec)], writes=[acc(self.gain)])
        for kt in range(8):
            for c in range(2):
                lo, hi = c * 512, (c + 1) * 512
                tmp = self.TMP[self.nxt("tmp", self.NTMP)]
                p.op("dve", lambda h, kt=kt, lo=lo, hi=hi, tmp=tmp: h.scalar_tensor_tensor(
                    tmp.t[:, :], self.xT.t[:, kt, lo:hi], self.gain.t[:, kt:kt + 1], self.rstd.t[:, lo:hi],
                    ALU.mult, ALU.mult),
                    reads=[acc3(self.xT, [kt], lo, hi), acc(self.gain), acc(self.rstd, lo, hi)], writes=[acc(tmp)])
                p.op("act", lambda h, kt=kt, lo=lo, hi=hi, tmp=tmp: h.activation(
                    self.hT.t[:, kt, lo:hi], tmp.t[:, :], AF.Identity, bias=self.mT.t[:, kt:kt + 1], scale=1.0),
                    reads=[acc(tmp), acc(self.mT)], writes=[acc3(self.hT, [kt], lo, hi)])

    def proj_tile(self, buf, cl, c, src=None):
        p = self.p
        src = src or self.hT
        ps = self.ps_proj()
        lo, hi = c * 512, (c + 1) * 512
        for kt in range(8):
            p.op("pe", lambda h, kt=kt: h.matmul(ps.t[:, :], buf.t[:, kt, cl * 128:(cl + 1) * 128], src.t[:, kt, lo:hi],
                                                 start=(kt == 0), stop=(kt == 7)),
                 reads=[acc(buf), acc3(src, [kt], lo, hi)], writes=[acc(ps)], skip_same=True)
        self.rope_flush()
        return ps

    def rope_evac(self, ps, dst, dt_, c):
        p = self.p
        self.rope_flush()
        lo, hi = c * 512, (c + 1) * 512
        qb = self.PT[self.nxt("pt", self.NPT)]
        p.op("act", lambda h: h.copy(qb.t[:, :], ps.t[:, :]), reads=[acc(ps)], writes=[acc(qb)])

        def stage_b():
            pr = self.ps_rope()
            p.op("pe", lambda h: h.matmul(pr.t[:, :], self.Rm.t[:, :], qb.t[:, :], start=True, stop=True),
                 reads=[acc(self.Rm), acc(qb)], writes=[acc(pr)], skip_same=True)
            t1 = self.TMP[self.nxt("tmp", self.NTMP)]
            t2 = self.TMP[self.nxt("tmp", self.NTMP)]
            p.op("dve", lambda h: h.tensor_tensor(t1.t[:, :], ps.t[:, :], self.ropeC.t[:, lo:hi], ALU.mult),
                 reads=[acc(ps), acc(self.ropeC)], writes=[acc(t1)])
            p.op("dve", lambda h: h.tensor_tensor(t2.t[:, :], pr.t[:, :], self.ropeS.t[:, lo:hi], ALU.mult),
                 reads=[acc(pr), acc(self.ropeS)], writes=[acc(t2)])
            p.op("pool", lambda h: h.tensor_tensor(dst.t[:, dt_, lo:hi], t1.t[:, :], t2.t[:, :], ALU.add),
                 reads=[acc(t1), acc(t2)], writes=[acc3(dst, [dt_], lo, hi)])

        self._rope_pending = stage_b

    def rope_flush(self):
        f = getattr(self, "_rope_pending", None)
        if f is not None:
            self._rope_pending = None
            f()

    def out_proj(self, li):
        p = self.p
        for b in range(1):
            buf = self.next_block("l%d_out_w" % li, 0)
            for cl in range(8):
                mt = cl
                for c in range(2):
                    lo, hi = c * 512, (c + 1) * 512
                    ps = self.proj_tile(buf, cl, c, src=self.oT)
                    p.op("dve", lambda h, ps=ps, mt=mt, lo=lo, hi=hi: h.scalar_tensor_tensor(
                        self.xT.t[:, mt, lo:hi], ps.t[:, :], self.mT.t[:, 16 + mt:17 + mt], self.xT.t[:, mt, lo:hi],
                        ALU.mult, ALU.add),
                        reads=[acc(ps), acc(self.mT), acc3(self.xT, [mt], lo, hi)], writes=[acc3(self.xT, [mt], lo, hi)])

    def final(self):
        p = self.p
        self.compute_rstd()
        yT = self.outp("yT", [128, 8, 1024])
        for kt in range(8):
            for c in range(2):
                lo, hi = c * 512, (c + 1) * 512
                st = self.ST[self.nxt("st", self.NST)]
                p.op("dve", lambda h, kt=kt, lo=lo, hi=hi, st=st: h.scalar_tensor_tensor(
                    st.t[:, :], self.xT.t[:, kt, lo:hi], self.finw.t[:, kt:kt + 1], self.rstd.t[:, lo:hi],
                    ALU.mult, ALU.mult),
                    reads=[acc3(self.xT, [kt], lo, hi), acc(self.finw), acc(self.rstd, lo, hi)], writes=[acc(st)])
                p.dma("sp", yT[:, kt, lo:hi], st.t[:, :], reads=[acc(st)], is_output=True)

    def dump_x(self, name):
        out = self.outp(name, [128, 8, 1024])
        for kt in range(8):
            self.p.dma("sp", out[:, kt, :], self.xT.t[:, kt, :], reads=[acc3(self.xT, [kt], 0, 1024)], is_output=True)


def layer_a(self, li):
    p = self.p
    self.modulate(li)
    p.dma("sp", self.sink.t[:, :], self.inp("l%d_sink" % li, [128, 16]), writes=[acc(self.sink)])
    p.op("act", lambda h: h.activation(self.esink.t[:, :], self.sink.t[:, :], AF.Exp),
         reads=[acc(self.sink)], writes=[acc(self.esink)])
    kc = self.inp("l%d_kcT" % li, [128, 2, 512])
    p.dma("pool", self.kcT.t[:, 0:2, :], kc, writes=[acc3(self.kcT, [0, 1], 0, 512)], slot="kcT")
    vc = self.inp("l%d_vc" % li, [512, 256])
    VB = self.VB
    for kt in range(12):
        v4 = VB.t[:, kt, 0:512].rearrange("q (g e) -> q g e", g=4)
        p.op("pool", lambda h, v4=v4: h.memset(v4[:, :, 64:128], 1.0), writes=[acc3(VB, [kt], 0, 512)])
    for t in range(4):
        v4 = VB.t[:, 8 + t, 0:512].rearrange("q (g e) -> q g e", g=4)
        p.dma("pool", v4[:, :, 0:64], vc[t * 128:(t + 1) * 128, :].rearrange("q (g e) -> q g e", g=4),
              writes=[acc3(VB, [8 + t], 0, 512)], slot="VBc%d" % t)
    wn = "l%d_in_w" % li
    nkT = self.outp("l%d_nkT" % li, [128, 2, 1024])
    nv = self.outp("l%d_nv" % li, [1024, 256])
    for b in range(3):
        buf = self.next_block(wn, b * 1024)
        if b == 0:
            for cl in range(8):
                for c in range(2):
                    ps = self.proj_tile(buf, cl, c)
                    self.rope_evac(ps, self.qT, cl, c)
        elif b == 1:
            for cl in range(2):
                for c in range(2):
                    lo, hi = c * 512, (c + 1) * 512
                    ps = self.proj_tile(buf, cl, c)
                    st = self.ST[self.nxt("st", self.NST)]
                    p.op("act", lambda h, ps=ps, st=st: h.copy(st.t[:, :], ps.t[:, :]), reads=[acc(ps)], writes=[acc(st)])
                    p.dma("sp", nkT[:, cl, lo:hi], st.t[:, :], reads=[acc(st)], is_output=True)
                    self.rope_evac(ps, self.kT, cl, c)
            self.rope_flush()
            for tt in range(8):
                ps = self.ps_proj()
                for kt in range(8):
                    p.op("pe", lambda h, kt=kt, tt=tt, ps=ps, buf=buf: h.matmul(
                        ps.t[:, 0:256], self.hT.t[:, kt, tt * 128:(tt + 1) * 128], buf.t[:, kt, 256:512],
                        start=(kt == 0), stop=(kt == 7)),
                        reads=[acc(buf), acc3(self.hT, [kt], tt * 128, (tt + 1) * 128)], writes=[acc(ps)], skip_same=True)
                st = self.ST[self.nxt("st", self.NST)]
                p.op("dve", lambda h, ps=ps, st=st: h.tensor_copy(st.t[:, 0:256], ps.t[:, 0:256]), reads=[acc(ps)], writes=[acc(st)])
                p.dma("sp", nv[tt * 128:(tt + 1) * 128, :], st.t[:, 0:256], reads=[acc(st)], is_output=True)
                v4 = VB.t[:, tt, 0:512].rearrange("q (g e) -> q g e", g=4)
                p.op("act", lambda h, ps=ps, v4=v4: h.copy(v4[:, :, 0:64], ps.t[:, 0:256].rearrange("q (g e) -> q g e", g=4)),
                     reads=[acc(ps)], writes=[acc3(VB, [tt], 0, 512)])
            zlist = [(4 + i, i) for i in range(4)]
        if b >= 1:
            if b == 2:
                zlist = [(i, 4 + i) for i in range(4)]
            for (cl, zt) in zlist:
                for c in range(2):
                    lo, hi = c * 512, (c + 1) * 512
                    ps = self.proj_tile(buf, cl, c)
                    p.op("act", lambda h, ps=ps, zt=zt, lo=lo, hi=hi: h.activation(self.zT.t[:, zt, lo:hi], ps.t[:, :], AF.Silu),
                         reads=[acc(ps)], writes=[acc3(self.zT, [zt], lo, hi)])
    items = []
    for Tk in range(2):
        for u in range(2):
            for j in range(8):
                blocks = []
                if j > 0:
                    blocks.append(("P", j - 1))
                blocks.append(("L", j))
                if j < 7:
                    blocks.append(("N", j + 1))
                for t in range(4):
                    blocks.append(("C", t))
                for bi, (kind, kb) in enumerate(blocks):
                    items.append(dict(Tk=Tk, u=u, j=j, kind=kind, kb=kb, bi=bi, nb=len(blocks)))

    def emit_s(it):
        Tk, u, j, kind, kb = it["Tk"], it["u"], it["j"], it["kind"], it["kb"]
        pl, ph = u * 64, (u + 1) * 64
        qlo, qhi = j * 128, (j + 1) * 128
        sb = self.ps_s()
        if kind == "C":
            lhsT = self.kcT.t[pl:ph, Tk, kb * 128:(kb + 1) * 128]
            rl = acc3(self.kcT, [Tk], kb * 128, (kb + 1) * 128)
            it["vt"] = 8 + kb
            bias = self.ctxb
        else:
            lhsT = self.kT.t[pl:ph, Tk, kb * 128:(kb + 1) * 128]
            rl = acc3(self.kT, [Tk], kb * 128, (kb + 1) * 128)
            it["vt"] = kb
            bias = self.zero1
        rhs = self.qT.t[pl:ph, 4 * Tk:4 * Tk + 4, qlo:qhi]
        p.op("pe", lambda h: h.matmul(sb.t[:, :].rearrange("q (g e) -> q g e", g=4), lhsT, rhs, start=True, stop=True),
             reads=[rl, acc3(self.qT, range(4 * Tk, 4 * Tk + 4), qlo, qhi)], writes=[acc(sb)], skip_same=True)
        pt = self.PT[self.nxt("pt", self.NPT)]
        it["pt"] = pt
        p.op("act", lambda h: h.activation(pt.t[:, :], sb.t[:, :], AF.Exp, bias=bias.t[:, 0:1], scale=0.125),
             reads=[acc(sb), acc(bias)], writes=[acc(pt)])
        if kind in ("P", "N"):
            mk = self.maskP if kind == "P" else self.maskN
            p.op("pool", lambda h: h.tensor_tensor(
                pt.t[:, :].rearrange("q (g e) -> q g e", g=4), pt.t[:, :].rearrange("q (g e) -> q g e", g=4),
                mk.t[:, j, :].unsqueeze(1).broadcast_to([128, 4, 128]), ALU.mult),
                reads=[acc(pt), acc(mk)], writes=[acc(pt)])

    cur_po = [None]

    def emit_pv(it):
        Tk, u, j, bi, nb = it["Tk"], it["u"], it["j"], it["bi"], it["nb"]
        hkv = 2 * Tk + u
        pl, ph = u * 64, (u + 1) * 64
        qlo, qhi = j * 128, (j + 1) * 128
        if bi == 0:
            cur_po[0] = self.ps_o()
        po = cur_po[0]
        vt, pt = it["vt"], it["pt"]
        p.op("pe", lambda h: h.matmul(po.t[:, :], VB.t[:, vt, hkv * 128:(hkv + 1) * 128], pt.t[:, :], start=(bi == 0), stop=(bi == nb - 1)),
             reads=[acc3(VB, [vt], hkv * 128, (hkv + 1) * 128), acc(pt)], writes=[acc(po)], skip_same=True)
        if bi == nb - 1:
            tmp = self.TMP[self.nxt("tmp", self.NTMP)]
            p.op("dve", lambda h: h.tensor_tensor(
                tmp.t[64:128, :].rearrange("q (g e) -> q g e", g=4), po.t[64:128, :].rearrange("q (g e) -> q g e", g=4),
                self.esink.t[64:128, hkv * 4:hkv * 4 + 4].unsqueeze(2).broadcast_to([64, 4, 128]), ALU.add),
                reads=[acc(po), acc(self.esink)], writes=[acc(tmp)])
            p.op("dve", lambda h: h.reciprocal(tmp.t[64:128, :], tmp.t[64:128, :]), reads=[acc(tmp)], writes=[acc(tmp)])
            p.op("dve", lambda h: h.tensor_tensor(
                self.oT.t[pl:ph, 4 * Tk:4 * Tk + 4, qlo:qhi], po.t[0:64, :].rearrange("q (g e) -> q g e", g=4),
                tmp.t[64:128, :].rearrange("q (g e) -> q g e", g=4), ALU.mult),
                reads=[acc(po), acc(tmp)], writes=[acc3(self.oT, range(4 * Tk, 4 * Tk + 4), qlo, qhi)])

    LA = 2
    for i in range(len(items) + LA):
        if i < len(items):
            emit_s(items[i])
        if i >= LA:
            emit_pv(items[i - LA])
    for t in range(8):
        for c in range(2):
            lo, hi = c * 512, (c + 1) * 512
            p.op("pool", lambda h, t=t, lo=lo, hi=hi: h.tensor_tensor(self.oT.t[:, t, lo:hi], self.oT.t[:, t, lo:hi], self.zT.t[:, t, lo:hi], ALU.mult),
                 reads=[acc3(self.oT, [t], lo, hi), acc3(self.zT, [t], lo, hi)], writes=[acc3(self.oT, [t], lo, hi)])
    self.out_proj(li)


Builder.layer_a = layer_a


def build(layers=(0, 1, 2, 3), debug=False):
    b = Builder(layers, debug)
    b.plan_weights()
    b.setup()
    for li in layers:
        k = KIND[li]
        if k == "a":
            b.layer_a(li)
        elif k == "b":
            b.layer_b(li)
        else:
            b.layer_c(li)
        if debug:
            b.dump_x("dbg_x%d" % li)
    b.final()
    print("sbuf bytes remaining", b.nc.sbuf_bytes_remaining)
    b.p.finish()
    return b


def fm(v, n):
    return np.ascontiguousarray(np.asarray(v, np.float32).reshape(n, 128).T)


def to_fm3(a):
    a = np.asarray(a, np.float32)
    n = a.shape[1] // 128
    return np.ascontiguousarray(a.T.reshape(n, 128, a.shape[0]).transpose(1, 0, 2))


def from_fm3(a):
    return np.ascontiguousarray(a.transpose(2, 1, 0).reshape(a.shape[2], -1))


def perm_a():
    perm = np.zeros(1024, np.int64)
    for tq in range(8):
        Tk, g = tq // 4, tq % 4
        for u in range(2):
            hq = 4 * (2 * Tk + u) + g
            perm[tq * 128 + u * 64: tq * 128 + u * 64 + 64] = hq * 64 + np.arange(64)
    return perm


def rope_tables(sample):
    C = np.ones((128, 1024), np.float32)
    S = np.zeros((128, 1024), np.float32)
    if sample:
        tok = np.arange(1024)
        row = (tok // 64).astype(np.float32)
        col = (tok % 64).astype(np.float32)
        inv = (np.float32(10000.0) ** (-np.arange(16, dtype=np.float32) / np.float32(16))).astype(np.float32)
        for pp in range(128):
            d = pp % 64
            pos = row if d < 32 else col
            ang = (pos * inv[d % 16]).astype(np.float32)
            C[pp] = np.cos(ang)
            sgn = -1.0 if (d % 32) < 16 else 1.0
            S[pp] = sgn * np.sin(ang)
    return C, S


def rope_perm_matrix():
    Rm = np.zeros((128, 128), np.float32)
    for m in range(128):
        d = m % 64
        base = m - d
        partner = d + 16 if (d % 32) < 16 else d - 16
        Rm[base + partner, m] = 1.0
    return Rm


def win_masks(sample):
    mp = np.zeros((128, 8, 128), np.float32)
    mn = np.zeros((128, 8, 128), np.float32)
    k = np.arange(128)[:, None]
    q = np.arange(128)[None, :]
    for j in range(8):
        if sample:
            mp[:, j, :] = (k >= q)
            mn[:, j, :] = (k <= q)
        else:
            mp[:, j, :] = 1.0 if (j % 2 == 1) else 0.0
            mn[:, j, :] = 1.0 if (j % 2 == 0) else 0.0
    return mp, mn


_CACHE = {}
LAYERS = (0, 1, 2, 3)
DEBUG = False


def kernel(**inp):
    inp = {k: np.asarray(v) for k, v in inp.items()}
    layers = LAYERS
    key = (tuple(layers), DEBUG)
    if key not in _CACHE:
        _CACHE[key] = build(layers, DEBUG)
    b = _CACHE[key]
    pa = perm_a()
    shared = {}
    for li in layers:
        shared["l%d_mod_w" % li] = np.ascontiguousarray(inp["l%d_mod_w" % li], np.float32)
        w = inp["l%d_in_w" % li]
        ow = inp["l%d_out_w" % li]
        if KIND[li] == "a":
            w = np.concatenate([w[:, pa], w[:, 1024:1536], w[:, 1536 + pa]], axis=1)
            ow = ow[pa, :]
        shared["l%d_in_w" % li] = np.ascontiguousarray(w, np.float32)
        shared["l%d_out_w" % li] = np.ascontiguousarray(ow, np.float32)
        shared["l%d_norm_w" % li] = fm(inp["l%d_norm_w" % li], 8)
        shared["l%d_mod_b" % li] = fm(inp["l%d_mod_b" % li], 24)
        if KIND[li] == "a":
            shared["l%d_sink" % li] = np.ascontiguousarray(np.broadcast_to(inp["l%d_sink" % li].astype(np.float32)[None, :], (128, 16)))
    if 1 in layers:
        lamcat = np.concatenate([inp["l1_lambda_q1"], inp["l1_lambda_k1"], inp["l1_lambda_q2"], inp["l1_lambda_k2"]]).astype(np.float32)
        shared["l1_lam"] = np.ascontiguousarray(np.broadcast_to(lamcat[None, :], (128, 256)))
        shared["l1_subw"] = fm(inp["l1_subln_w"], 1)
    if 2 in layers:
        pp_ = np.arange(128)[:, None]
        ff_ = np.arange(128)[None, :]
        LOW, UPP, LOWI, UPPI = (pp_ > ff_), (pp_ < ff_), (pp_ >= ff_), (pp_ <= ff_)
        consts = [NEG * (1 - UPPI), NEG * (1 - LOWI), NEG * (1 - UPP), NEG * (1 - LOW),
                  -NEG * (1 - LOW), -NEG * (1 - UPP), 1.0 * UPPI, 1.0 * LOWI]
        shared["l2_consts"] = np.ascontiguousarray(np.concatenate([c_.astype(np.float32) for c_ in consts], axis=1))
        shared["l2_id32"] = np.eye(128, dtype=np.float32)
        cwv = inp["l2_conv_w"].astype(np.float32)
        shared["l2_cw"] = np.ascontiguousarray(cwv.reshape(3, 24, 128).transpose(2, 1, 0).reshape(128, 72))
    shared["finw"] = fm(inp["final_norm_w"], 8)
    shared["Rm"] = rope_perm_matrix()
    in_maps = []
    for core in range(8):
        sample = core < 4
        m = dict(shared)
        if sample:
            bb = core
            x = inp["x_sample"][bb]
            cond = inp["c"][bb]
        else:
            bb = core % 4
            s0 = 4 * (core - 4)
            x = inp["x_prompt"][s0:s0 + 4].reshape(1024, 1024)
            cond = inp["c_ctx"]
        m["xT"] = to_fm3(x)
        m["cond"] = fm(cond, 8)
        C, S = rope_tables(sample)
        m["ropeC"], m["ropeS"] = C, S
        mp, mn = win_masks(sample)
        m["maskP"], m["maskN"] = mp, mn
        m["ctxb"] = np.full((128, 1), 0.0 if sample else NEG, np.float32)
        for li in layers:
            if KIND[li] == "a":
                ck = inp["cache_l%d_k" % li][bb]
                cv = inp["cache_l%d_v" % li][bb]
                kcT = ck.transpose(1, 2, 0).reshape(2, 128, 512).transpose(1, 0, 2)
                m["l%d_kcT" % li] = np.ascontiguousarray(kcT, np.float32)
                m["l%d_vc" % li] = np.ascontiguousarray(cv.reshape(512, 256), np.float32)
        if 2 in layers:
            sc_ = np.zeros((128, 36), np.float32)
            sc_[:, 0] = 1.0 if sample else 0.0
            sc_[:, 1] = 0.0 if sample else 1.0
            sc_[:, 2] = inp["l2_onorm_w"].astype(np.float32)
            sc_[:, 3] = 1.0
            sc_[:, 4:20] = inp["l2_a_log"].astype(np.float32).reshape(1, 16)
            sc_[:, 20:36] = inp["l2_dt_bias"].astype(np.float32).reshape(1, 16)
            m["l2_scal"] = sc_
            m["l2_s0"] = np.ascontiguousarray(inp["state_l2"][bb], np.float32)
        if 1 in layers:
            m["l1_kcT"] = to_fm3(inp["cache_l1_k"][bb].reshape(512, 1024))
            m["l1_vc"] = np.ascontiguousarray(inp["cache_l1_v"][bb].reshape(512, 1024), np.float32)
            bt = np.zeros((128, 48), np.float32)
            if not sample:
                for qc in range(4):
                    for kb in range(12):
                        if not (kb < 8 and kb // 2 == qc):
                            bt[:, qc * 12 + kb] = NEG
            m["l1_bias"] = bt
        in_maps.append({k: m[k] for k in b.din})
    res = run_bass_kernel_spmd(b.nc, in_maps, core_ids=list(range(8)))
    R = res.results
    kernel.last = R
    y_sample = np.stack([from_fm3(R[c]["yT"]) for c in range(4)])
    y_prompt = np.concatenate([from_fm3(R[c]["yT"]).reshape(4, 256, 1024) for c in range(4, 8)])
    outs = {}
    for li in (0, 3):
        if li in layers:
            outs["nk%d" % li] = np.concatenate([from_fm3(R[c]["l%d_nkT" % li]).reshape(4, 256, 4, 64) for c in range(4, 8)])
            outs["nv%d" % li] = np.concatenate([R[c]["l%d_nv" % li].reshape(4, 256, 4, 64) for c in range(4, 8)])
        else:
            outs["nk%d" % li] = np.zeros((16, 256, 4, 64), np.float32)
            outs["nv%d" % li] = np.zeros((16, 256, 4, 64), np.float32)
    if 1 in layers:
        nk1 = np.concatenate([from_fm3(R[c]["l1_nkT"]).reshape(4, 256, 8, 2, 64) for c in range(4, 8)])
        nv1 = np.concatenate([R[c]["l1_nv"].reshape(4, 256, 8, 128) for c in range(4, 8)])
    else:
        nk1 = np.zeros((16, 256, 8, 2, 64), np.float32)
        nv1 = np.zeros((16, 256, 8, 128), np.float32)
    if 2 in layers:
        st = np.concatenate([R[c]["l2_nst"].transpose(1, 0, 2, 3, 4) for c in range(4, 8)])
    else:
        st = np.zeros((16, 2, 8, 128, 128), np.float32)
    return (y_prompt.astype(np.float32), y_sample.astype(np.float32), outs["nk0"], outs["nv0"], nk1, nv1, st,
            outs["nk3"], outs["nv3"])


def layer_b(self, li):
    p = self.p
    lam_init = lambda_init_for(li)
    if not hasattr(self, "lamv"):
        self.lamv = p.tensor("lamv", [128, 512], BF16)
        self.lsm = p.tensor("lsm", [128, 8], F32)
        self.l1bias = p.tensor("l1bias", [128, 48], F32)
    lamv, lsm, l1bias = self.lamv, self.lsm, self.l1bias
    lam32 = lamv.t[:, :].bitcast(F32)
    self.modulate(li)
    p.dma("sp", lam32, self.inp("l1_lam", [128, 256]), writes=[acc(lamv)])
    p.dma("sp", l1bias.t[:, :], self.inp("l1_bias", [128, 48]), writes=[acc(l1bias)])
    p.dma("sp", lsm.t[:, 7:8], self.inp("l1_subw", [128, 1]), writes=[acc(lsm)])
    p.op("dve", lambda h: h.tensor_tensor(lam32[:, 0:64], lam32[:, 0:64], lam32[:, 64:128], ALU.mult), reads=[acc(lamv)], writes=[acc(lamv)])
    p.op("dve", lambda h: h.tensor_tensor(lam32[:, 128:192], lam32[:, 128:192], lam32[:, 192:256], ALU.mult), reads=[acc(lamv)], writes=[acc(lamv)])
    p.op("dve", lambda h: h.reduce_sum(lsm.t[:, 0:1], lam32[:, 0:64], AX.X), reads=[acc(lamv), acc(lsm)], writes=[acc(lsm)])
    p.op("dve", lambda h: h.reduce_sum(lsm.t[:, 1:2], lam32[:, 128:192], AX.X), reads=[acc(lamv), acc(lsm)], writes=[acc(lsm)])
    p.op("act", lambda h: h.activation(lsm.t[:, 2:4], lsm.t[:, 0:2], AF.Exp), reads=[acc(lsm)], writes=[acc(lsm)])
    p.op("dve", lambda h: h.tensor_tensor(lsm.t[:, 4:5], lsm.t[:, 3:4], lsm.t[:, 2:3], ALU.subtract), reads=[acc(lsm)], writes=[acc(lsm)])
    p.op("dve", lambda h: h.tensor_scalar(lsm.t[:, 4:5], lsm.t[:, 4:5], -lam_init, None, ALU.add), reads=[acc(lsm)], writes=[acc(lsm)])
    p.op("dve", lambda h: h.tensor_scalar(lsm.t[:, 5:6], lsm.t[:, 7:8], 1.0 - lam_init, None, ALU.mult), reads=[acc(lsm)], writes=[acc(lsm)])
    kc = self.inp("l1_kcT", [128, 8, 512])
    p.dma("pool", self.kcT.t[:, :, :], kc, writes=[acc(self.kcT)])
    vc = self.inp("l1_vc", [512, 1024])
    VB = self.VB
    for t in range(4):
        p.dma("pool", VB.t[:, 8 + t, :], vc[t * 128:(t + 1) * 128, :], writes=[acc3(VB, [8 + t], 0, 1024)])
    wn = "l1_in_w"
    nkT = self.outp("l1_nkT", [128, 8, 1024])
    nv = self.outp("l1_nv", [1024, 1024])
    for b in range(4):
        buf = self.next_block(wn, b * 1024)
        if b == 0:
            for cl in range(8):
                for c in range(2):
                    ps = self.proj_tile(buf, cl, c)
                    self.rope_evac(ps, self.qT, cl, c)
        elif b == 1:
            for cl in range(8):
                for c in range(2):
                    lo, hi = c * 512, (c + 1) * 512
                    ps = self.proj_tile(buf, cl, c)
                    st = self.ST[self.nxt("st", self.NST)]
                    p.op("act", lambda h, ps=ps, st=st: h.copy(st.t[:, :], ps.t[:, :]), reads=[acc(ps)], writes=[acc(st)])
                    p.dma("sp", nkT[:, cl, lo:hi], st.t[:, :], reads=[acc(st)], is_output=True)
                    self.rope_evac(ps, self.kT, cl, c)
        elif b == 2:
            self.rope_flush()
            for tt in range(8):
                for g in range(2):
                    ps = self.ps_proj()
                    for kt in range(8):
                        p.op("pe", lambda h, kt=kt, tt=tt, ps=ps, buf=buf, g=g: h.matmul(
                            ps.t[:, :], self.hT.t[:, kt, tt * 128:(tt + 1) * 128], buf.t[:, kt, g * 512:(g + 1) * 512],
                            start=(kt == 0), stop=(kt == 7)),
                            reads=[acc(buf), acc3(self.hT, [kt], tt * 128, (tt + 1) * 128)], writes=[acc(ps)], skip_same=True)
                    st = self.ST[self.nxt("st", self.NST)]
                    p.op("dve", lambda h, ps=ps, st=st: h.tensor_copy(st.t[:, :], ps.t[:, :]), reads=[acc(ps)], writes=[acc(st)])
                    p.dma("sp", nv[tt * 128:(tt + 1) * 128, g * 512:(g + 1) * 512], st.t[:, :], reads=[acc(st)], is_output=True)
                    p.op("act", lambda h, ps=ps, tt=tt, g=g: h.copy(VB.t[:, tt, g * 512:(g + 1) * 512], ps.t[:, :]),
                         reads=[acc(ps)], writes=[acc3(VB, [tt], g * 512, (g + 1) * 512)])
        else:
            for cl in range(8):
                for c in range(2):
                    lo, hi = c * 512, (c + 1) * 512
                    ps = self.proj_tile(buf, cl, c)
                    p.op("act", lambda h, ps=ps, cl=cl, lo=lo, hi=hi: h.activation(self.zT.t[:, cl, lo:hi], ps.t[:, :], AF.Silu),
                         reads=[acc(ps)], writes=[acc3(self.zT, [cl], lo, hi)])
    items = []
    for hh in range(8):
        for qc in range(2):
            for c in range(2):
                for kb in range(12):
                    items.append(dict(hh=hh, qc=qc, c=c, kb=kb))

    def emit_s(it):
        hh, qc, c, kb = it["hh"], it["qc"], it["c"], it["kb"]
        qlo, qhi = qc * 512, (qc + 1) * 512
        pl, ph = c * 64, (c + 1) * 64
        sb = self.ps_s()
        if kb >= 8:
            lhsT = self.kcT.t[pl:ph, hh, (kb - 8) * 128:(kb - 7) * 128]
            rl = acc3(self.kcT, [hh], (kb - 8) * 128, (kb - 7) * 128)
        else:
            lhsT = self.kT.t[pl:ph, hh, kb * 128:(kb + 1) * 128]
            rl = acc3(self.kT, [hh], kb * 128, (kb + 1) * 128)
        rhs = self.qT.t[pl:ph, hh, qlo:qhi]
        p.op("pe", lambda h: h.matmul(sb.t[:, :], lhsT, rhs, start=True, stop=True),
             reads=[rl, acc3(self.qT, [hh], qlo, qhi)], writes=[acc(sb)], skip_same=True)
        pt = self.PT[self.nxt("pt", self.NPT)]
        it["pt"] = pt
        for hf in range(2):
            bi = (2 * qc + hf) * 12 + kb
            p.op("act", lambda h, hf=hf, bi=bi: h.activation(pt.t[:, hf * 256:(hf + 1) * 256], sb.t[:, hf * 256:(hf + 1) * 256], AF.Exp,
                                                         bias=l1bias.t[:, bi:bi + 1], scale=0.125),
                 reads=[acc(sb), acc(l1bias)], writes=[acc(pt, hf * 256, (hf + 1) * 256)])

    ocs = []

    def emit_pv(it):
        hh, qc, c, kb, pt = it["hh"], it["qc"], it["c"], it["kb"], it["pt"]
        qlo, qhi = qc * 512, (qc + 1) * 512
        po, pd = (self.PS[6], self.PS[7]) if c == 0 else (self.PS[0], self.PS[1])
        p.op("pe", lambda h: h.matmul(po.t[:, :], VB.t[:, kb, hh * 128:(hh + 1) * 128], pt.t[:, :], start=(kb == 0), stop=(kb == 11)),
             reads=[acc3(VB, [kb], hh * 128, (hh + 1) * 128), acc(pt)], writes=[acc(po)], skip_same=True)
        p.op("pe", lambda h: h.matmul(pd.t[:, :], self.ones.t[:, :], pt.t[:, :], start=(kb == 0), stop=(kb == 11)),
             reads=[acc(self.ones), acc(pt)], writes=[acc(pd)], skip_same=True)
        if kb < 11:
            return
        rd = self.TMP[self.nxt("tmp", self.NTMP)]
        p.op("dve", lambda h: h.reciprocal(rd.t[:, :], pd.t[:, :]), reads=[acc(pd)], writes=[acc(rd)])
        oc = self.TMP[self.nxt("tmp", self.NTMP)]
        p.op("dve", lambda h: h.tensor_tensor(oc.t[:, :], po.t[:, :], rd.t[:, :], ALU.mult),
             reads=[acc(po), acc(rd)], writes=[acc(oc)])
        ocs.append(oc)
        if c == 0:
            return
        o0, o1 = ocs[0], ocs[1]
        del ocs[:]
        o = self.TMP[self.nxt("tmp", self.NTMP)]
        p.op("dve", lambda h: h.scalar_tensor_tensor(o.t[:, :], o1.t[:, :], lsm.t[:, 4:5], o0.t[:, :], ALU.mult, ALU.add),
             reads=[acc(o0), acc(o1), acc(lsm)], writes=[acc(o)])
        sq = self.PT[self.nxt("pt", self.NPT)]
        p.op("act", lambda h: h.activation(sq.t[:, :], o.t[:, :], AF.Square), reads=[acc(o)], writes=[acc(sq)])
        pr = self.PS[2]
        p.op("pe", lambda h: h.matmul(pr.t[:, :], self.ones.t[:, :], sq.t[:, :], start=True, stop=True),
             reads=[acc(self.ones), acc(sq)], writes=[acc(pr)], skip_same=True)
        rs = self.TMP[self.nxt("tmp", self.NTMP)]
        p.op("act", lambda h: h.activation(rs.t[:, :], pr.t[:, :], AF.Sqrt, bias=self.eps1.t[:, 0:1], scale=1.0 / 128.0),
             reads=[acc(pr), acc(self.eps1)], writes=[acc(rs)])
        p.op("dve", lambda h: h.reciprocal(rs.t[:, :], rs.t[:, :]), reads=[acc(rs)], writes=[acc(rs)])
        p.op("dve", lambda h: h.scalar_tensor_tensor(o.t[:, :], o.t[:, :], lsm.t[:, 5:6], rs.t[:, :], ALU.mult, ALU.mult),
             reads=[acc(o), acc(rs), acc(lsm)], writes=[acc(o)])
        p.op("pool", lambda h: h.tensor_tensor(self.oT.t[:, hh, qlo:qhi], o.t[:, :], self.zT.t[:, hh, qlo:qhi], ALU.mult),
             reads=[acc(o), acc3(self.zT, [hh], qlo, qhi)], writes=[acc3(self.oT, [hh], qlo, qhi)])

    LA = 2
    for i in range(len(items) + LA):
        if i < len(items):
            emit_s(items[i])
        if i >= LA:
            emit_pv(items[i - LA])
    self.out_proj(li)


Builder.layer_b = layer_b


class Slot:
    def __init__(self, t, a, n=128):
        self.t = t
        self.a = a
        self.n = n
        if len(t.shape) == 3:
            self.ap = t.t[:, :, :].rearrange("q a b -> q (a b)")[:, a:a + n]
        else:
            self.ap = t.t[:, a:a + n]
        self.acc = acc(t, a, a + n)

    def cols(self, lo, hi):
        return Slot(self.t, self.a + lo, hi - lo)


def layer_c(self, li):
    p = self.p
    VB = self.VB
    if not hasattr(self, "S32"):
        self.S32 = p.tensor("S32", [128, 4, 128], F32, gran=128)
        self.S16 = p.tensor("S16", [128, 4, 128], BF16, gran=128)
        self.id32 = p.tensor("id32", [128, 128], F32)
        self.id16 = p.tensor("id16", [128, 128], BF16)
        self.ones32 = p.tensor("ones32", [128, 128], F32)
        self.cw = p.tensor("cw", [128, 160], F32)
        self.l2s = p.tensor("l2s", [128, 64], F32)
    S32, S16, id32, id16, ones32, cw, l2s = self.S32, self.S16, self.id32, self.id16, self.ones32, self.cw, self.l2s
    GS = self.ropeC
    MK = self.ropeS
    KEEP, BND, ONW, ONE = 0, 1, 2, 3
    G_BETA, G_GC, G_NGC, G_EGC, G_KDEC, G_BG, G_EGT = range(7)

    def gs(idx, t, c0, c1):
        a = idx * 128 + t * 16
        return Slot(GS, a + c0, c1 - c0)

    def mk(idx):
        return Slot(MK, idx * 128)

    self.modulate(li)
    if not hasattr(self, "lamv"):
        self.lamv = p.tensor("lamv", [128, 512], BF16)
    M16 = self.lamv
    p.dma("sp", MK.t[:, :], self.inp("l2_consts", [128, 1024]), writes=[acc(MK)])
    for k_, src_ in enumerate((0, 1, 4, 5)):
        p.op("dve", lambda h, k_=k_, src_=src_: h.tensor_copy(M16.t[:, k_ * 128:(k_ + 1) * 128], MK.t[:, src_ * 128:(src_ + 1) * 128]),
             reads=[acc(MK, src_ * 128, (src_ + 1) * 128)], writes=[acc(M16)])
    p.dma("sp", id32.t[:, :], self.inp("l2_id32", [128, 128]), writes=[acc(id32)])
    p.op("dve", lambda h: h.tensor_copy(id16.t[:, :], id32.t[:, :]), reads=[acc(id32)], writes=[acc(id16)])
    p.op("dve", lambda h: h.memset(ones32.t[:, :], 1.0), writes=[acc(ones32)])
    p.dma("sp", cw.t[:, 0:72], self.inp("l2_cw", [128, 72]), writes=[acc(cw)])
    p.dma("sp", l2s.t[:, 0:36], self.inp("l2_scal", [128, 36]), writes=[acc(l2s)])
    p.op("dve", lambda h: h.tensor_scalar(cw.t[:, 72:144], cw.t[:, 0:72], l2s.t[:, BND:BND + 1], -1.0, ALU.mult, ALU.mult),
         reads=[acc(cw), acc(l2s)], writes=[acc(cw)])
    p.op("act", lambda h: h.activation(l2s.t[:, 36:52], l2s.t[:, 4:20], AF.Exp), reads=[acc(l2s)], writes=[acc(l2s)])
    p.op("dve", lambda h: h.tensor_scalar(l2s.t[:, 36:52], l2s.t[:, 36:52], -1.0, None, ALU.mult), reads=[acc(l2s)], writes=[acc(l2s)])

    rr = {"s16": 0, "s32": 0, "ps": 0}

    def s16():
        i = rr["s16"]
        rr["s16"] = (i + 1) % (self.NPT * 4)
        return Slot(self.PT[i // 4], (i % 4) * 128)

    pool32 = self.TMP + self.ST

    def s32():
        i = rr["s32"]
        rr["s32"] = (i + 1) % (len(pool32) * 4)
        return Slot(pool32[i // 4], (i % 4) * 128)

    def pst(n=128):
        i = rr["ps"]
        rr["ps"] = (i + 1) % 8
        return Slot(self.PS[i], 0, n)

    def mm(out, lhsT, rhs, start=True, stop=True, extra_r=()):
        p.op("pe", lambda h: h.matmul(out[0], lhsT[0], rhs[0], start=start, stop=stop),
             reads=[lhsT[1], rhs[1]] + list(extra_r), writes=[out[1]], skip_same=True)

    def sl(s):
        return (s.ap, s.acc)

    wn = "l2_in_w"
    QSC = 128.0 ** -0.5

    def ktok(tt):
        if tt < 4:
            return VB, (8 + tt) * 1024
        return self.kcT, (tt - 4) * 1024

    kflat = self.kcT.t[:, :, :].rearrange("q a b -> q (a b)")
    vflat = VB.t[:, :, :].rearrange("q a b -> q (a b)")

    def ktok_slot(tt, hh):
        if tt < 4:
            a = (8 + tt) * 1024 + hh * 128
            s_ = Slot.__new__(Slot)
            s_.t, s_.a, s_.n = VB, a, 128
            s_.ap = vflat[:, a:a + 128]
            s_.acc = acc(VB, a, a + 128)
            return s_
        a = (tt - 4) * 1024 + hh * 128
        s_ = Slot.__new__(Slot)
        s_.t, s_.a, s_.n = self.kcT, a, 128
        s_.ap = kflat[:, a:a + 128]
        s_.acc = acc(self.kcT, a, a + 128)
        return s_

    def vtok_slot(tt, hh):
        a = tt * 1024 + hh * 128
        s_ = Slot.__new__(Slot)
        s_.t, s_.a, s_.n = VB, a, 128
        s_.ap = vflat[:, a:a + 128]
        s_.acc = acc(VB, a, a + 128)
        return s_

    for b in range(3):
        buf = self.next_block(wn, b * 1024)
        for cl in range(8):
            ft = b * 8 + cl
            xp = [None, None]
            xs = []
            for c in range(2):
                ps = self.proj_tile(buf, cl, c)
                xt = self.TMP[self.nxt("tmp", self.NTMP)]
                p.op("act", lambda h, ps=ps, xt=xt: h.copy(xt.t[:, :], ps.t[:, :]), reads=[acc(ps)], writes=[acc(xt)])
                xs.append(xt)
            w0 = cw.t[:, ft * 3 + 0:ft * 3 + 1]
            w1 = cw.t[:, ft * 3 + 1:ft * 3 + 2]
            w2 = cw.t[:, ft * 3 + 2:ft * 3 + 3]
            nb0 = cw.t[:, 72 + ft * 3 + 0:72 + ft * 3 + 1]
            nb2 = cw.t[:, 72 + ft * 3 + 2:72 + ft * 3 + 3]
            ys = []
            for c in range(2):
                y = self.ST[self.nxt("st", self.NST)]
                x = xs[c]
                xo = xs[1 - c]
                p.op("dve", lambda h, y=y, x=x, w1=w1: h.tensor_scalar(y.t[:, :], x.t[:, :], w1, None, ALU.mult),
                     reads=[acc(x), acc(cw)], writes=[acc(y)])
                p.op("dve", lambda h, y=y, x=x, w0=w0: h.scalar_tensor_tensor(y.t[:, 1:512], x.t[:, 0:511], w0, y.t[:, 1:512], ALU.mult, ALU.add),
                     reads=[acc(x), acc(cw), acc(y)], writes=[acc(y)])
                p.op("dve", lambda h, y=y, x=x, w2=w2: h.scalar_tensor_tensor(y.t[:, 0:511], x.t[:, 1:512], w2, y.t[:, 0:511], ALU.mult, ALU.add),
                     reads=[acc(x), acc(cw), acc(y)], writes=[acc(y)])
                if c == 1:
                    p.op("dve", lambda h, y=y, xo=xo, w0=w0: h.scalar_tensor_tensor(y.t[:, 0:1], xo.t[:, 511:512], w0, y.t[:, 0:1], ALU.mult, ALU.add),
                         reads=[acc(xo), acc(cw), acc(y)], writes=[acc(y)])
                    p.op("dve", lambda h, y=y, xo=xo, nb0=nb0: h.scalar_tensor_tensor(y.t[:, 0:1], xo.t[:, 511:512], nb0, y.t[:, 0:1], ALU.mult, ALU.add),
                         reads=[acc(xo), acc(cw), acc(y)], writes=[acc(y)])
                else:
                    p.op("dve", lambda h, y=y, xo=xo, w2=w2: h.scalar_tensor_tensor(y.t[:, 511:512], xo.t[:, 0:1], w2, y.t[:, 511:512], ALU.mult, ALU.add),
                         reads=[acc(xo), acc(cw), acc(y)], writes=[acc(y)])
                    p.op("dve", lambda h, y=y, xo=xo, nb2=nb2: h.scalar_tensor_tensor(y.t[:, 511:512], xo.t[:, 0:1], nb2, y.t[:, 511:512], ALU.mult, ALU.add),
                         reads=[acc(xo), acc(cw), acc(y)], writes=[acc(y)])
                p.op("dve", lambda h, y=y, x=x, nb0=nb0: h.scalar_tensor_tensor(y.t[:, 256:257], x.t[:, 255:256], nb0, y.t[:, 256:257], ALU.mult, ALU.add),
                     reads=[acc(x), acc(cw), acc(y)], writes=[acc(y)])
                p.op("dve", lambda h, y=y, x=x, nb2=nb2: h.scalar_tensor_tensor(y.t[:, 255:256], x.t[:, 256:257], nb2, y.t[:, 255:256], ALU.mult, ALU.add),
                     reads=[acc(x), acc(cw), acc(y)], writes=[acc(y)])
                ys.append(y)
            for c in range(2):
                y = ys[c]
                lo, hi = c * 512, (c + 1) * 512
                if b == 2:
                    hh = cl
                    v16 = self.PT[self.nxt("pt", self.NPT)]
                    p.op("act", lambda h, y=y, v16=v16: h.activation(v16.t[:, :], y.t[:, :], AF.Silu), reads=[acc(y)], writes=[acc(v16)])
                    for q4 in range(4):
                        tt = c * 4 + q4
                        pt_ = pst()
                        mm(sl(pt_), (v16.t[:, q4 * 128:(q4 + 1) * 128], acc(v16, q4 * 128, (q4 + 1) * 128)), (id16.t[:, :], acc(id16)))
                        vs = vtok_slot(tt, hh)
                        p.op("act", lambda h, vs=vs, pt_=pt_: h.copy(vs.ap, pt_.ap), reads=[pt_.acc], writes=[vs.acc])
                else:
                    hh = cl
                    p.op("act", lambda h, y=y: h.activation(y.t[:, :], y.t[:, :], AF.Silu), reads=[acc(y)], writes=[acc(y)])
                    sq = self.PT[self.nxt("pt", self.NPT)]
                    p.op("act", lambda h, y=y, sq=sq: h.activation(sq.t[:, :], y.t[:, :], AF.Square), reads=[acc(y)], writes=[acc(sq)])
                    pss = self.ps_proj()
                    p.op("pe", lambda h, pss=pss, sq=sq: h.matmul(pss.t[:, :], self.ones.t[:, :], sq.t[:, :], start=True, stop=True),
                         reads=[acc(self.ones), acc(sq)], writes=[acc(pss)], skip_same=True)
                    rs = self.TMP[self.nxt("tmp", self.NTMP)]
                    p.op("act", lambda h, rs=rs, pss=pss: h.activation(rs.t[:, :], pss.t[:, :], AF.Sqrt, bias=self.eps1.t[:, 0:1], scale=1.0),
                         reads=[acc(pss), acc(self.eps1)], writes=[acc(rs)])
                    p.op("dve", lambda h, rs=rs: h.reciprocal(rs.t[:, :], rs.t[:, :]), reads=[acc(rs)], writes=[acc(rs)])
                    dst = self.qT if b == 0 else self.kT
                    sc_ = QSC if b == 0 else 1.0
                    p.op("dve", lambda h, y=y, rs=rs, dst=dst, hh=hh, lo=lo, hi=hi, sc_=sc_: h.scalar_tensor_tensor(
                        dst.t[:, hh, lo:hi], y.t[:, :], sc_, rs.t[:, :], ALU.mult, ALU.mult),
                        reads=[acc(y), acc(rs)], writes=[acc3(dst, [hh], lo, hi)])
                    if b == 1:
                        for q4 in range(4):
                            tt = c * 4 + q4
                            pt_ = pst()
                            mm(sl(pt_), (self.kT.t[:, hh, tt * 128:(tt + 1) * 128], acc3(self.kT, [hh], tt * 128, (tt + 1) * 128)), (id16.t[:, :], acc(id16)))
                            ks = ktok_slot(tt, hh)
                            p.op("act", lambda h, ks=ks, pt_=pt_: h.copy(ks.ap, pt_.ap), reads=[pt_.acc], writes=[ks.acc])
    buf = self.next_block(wn, 3072)
    for cl in range(8):
        for c in range(2):
            lo, hi = c * 512, (c + 1) * 512
            ps = self.proj_tile(buf, cl, c)
            p.op("act", lambda h, ps=ps, cl=cl, lo=lo, hi=hi: h.activation(self.zT.t[:, cl, lo:hi], ps.t[:, :], AF.Silu),
                 reads=[acc(ps)], writes=[acc3(self.zT, [cl], lo, hi)])
    buf = self.next_block(wn, 4096)
    alog_nA = l2s.t[:, 36:52]
    dtb = l2s.t[:, 20:36]
    one1 = l2s.t[:, ONE:ONE + 1]
    for t in range(8):
        gp = pst(32)
        for kt in range(8):
            p.op("pe", lambda h, kt=kt, t=t, gp=gp, buf=buf: h.matmul(gp.ap, self.hT.t[:, kt, t * 128:(t + 1) * 128], buf.t[:, kt, 0:32],
                                                                  start=(kt == 0), stop=(kt == 7)),
                 reads=[acc(buf), acc3(self.hT, [kt], t * 128, (t + 1) * 128)], writes=[gp.acc], skip_same=True)
        beta = gs(G_BETA, t, 0, 16)
        p.op("act", lambda h, beta=beta, gp=gp: h.activation(beta.ap, gp.ap[:, 0:16], AF.Sigmoid), reads=[gp.acc], writes=[beta.acc])
        sc = s32()
        p.op("dve", lambda h, sc=sc, gp=gp: h.tensor_tensor(sc.ap[:, 0:16], gp.ap[:, 16:32], dtb, ALU.add), reads=[gp.acc, acc(l2s)], writes=[sc.acc])
        p.op("act", lambda h, sc=sc: h.activation(sc.ap[:, 16:32], sc.ap[:, 0:16], AF.Exp), reads=[sc.acc], writes=[sc.acc])
        p.op("act", lambda h, sc=sc: h.activation(sc.ap[:, 32:48], sc.ap[:, 16:32], AF.Ln, bias=one1, scale=1.0), reads=[sc.acc, acc(l2s)], writes=[sc.acc])
        p.op("dve", lambda h, sc=sc: h.tensor_tensor(sc.ap[:, 48:64], sc.ap[:, 32:48], alog_nA, ALU.mult), reads=[sc.acc, acc(l2s)], writes=[sc.acc])
        g32 = (sc.ap[:, 48:64], sc.acc)
        pg = pst(32)
        UT, LT = mk(6), mk(7)
        p.op("pe", lambda h, pg=pg, sc=sc, UT=UT: h.matmul(pg.ap[:, 0:8], UT.ap, sc.ap[:, 48:56], start=True, stop=True),
             reads=[UT.acc, sc.acc], writes=[pg.acc], skip_same=True)
        p.op("pe", lambda h, pg=pg, sc=sc, LT=LT: h.matmul(pg.ap[:, 8:16], LT.ap, sc.ap[:, 56:64], start=True, stop=True),
             reads=[LT.acc, sc.acc], writes=[pg.acc], skip_same=True)
        p.op("pe", lambda h, pg=pg, sc=sc: h.matmul(pg.ap[:, 16:32], ones32.t[:, :], sc.ap[:, 48:64], start=True, stop=True),
             reads=[acc(ones32), sc.acc], writes=[pg.acc], skip_same=True)
        gc, ngc, egc, kdec, bg, egt = (gs(i, t, 0, 16) for i in (G_GC, G_NGC, G_EGC, G_KDEC, G_BG, G_EGT))
        p.op("dve", lambda h, gc=gc, pg=pg: h.tensor_copy(gc.ap, pg.ap[:, 0:16]), reads=[pg.acc], writes=[gc.acc])
        p.op("dve", lambda h, gc=gc, ngc=ngc: h.tensor_scalar(ngc.ap, gc.ap, -1.0, None, ALU.mult), reads=[gc.acc], writes=[ngc.acc])
        p.op("act", lambda h, gc=gc, egc=egc: h.activation(egc.ap, gc.ap, AF.Exp), reads=[gc.acc], writes=[egc.acc])
        p.op("act", lambda h, egt=egt, pg=pg: h.activation(egt.ap, pg.ap[:, 16:32], AF.Exp), reads=[pg.acc], writes=[egt.acc])
        p.op("dve", lambda h, kdec=kdec, pg=pg, ngc=ngc: h.tensor_tensor(kdec.ap, pg.ap[:, 16:32], ngc.ap, ALU.add), reads=[pg.acc, ngc.acc], writes=[kdec.acc])
        p.op("act", lambda h, kdec=kdec: h.activation(kdec.ap, kdec.ap, AF.Exp), reads=[kdec.acc], writes=[kdec.acc])
        p.op("dve", lambda h, bg=bg, beta=beta, egc=egc: h.tensor_tensor(bg.ap, beta.ap, egc.ap, ALU.mult), reads=[beta.acc, egc.acc], writes=[bg.acc])

    if self.debug:
        dq = self.outp("dbg_qT", [128, 8, 1024], BF16)
        dk = self.outp("dbg_kT", [128, 8, 1024], BF16)
        dv = self.outp("dbg_VB", [128, 12, 1024], BF16)
        dkc = self.outp("dbg_kcT", [128, 8, 512], BF16)
        dg = self.outp("dbg_GS", [128, 1024], F32)
        dz = self.outp("dbg_zT", [128, 8, 1024], BF16)
        p.dma("sp", dq, self.qT.t[:, :, :], reads=[acc(self.qT)], is_output=True)
        p.dma("sp", dk, self.kT.t[:, :, :], reads=[acc(self.kT)], is_output=True)
        p.dma("sp", dv, VB.t[:, :, :], reads=[acc(VB)], is_output=True)
        p.dma("sp", dkc, self.kcT.t[:, :, :], reads=[acc(self.kcT)], is_output=True)
        p.dma("sp", dg, GS.t[:, :], reads=[acc(GS)], is_output=True)
        p.dma("sp", dz, self.zT.t[:, :, :], reads=[acc(self.zT)], is_output=True)
    s0 = self.inp("l2_s0", [2, 8, 128, 128])
    nst = self.outp("l2_nst", [2, 4, 8, 128, 128])
    keep = l2s.t[:, KEEP:KEEP + 1]
    onw = l2s.t[:, ONW:ONW + 1]

    def col(idx, t, c):
        s_ = gs(idx, t, c, c + 1)
        return s_

    if not hasattr(self, "XF"):
        self.XF = p.tensor("XF", [128, 512], F32, gran=128)
        self.XB = [p.tensor("XB%d" % i, [128, 512], BF16, gran=128) for i in range(4)]
    f_t = self.TMP + self.ST + [self.XF]
    b_t = self.PT + self.XB
    FS = [[Slot(f_t[2 * ci + j // 4], (j % 4) * 128) for j in range(8)] for ci in range(4)]
    BS = [[Slot(b_t[2 * ci + j // 4], (j % 4) * 128) for j in range(8)] for ci in range(4)]
    psrr = [0, 0, 0, 0]

    def cps(ci):
        i = psrr[ci]
        psrr[ci] = 1 - i
        return Slot(self.PS[2 * ci + i], 0, 128)

    def inst_gen(t, d, hh, ci, first):
        c = d * 8 + hh
        tl, th = t * 128, (t + 1) * 128
        F, B = FS[ci], BS[ci]
        KT_ = (self.kT.t[:, hh, tl:th], acc3(self.kT, [hh], tl, th))
        QT_ = (self.qT.t[:, hh, tl:th], acc3(self.qT, [hh], tl, th))
        ktk = ktok_slot(t, hh)
        vtk = vtok_slot(t, hh)
        gcC, ngcC, egcC, kdecC, bgC, egtC, betaC = (col(i, t, c) for i in (G_GC, G_NGC, G_EGC, G_KDEC, G_BG, G_EGT, G_BETA))
        MTi = (M16.t[:, d * 128:(d + 1) * 128], acc(M16))
        MAs = (M16.t[:, (2 + d) * 128:(3 + d) * 128], acc(M16))

        def diag(colslot, dg):
            p.op("dve", lambda h: h.tensor_scalar(dg.ap, id32.t[:, :], colslot.ap, None, ALU.mult),
                 reads=[acc(id32), colslot.acc], writes=[dg.acc])

        def decay(mask, scale, biascol, o_):
            ps_ = cps(ci)
            mm(sl(ps_), (ones32.t[:, :], acc(ones32)), sl(F[0]), start=True, stop=False)
            mm(sl(ps_), (id16.t[:, :], acc(id16)), mask, start=False, stop=True)
            p.op("act", lambda h: h.activation(o_.ap, ps_.ap, AF.Exp, bias=biascol.ap, scale=scale),
                 reads=[ps_.acc, biascol.acc], writes=[o_.acc])

        def rowscaled(colslot, src, dg, o_):
            diag(colslot, dg)
            ps_ = cps(ci)
            mm(sl(ps_), (ones32.t[:, :], acc(ones32)), sl(dg))
            p.op("dve", lambda h: h.tensor_tensor(o_.ap, src[0], ps_.ap, ALU.mult), reads=[src[1], ps_.acc], writes=[o_.acc])

        def masked(lhsT, rhs, dm, o_):
            ps_ = cps(ci)
            mm(sl(ps_), lhsT, rhs)
            p.op("dve", lambda h: h.tensor_tensor(o_.ap, ps_.ap, dm.ap, ALU.mult), reads=[ps_.acc, dm.acc], writes=[o_.acc])

        diag(gcC, F[0])
        yield
        decay(MAs, -1.0, gcC, B[1])
        yield
        psG = cps(ci)
        mm(sl(psG), KT_, KT_)
        p.op("dve", lambda h: h.scalar_tensor_tensor(F[2].ap, psG.ap, betaC.ap, B[1].ap, ALU.mult, ALU.mult),
             reads=[psG.acc, betaC.acc, B[1].acc], writes=[F[2].acc])
        yield
        psPt = cps(ci)
        mm(sl(psPt), sl(F[2]), (id32.t[:, :], acc(id32)))
        p.op("act", lambda h: h.copy(F[3].ap, psPt.ap), reads=[psPt.acc], writes=[F[3].acc])
        yield
        Tt32 = F[6]
        p.op("dve", lambda h: h.tensor_tensor(Tt32.ap, id32.t[:, :], F[3].ap, ALU.subtract), reads=[acc(id32), F[3].acc], writes=[Tt32.acc])
        yield
        cur, nxt_ = (F[2], F[3]), (F[4], F[5])
        for m in (1, 2, 4, 8, 16, 32):
            Am, Pm = cur
            A2, P2 = nxt_
            psA = cps(ci)
            mm(sl(psA), sl(Pm), sl(Am))
            p.op("act", lambda h, A2=A2, psA=psA: h.copy(A2.ap, psA.ap), reads=[psA.acc], writes=[A2.acc])
            yield
            if m < 32:
                psP = cps(ci)
                mm(sl(psP), sl(Am), sl(Pm))
                p.op("dve", lambda h, P2=P2, psP=psP: h.tensor_copy(P2.ap, psP.ap), reads=[psP.acc], writes=[P2.acc])
                yield
            psT = cps(ci)
            mm(sl(psT), sl(A2), sl(Tt32))
            p.op("dve", lambda h, psT=psT: h.tensor_tensor(Tt32.ap, Tt32.ap, psT.ap, ALU.add), reads=[Tt32.acc, psT.acc], writes=[Tt32.acc])
            yield
            cur, nxt_ = nxt_, cur
        Tt16 = B[1]
        p.op("act", lambda h: h.copy(Tt16.ap, Tt32.ap), reads=[Tt32.acc], writes=[Tt16.acc])
        yield
        decay(MTi, 1.0, ngcC, B[0])
        yield
        p.op("dve", lambda h: h.tensor_scalar(B[3].ap, id16.t[:, :], egcC.ap, None, ALU.mult), reads=[acc(id16), egcC.acc], writes=[B[3].acc])
        psR = cps(ci)
        mm(sl(psR), (self.ones.t[:, :], acc(self.ones)), sl(B[3]))
        p.op("dve", lambda h: h.tensor_tensor(B[2].ap, QT_[0], psR.ap, ALU.mult), reads=[QT_[1], psR.acc], writes=[B[2].acc])
        yield
        masked(KT_, QT_, B[0], B[3])
        yield
        Vb, Kbg, Kdec, WT = B[4], B[5], B[6], B[7]
        p.op("pool", lambda h: h.tensor_scalar(Vb.ap, vtk.ap, betaC.ap, None, ALU.mult), reads=[vtk.acc, betaC.acc], writes=[Vb.acc])
        p.op("pool", lambda h: h.tensor_scalar(Kbg.ap, ktk.ap, bgC.ap, None, ALU.mult), reads=[ktk.acc, bgC.acc], writes=[Kbg.acc])
        p.op("pool", lambda h: h.tensor_scalar(Kdec.ap, ktk.ap, kdecC.ap, None, ALU.mult), reads=[ktk.acc, kdecC.acc], writes=[Kdec.acc])
        yield
        psU = cps(ci)
        mm(sl(psU), sl(Tt16), sl(Vb))
        U = F[7]
        p.op("act", lambda h: h.copy(U.ap, psU.ap), reads=[psU.acc], writes=[U.acc])
        yield
        psW = cps(ci)
        mm(sl(psW), sl(Kbg), sl(Tt16))
        p.op("dve", lambda h: h.tensor_copy(WT.ap, psW.ap), reads=[psW.acc], writes=[WT.acc])
        yield
        S16s = Slot(S16, ci * 128)
        S32s = Slot(S32, ci * 128)
        S16ap = (S16.t[:, ci, :], S16s.acc)
        S32ap = S32.t[:, ci, :]
        psWS = cps(ci)
        mm(sl(psWS), sl(WT), S16ap)
        Vn = B[4]
        p.op("dve", lambda h: h.tensor_tensor(Vn.ap, U.ap, psWS.ap, ALU.subtract), reads=[U.acc, psWS.acc], writes=[Vn.acc])
        yield
        psS = cps(ci)
        mm(sl(psS), sl(Kdec), sl(Vn))
        p.op("dve", lambda h: h.scalar_tensor_tensor(S32ap, S32ap, egtC.ap, psS.ap, ALU.mult, ALU.add),
             reads=[S32s.acc, egtC.acc, psS.acc], writes=[S32s.acc])
        yield
        psO = cps(ci)
        mm(sl(psO), S16ap, sl(B[2]), start=True, stop=False)
        mm(sl(psO), sl(Vn), sl(B[3]), start=False, stop=True)
        oslot_ap = self.oT.t[:, hh, tl:th]
        oacc = acc3(self.oT, [hh], tl, th)
        if first:
            p.op("act", lambda h: h.copy(oslot_ap, psO.ap), reads=[psO.acc], writes=[oacc])
            yield
        else:
            ot, rs, sq = F[2], F[3], B[5]
            p.op("dve", lambda h: h.tensor_tensor(ot.ap, psO.ap, oslot_ap, ALU.add), reads=[psO.acc, oacc], writes=[ot.acc])
            p.op("act", lambda h: h.activation(sq.ap, ot.ap, AF.Square), reads=[ot.acc], writes=[sq.acc])
            yield
            pss = cps(ci)
            mm(sl(pss), (self.ones.t[:, :], acc(self.ones)), sl(sq))
            p.op("act", lambda h: h.activation(rs.ap, pss.ap, AF.Sqrt, bias=self.eps1.t[:, 0:1], scale=1.0 / 128.0),
                 reads=[pss.acc, acc(self.eps1)], writes=[rs.acc])
            yield
            p.op("dve", lambda h: h.reciprocal(rs.ap, rs.ap), reads=[rs.acc], writes=[rs.acc])
            p.op("dve", lambda h: h.scalar_tensor_tensor(ot.ap, ot.ap, onw, rs.ap, ALU.mult, ALU.mult),
                 reads=[ot.acc, rs.acc, acc(l2s)], writes=[ot.acc])
            zacc = acc3(self.zT, [hh], tl, th)
            p.op("pool", lambda h: h.tensor_tensor(oslot_ap, ot.ap, self.zT.t[:, hh, tl:th], ALU.mult),
                 reads=[ot.acc, zacc], writes=[oacc])
            yield

    def chain_gen(ci, d, hh):
        S32s = Slot(S32, ci * 128)
        S16s = Slot(S16, ci * 128)
        p.dma("sp", S32.t[:, ci, :], s0[d, hh, :, :], writes=[S32s.acc])
        p.op("dve", lambda h: h.tensor_scalar(S32.t[:, ci, :], S32.t[:, ci, :], keep, None, ALU.mult),
             reads=[S32s.acc, acc(l2s)], writes=[S32s.acc])
        p.op("act", lambda h: h.copy(S16.t[:, ci, :], S32.t[:, ci, :]), reads=[S32s.acc], writes=[S16s.acc])
        yield
        for n in range(8):
            t = n if d == 0 else 7 - n
            yield from inst_gen(t, d, hh, ci, n < 4)
            if n % 2 == 1:
                p.dma("sp", nst[d, t // 2, hh, :, :], S32.t[:, ci, :], reads=[S32s.acc], is_output=True)
                if n < 7:
                    p.op("dve", lambda h: h.tensor_scalar(S32.t[:, ci, :], S32.t[:, ci, :], keep, None, ALU.mult),
                         reads=[S32s.acc, acc(l2s)], writes=[S32s.acc])
            if n < 7:
                p.op("act", lambda h: h.copy(S16.t[:, ci, :], S32.t[:, ci, :]), reads=[S32s.acc], writes=[S16s.acc])
            yield

    for hp in range(4):
        chains = [(d, 2 * hp + e) for e in range(2) for d in range(2)]
        gens = [chain_gen(ci, d, hh) for ci, (d, hh) in enumerate(chains)]
        alive = list(gens)
        while alive:
            for g_ in list(alive):
                try:
                    next(g_)
                except StopIteration:
                    alive.remove(g_)
    if self.debug:
        do = self.outp("dbg_oT", [128, 8, 1024], BF16)
        p.dma("sp", do, self.oT.t[:, :, :], reads=[acc(self.oT)], is_output=True)
    p.dma("sp", self.ropeC.t[:, :], self.inp("ropeC", [128, 1024]), writes=[acc(self.ropeC)])
    p.dma("sp", self.ropeS.t[:, :], self.inp("ropeS", [128, 1024]), writes=[acc(self.ropeS)])
    self.out_proj(li)


Builder.layer_c = layer_c
```

```python
import numpy as np
from contextlib import ExitStack
import concourse.bass as bass
import concourse.mybir as mybir
from concourse.bass_utils import run_bass_kernel_spmd

F32 = mybir.dt.float32
BF16 = mybir.dt.bfloat16
AF = mybir.ActivationFunctionType
ALU = mybir.AluOpType
AX = mybir.AxisListType

ENGS = ("pe", "act", "dve", "pool", "sp")
NEG = -30000.0


class T:
    def __init__(self, prog, name, shape, dtype, gran, psum=False):
        self.name = name
        self.shape = shape
        self.F = int(np.prod(shape[1:]))
        self.gran = gran
        self.nreg = (self.F + gran - 1) // gran
        self.w = [None] * self.nreg
        self.r = [[] for _ in range(self.nreg)]
        self.psum = psum
        if psum:
            self.t = prog.es.enter_context(prog.nc.psum_tensor("pp_" + name, list(shape), dtype))
        else:
            self.t = prog.es.enter_context(prog.nc.sbuf_tensor("sb_" + name, list(shape), dtype))

    def regs(self, a, b):
        return range(a // self.gran, (b - 1) // self.gran + 1)


class Acc:
    def __init__(self, t, ranges):
        self.t = t
        self.ranges = ranges


def acc(t, a=None, b=None):
    if a is None:
        return Acc(t, [(0, t.F)])
    return Acc(t, [(a, b)])


def acc3(t, kts, lo, hi):
    inner = t.shape[-1] if len(t.shape) == 3 else None
    return Acc(t, [(k * inner + lo, k * inner + hi) for k in kts])


class Prog:
    def __init__(self, nc, n_dma_sems=8):
        self.nc = nc
        self.es = ExitStack()
        self.ops = {e: [] for e in ENGS}
        self.sem = {}
        for e in ENGS:
            self.sem[e] = self.es.enter_context(nc.semaphore("s_" + e))
        self.cnt = {e: 0 for e in ENGS}
        self.waited = {e: {} for e in ENGS}
        self.dsem = [self.es.enter_context(nc.semaphore("d%d" % i)) for i in range(n_dma_sems)]
        self.dval = [0] * n_dma_sems
        self.dnext = 0
        self.final_events = []

    def tensor(self, name, shape, dtype, gran=None, psum=False):
        F = int(np.prod(shape[1:]))
        return T(self, name, shape, dtype, gran or F, psum)

    def _deps(self, reads, writes):
        deps = set()
        for a in reads:
            for (lo, hi) in a.ranges:
                for g in a.t.regs(lo, hi):
                    if a.t.w[g] is not None:
                        deps.add(a.t.w[g])
        for a in writes:
            for (lo, hi) in a.ranges:
                for g in a.t.regs(lo, hi):
                    if a.t.w[g] is not None:
                        deps.add(a.t.w[g])
                    for ev in a.t.r[g]:
                        deps.add(ev)
        return deps

    def _mark(self, reads, writes, ev):
        for a in reads:
            for (lo, hi) in a.ranges:
                for g in a.t.regs(lo, hi):
                    a.t.r[g].append(ev)
        for a in writes:
            for (lo, hi) in a.ranges:
                for g in a.t.regs(lo, hi):
                    a.t.w[g] = ev
                    a.t.r[g] = []

    def _waits(self, eng, deps, skip_same=False):
        best = {}
        for (k, v) in deps:
            if skip_same and k == eng:
                continue
            if v > best.get(k, 0):
                best[k] = v
        out = []
        for k, v in best.items():
            if self.waited[eng].get(k, 0) >= v:
                continue
            self.waited[eng][k] = v
            out.append((k, v))
        return out

    def _semh(self, k):
        if isinstance(k, str):
            return self.sem[k]
        if isinstance(k, tuple):
            return self._swh[k]
        return self.dsem[k]

    def op(self, eng, fn, reads=(), writes=(), skip_same=False):
        pr = [a for a in reads if a.t.psum]
        if pr:
            reads = [a for a in reads if not a.t.psum]
            writes = list(writes) + pr
        deps = self._deps(reads, writes)
        waits = self._waits(eng, deps, skip_same)
        self.cnt[eng] += 1
        ev = (eng, self.cnt[eng])
        self._mark(reads, writes, ev)
        semh = self.sem[eng]
        wl = [(self._semh(k), v) for (k, v) in waits]

        def emit(h):
            for (s, v) in wl[:-1]:
                h.wait_ge(s, v)
            ins = fn(h)
            if wl:
                ins = ins._wait_ge(wl[-1][0], wl[-1][1])
            ins.then_inc(semh, 1)

        self.ops[eng].append(emit)
        return ev

    def dma(self, eng, out_ap, in_ap, reads=(), writes=(), is_output=False, slot=None):
        if eng == "pool":
            return self.dma_sw(out_ap, in_ap, reads, writes, slot)
        deps = self._deps(reads, writes)
        i = self.dnext
        self.dnext = (self.dnext + 1) % len(self.dsem)
        if self.dval[i] > 0:
            deps.add((i, self.dval[i]))
        waits = self._waits(eng, deps)
        self.dval[i] += 16
        ev = (i, self.dval[i])
        self._mark(reads, writes, ev)
        semh = self.dsem[i]
        wl = [(self._semh(k), v) for (k, v) in waits]

        def emit(h):
            for (s, v) in wl:
                h.wait_ge(s, v)
            h.dma_start(out=out_ap, in_=in_ap).then_inc(semh, 16)

        self.ops[eng].append(emit)
        if is_output:
            self.final_events.append(ev)
        return ev

    def dma_sw(self, out_ap, in_ap, reads, writes, slot):
        eng = "pool"
        if not hasattr(self, "_swh"):
            self._swh = {}
        n = len(self._swh)
        semh = self.es.enter_context(self.nc.semaphore("w%d" % n))
        key = ("sw", n)
        self._swh[key] = semh
        deps = self._deps(reads, writes)
        waits = self._waits(eng, deps)
        ev = (key, 16)
        self._mark(reads, writes, ev)
        wl = [(self._semh(k), v) for (k, v) in waits]

        def emit(h):
            for (s, v) in wl:
                h.wait_ge(s, v)
            h.dma_start(out=out_ap, in_=in_ap).then_inc(semh, 16)

        self.ops[eng].append(emit)
        return ev

    def finish(self):
        waits = self._waits("sp", set(self.final_events))
        wl = [(self._semh(k), v) for (k, v) in waits]

        def emit(h):
            for (s, v) in wl:
                h.wait_ge(s, v)

        self.ops["sp"].append(emit)
        nc = self.nc
        ops = self.ops
        with nc.Block() as block:
            @block.tensor
            def _(h):
                for f in ops["pe"]:
                    f(h)

            @block.scalar
            def _(h):
                for f in ops["act"]:
                    f(h)

            @block.vector
            def _(h):
                for f in ops["dve"]:
                    f(h)

            @block.gpsimd
            def _(h):
                for f in ops["pool"]:
                    f(h)

            @block.sync
            def _(h):
                for f in ops["sp"]:
                    f(h)
        self.es.close()


D = 1024
NT = 1024
KT = 8
IN_COLS = {0: 2560, 1: 4096, 2: 4128, 3: 2560}
KIND = {0: "a", 1: "b", 2: "c", 3: "a"}


def lambda_init_for(layer):
    import math
    return 0.8 - 0.6 * math.exp(-0.3 * layer)


class Builder:
    def __init__(self, layers=(0, 1, 2, 3), debug=False):
        self.layers = layers
        self.debug = debug
        self.nc = nc = bass.Bass("TRN2", target_bir_lowering=False)
        self.p = p = Prog(nc)
        self.din = {}
        self.dout = {}
        self.rr = {}
        self._alloc()

    def inp(self, name, shape, dtype=F32):
        if name not in self.din:
            self.din[name] = self.nc.dram_tensor(name, list(shape), dtype, kind="ExternalInput").ap()
        return self.din[name]

    def outp(self, name, shape, dtype=F32):
        if name not in self.dout:
            self.dout[name] = self.nc.dram_tensor(name, list(shape), dtype, kind="ExternalOutput").ap()
        return self.dout[name]

    def nxt(self, key, n):
        v = self.rr.get(key, 0)
        self.rr[key] = (v + 1) % n
        return v

    def _alloc(self):
        p = self.p
        self.xT = p.tensor("xT", [128, 8, 1024], F32, gran=128)
        self.hT = p.tensor("hT", [128, 8, 1024], BF16, gran=128)
        self.qT = p.tensor("qT", [128, 8, 1024], BF16, gran=128)
        self.kT = p.tensor("kT", [128, 8, 1024], BF16, gran=128)
        self.kcT = p.tensor("kcT", [128, 8, 512], BF16, gran=128)
        self.VB = p.tensor("VB", [128, 12, 1024], BF16, gran=128)
        self.zT = p.tensor("zT", [128, 8, 1024], BF16, gran=128)
        self.oT = self.hT
        self.NW = 2
        self.W = [p.tensor("W%d" % i, [128, 8, 1024], BF16) for i in range(self.NW)]
        self.rstd = p.tensor("rstd", [128, 1024], F32, gran=512)
        self.ropeC = p.tensor("ropeC", [128, 1024], F32, gran=128)
        self.ropeS = p.tensor("ropeS", [128, 1024], F32, gran=128)
        self.Rm = p.tensor("Rm", [128, 128], BF16)
        self.ones = p.tensor("ones", [128, 128], BF16)
        self.NPT = 4
        self.PT = [p.tensor("PT%d" % i, [128, 512], BF16, gran=128) for i in range(self.NPT)]
        self.NST = 3
        self.ST = [p.tensor("ST%d" % i, [128, 512], F32, gran=128) for i in range(self.NST)]
        self.NTMP = 4
        self.TMP = [p.tensor("TMP%d" % i, [128, 512], F32, gran=128) for i in range(self.NTMP)]
        self.maskP = p.tensor("maskP", [128, 8, 128], BF16)
        self.maskN = p.tensor("maskN", [128, 8, 128], BF16)
        self.ctxb = p.tensor("ctxb", [128, 1], F32)
        self.zero1 = p.tensor("zero1", [128, 1], F32)
        self.eps1 = p.tensor("eps1", [128, 1], F32)
        self.cond = p.tensor("cond", [128, 8], F32)
        self.scond = p.tensor("scond", [128, 8], BF16)
        self.vec = p.tensor("vec", [128, 64], F32)
        self.mT = p.tensor("mT", [128, 24], F32)
        self.gain = p.tensor("gain", [128, 8], F32)
        self.sink = p.tensor("sink", [128, 16], F32)
        self.esink = p.tensor("esink", [128, 16], F32)
        self.finw = p.tensor("finw", [128, 8], F32)
        self.PS = [p.tensor("ps%d" % i, [128, 512], F32, psum=True) for i in range(8)]
        self.wcount = 0

    def ps_proj(self):
        return self.PS[self.nxt("pj", 2)]

    def ps_rope(self):
        return self.PS[2]

    def ps_s(self):
        return self.PS[3 + self.nxt("s", 3)]

    def ps_o(self):
        return self.PS[6 + self.nxt("o", 2)]

    def plan_weights(self):
        plan = []
        for li in self.layers:
            for c0 in range(0, 3072, 1024):
                plan.append(("l%d_mod_w" % li, 3072, c0, 1024))
            nc_ = IN_COLS[li]
            for c0 in range(0, nc_, 1024):
                plan.append(("l%d_in_w" % li, nc_, c0, min(1024, nc_ - c0)))
            plan.append(("l%d_out_w" % li, 1024, 0, 1024))
        self.wplan = plan
        self.wissued = 0
        self.wused = 0

    def _issue_block(self):
        p = self.p
        wname, ncols, c0, ccount = self.wplan[self.wissued]
        w = self.inp(wname, [1024, ncols])
        buf = self.W[self.wissued % self.NW]
        self.wissued += 1
        src = w.rearrange("(kt q) c -> q kt c", q=128)[:, :, c0:c0 + ccount]
        p.dma("pool", buf.t[:, :, 0:ccount], src, writes=[acc(buf)])

    def next_block(self, wname, c0):
        i = self.wused
        assert self.wplan[i][0] == wname and self.wplan[i][2] == c0, (self.wplan[i], wname, c0)
        while self.wissued <= min(i + 1, len(self.wplan) - 1):
            self._issue_block()
        self.wused += 1
        return self.W[i % self.NW]

    def setup(self):
        p = self.p
        xin = self.inp("xT", [128, 8, 1024])
        for kt in range(8):
            p.dma("sp", self.xT.t[:, kt, :], xin[:, kt, :], writes=[acc3(self.xT, [kt], 0, 1024)])
        p.dma("sp", self.cond.t[:, :], self.inp("cond", [128, 8]), writes=[acc(self.cond)])
        p.dma("sp", self.ropeC.t[:, :], self.inp("ropeC", [128, 1024]), writes=[acc(self.ropeC)])
        p.dma("sp", self.ropeS.t[:, :], self.inp("ropeS", [128, 1024]), writes=[acc(self.ropeS)])
        p.dma("sp", self.ctxb.t[:, :], self.inp("ctxb", [128, 1]), writes=[acc(self.ctxb)])
        p.dma("sp", self.finw.t[:, :], self.inp("finw", [128, 8]), writes=[acc(self.finw)])
        p.dma("pool", self.Rm.t[:, :], self.inp("Rm", [128, 128]), writes=[acc(self.Rm)])
        p.dma("pool", self.maskP.t[:, :, :], self.inp("maskP", [128, 8, 128]), writes=[acc(self.maskP)])
        p.dma("pool", self.maskN.t[:, :, :], self.inp("maskN", [128, 8, 128]), writes=[acc(self.maskN)])
        p.op("dve", lambda h: h.memset(self.ones.t[:, :], 1.0), writes=[acc(self.ones)])
        p.op("dve", lambda h: h.memset(self.zero1.t[:, :], 0.0), writes=[acc(self.zero1)])
        p.op("dve", lambda h: h.memset(self.eps1.t[:, :], 1e-6), writes=[acc(self.eps1)])
        p.op("act", lambda h: h.activation(self.scond.t[:, :], self.cond.t[:, :], AF.Silu),
             reads=[acc(self.cond)], writes=[acc(self.scond)])

    def compute_rstd(self):
        p = self.p
        sq = self.hT
        for c in range(2):
            lo, hi = c * 512, (c + 1) * 512
            for kt in range(8):
                p.op("act", lambda h, kt=kt, lo=lo, hi=hi: h.activation(sq.t[:, kt, lo:hi], self.xT.t[:, kt, lo:hi], AF.Square),
                     reads=[acc3(self.xT, [kt], lo, hi)], writes=[acc3(sq, [kt], lo, hi)])
            ps = self.ps_proj()
            for kt in range(8):
                p.op("pe", lambda h, kt=kt, lo=lo, hi=hi, ps=ps: h.matmul(ps.t[:, :], self.ones.t[:, :], sq.t[:, kt, lo:hi],
                                                                     start=(kt == 0), stop=(kt == 7)),
                     reads=[acc(self.ones), acc3(sq, [kt], lo, hi)], writes=[acc(ps)], skip_same=True)
            p.op("act", lambda h, ps=ps, lo=lo, hi=hi: h.activation(self.rstd.t[:, lo:hi], ps.t[:, :], AF.Sqrt,
                                                               bias=self.eps1.t[:, 0:1], scale=1.0 / 1024.0),
                 reads=[acc(ps), acc(self.eps1)], writes=[acc(self.rstd, lo, hi)])
            p.op("dve", lambda h, lo=lo, hi=hi: h.reciprocal(self.rstd.t[:, lo:hi], self.rstd.t[:, lo:hi]),
                 reads=[acc(self.rstd, lo, hi)], writes=[acc(self.rstd, lo, hi)])

    def modulate(self, li):
        p = self.p
        vec = self.vec
        p.dma("sp", vec.t[:, 0:8], self.inp("l%d_norm_w" % li, [128, 8]), writes=[acc(vec)])
        p.dma("sp", vec.t[:, 8:32], self.inp("l%d_mod_b" % li, [128, 24]), writes=[acc(vec)])
        self.compute_rstd()
        ps = self.PS[2]
        for b in range(3):
            buf = self.next_block("l%d_mod_w" % li, b * 1024)
            for cl in range(8):
                c = b * 8 + cl
                for kt in range(8):
                    p.op("pe", lambda h, buf=buf, cl=cl, c=c, kt=kt: h.matmul(
                        ps.t[:, c:c + 1], buf.t[:, kt, cl * 128:(cl + 1) * 128], self.scond.t[:, kt:kt + 1],
                        start=(kt == 0), stop=(kt == 7)),
                        reads=[acc(buf), acc(self.scond)], writes=[acc(ps)], skip_same=True)
        p.op("dve", lambda h: h.tensor_tensor(self.mT.t[:, :], ps.t[:, 0:24], vec.t[:, 8:32], ALU.add),
             reads=[acc(ps), acc(vec)], writes=[acc(self.mT)])
        p.op("dve", lambda h: h.scalar_tensor_tensor(self.gain.t[:, :], self.mT.t[:, 8:16], 1.0, vec.t[:, 0:8],
                                                     ALU.add, ALU.mult),
             reads=[acc(self.mT), acc(vec)], writes=[acc(self.gain)])
        for kt in range(8):
            for c in range(2):
                lo, hi = c * 512, (c + 1) * 512
                tmp = self.TMP[self.nxt("tmp", self.NTMP)]
                p.op("dve", lambda h, kt=kt, lo=lo, hi=hi, tmp=tmp: h.scalar_tensor_tensor(
                    tmp.t[:, :], self.xT.t[:, kt, lo:hi], self.gain.t[:, kt:kt + 1], self.rstd.t[:, lo:hi],
                    ALU.mult, ALU.mult),
                    reads=[acc3(self.xT, [kt], lo, hi), acc(self.gain), acc(self.rstd, lo, hi)], writes=[acc(tmp)])
                p.op("act", lambda h, kt=kt, lo=lo, hi=hi, tmp=tmp: h.activation(
                    self.hT.t[:, kt, lo:hi], tmp.t[:, :], AF.Identity, bias=self.mT.t[:, kt:kt + 1], scale=1.0),
                    reads=[acc(tmp), acc(self.mT)], writes=[acc3(self.hT, [kt], lo, hi)])

    def proj_tile(self, buf, cl, c, src=None):
        p = self.p
        src = src or self.hT
        ps = self.ps_proj()
        lo, hi = c * 512, (c + 1) * 512
        for kt in range(8):
            p.op("pe", lambda h, kt=kt: h.matmul(ps.t[:, :], buf.t[:, kt, cl * 128:(cl + 1) * 128], src.t[:, kt, lo:hi],
                                                 start=(kt == 0), stop=(kt == 7)),
                 reads=[acc(buf), acc3(src, [kt], lo, hi)], writes=[acc(ps)], skip_same=True)
        self.rope_flush()
        return ps

    def rope_evac(self, ps, dst, dt_, c):
        p = self.p
        self.rope_flush()
        lo, hi = c * 512, (c + 1) * 512
        qb = self.PT[self.nxt("pt", self.NPT)]
        p.op("act", lambda h: h.copy(qb.t[:, :], ps.t[:, :]), reads=[acc(ps)], writes=[acc(qb)])

        def stage_b():
            pr = self.ps_rope()
            p.op("pe", lambda h: h.matmul(pr.t[:, :], self.Rm.t[:, :], qb.t[:, :], start=True, stop=True),
                 reads=[acc(self.Rm), acc(qb)], writes=[acc(pr)], skip_same=True)
            t1 = self.TMP[self.nxt("tmp", self.NTMP)]
            t2 = self.TMP[self.nxt("tmp", self.NTMP)]
            p.op("dve", lambda h: h.tensor_tensor(t1.t[:, :], ps.t[:, :], self.ropeC.t[:, lo:hi], ALU.mult),
                 reads=[acc(ps), acc(self.ropeC)], writes=[acc(t1)])
            p.op("dve", lambda h: h.tensor_tensor(t2.t[:, :], pr.t[:, :], self.ropeS.t[:, lo:hi], ALU.mult),
                 reads=[acc(pr), acc(self.ropeS)], writes=[acc(t2)])
            p.op("pool", lambda h: h.tensor_tensor(dst.t[:, dt_, lo:hi], t1.t[:, :], t2.t[:, :], ALU.add),
                 reads=[acc(t1), acc(t2)], writes=[acc3(dst, [dt_], lo, hi)])

        self._rope_pending = stage_b

    def rope_flush(self):
        f = getattr(self, "_rope_pending", None)
        if f is not None:
            self._rope_pending = None
            f()

    def out_proj(self, li):
        p = self.p
        for b in range(1):
            buf = self.next_block("l%d_out_w" % li, 0)
            for cl in range(8):
                mt = cl
                for c in range(2):
                    lo, hi = c * 512, (c + 1) * 512
                    ps = self.proj_tile(buf, cl, c, src=self.oT)
                    p.op("dve", lambda h, ps=ps, mt=mt, lo=lo, hi=hi: h.scalar_tensor_tensor(
                        self.xT.t[:, mt, lo:hi], ps.t[:, :], self.mT.t[:, 16 + mt:17 + mt], self.xT.t[:, mt, lo:hi],
                        ALU.mult, ALU.add),
                        reads=[acc(ps), acc(self.mT), acc3(self.xT, [mt], lo, hi)], writes=[acc3(self.xT, [mt], lo, hi)])

    def final(self):
        p = self.p
        self.compute_rstd()
        yT = self.outp("yT", [128, 8, 1024])
        for kt in range(8):
            for c in range(2):
                lo, hi = c * 512, (c + 1) * 512
                st = self.ST[self.nxt("st", self.NST)]
                p.op("dve", lambda h, kt=kt, lo=lo, hi=hi, st=st: h.scalar_tensor_tensor(
                    st.t[:, :], self.xT.t[:, kt, lo:hi], self.finw.t[:, kt:kt + 1], self.rstd.t[:, lo:hi],
                    ALU.mult, ALU.mult),
                    reads=[acc3(self.xT, [kt], lo, hi), acc(self.finw), acc(self.rstd, lo, hi)], writes=[acc(st)])
                p.dma("sp", yT[:, kt, lo:hi], st.t[:, :], reads=[acc(st)], is_output=True)

    def dump_x(self, name):
        out = self.outp(name, [128, 8, 1024])
        for kt in range(8):
            self.p.dma("sp", out[:, kt, :], self.xT.t[:, kt, :], reads=[acc3(self.xT, [kt], 0, 1024)], is_output=True)


def layer_a(self, li):
    p = self.p
    self.modulate(li)
    p.dma("sp", self.sink.t[:, :], self.inp("l%d_sink" % li, [128, 16]), writes=[acc(self.sink)])
    p.op("act", lambda h: h.activation(self.esink.t[:, :], self.sink.t[:, :], AF.Exp),
         reads=[acc(self.sink)], writes=[acc(self.esink)])
    kc = self.inp("l%d_kcT" % li, [128, 2, 512])
    p.dma("pool", self.kcT.t[:, 0:2, :], kc, writes=[acc3(self.kcT, [0, 1], 0, 512)], slot="kcT")
    vc = self.inp("l%d_vc" % li, [512, 256])
    VB = self.VB
    for kt in range(12):
        v4 = VB.t[:, kt, 0:512].rearrange("q (g e) -> q g e", g=4)
        p.op("pool", lambda h, v4=v4: h.memset(v4[:, :, 64:128], 1.0), writes=[acc3(VB, [kt], 0, 512)])
    for t in range(4):
        v4 = VB.t[:, 8 + t, 0:512].rearrange("q (g e) -> q g e", g=4)
        p.dma("pool", v4[:, :, 0:64], vc[t * 128:(t + 1) * 128, :].rearrange("q (g e) -> q g e", g=4),
              writes=[acc3(VB, [8 + t], 0, 512)], slot="VBc%d" % t)
    wn = "l%d_in_w" % li
    nkT = self.outp("l%d_nkT" % li, [128, 2, 1024])
    nv = self.outp("l%d_nv" % li, [1024, 256])
    for b in range(3):
        buf = self.next_block(wn, b * 1024)
        if b == 0:
            for cl in range(8):
                for c in range(2):
                    ps = self.proj_tile(buf, cl, c)
                    self.rope_evac(ps, self.qT, cl, c)
        elif b == 1:
            for cl in range(2):
                for c in range(2):
                    lo, hi = c * 512, (c + 1) * 512
                    ps = self.proj_tile(buf, cl, c)
                    st = self.ST[self.nxt("st", self.NST)]
                    p.op("act", lambda h, ps=ps, st=st: h.copy(st.t[:, :], ps.t[:, :]), reads=[acc(ps)], writes=[acc(st)])
                    p.dma("sp", nkT[:, cl, lo:hi], st.t[:, :], reads=[acc(st)], is_output=True)
                    self.rope_evac(ps, self.kT, cl, c)
            self.rope_flush()
            for tt in range(8):
                ps = self.ps_proj()
                for kt in range(8):
                    p.op("pe", lambda h, kt=kt, tt=tt, ps=ps, buf=buf: h.matmul(
                        ps.t[:, 0:256], self.hT.t[:, kt, tt * 128:(tt + 1) * 128], buf.t[:, kt, 256:512],
                        start=(kt == 0), stop=(kt == 7)),
                        reads=[acc(buf), acc3(self.hT, [kt], tt * 128, (tt + 1) * 128)], writes=[acc(ps)], skip_same=True)
                st = self.ST[self.nxt("st", self.NST)]
                p.op("dve", lambda h, ps=ps, st=st: h.tensor_copy(st.t[:, 0:256], ps.t[:, 0:256]), reads=[acc(ps)], writes=[acc(st)])
                p.dma("sp", nv[tt * 128:(tt + 1) * 128, :], st.t[:, 0:256], reads=[acc(st)], is_output=True)
                v4 = VB.t[:, tt, 0:512].rearrange("q (g e) -> q g e", g=4)
                p.op("act", lambda h, ps=ps, v4=v4: h.copy(v4[:, :, 0:64], ps.t[:, 0:256].rearrange("q (g e) -> q g e", g=4)),
                     reads=[acc(ps)], writes=[acc3(VB, [tt], 0, 512)])
            zlist = [(4 + i, i) for i in range(4)]
        if b >= 1:
            if b == 2:
                zlist = [(i, 4 + i) for i in range(4)]
            for (cl, zt) in zlist:
                for c in range(2):
                    lo, hi = c * 512, (c + 1) * 512
                    ps = self.proj_tile(buf, cl, c)
                    p.op("act", lambda h, ps=ps, zt=zt, lo=lo, hi=hi: h.activation(self.zT.t[:, zt, lo:hi], ps.t[:, :], AF.Silu),
                         reads=[acc(ps)], writes=[acc3(self.zT, [zt], lo, hi)])
    items = []
    for Tk in range(2):
        for u in range(2):
            for j in range(8):
                blocks = []
                if j > 0:
                    blocks.append(("P", j - 1))
                blocks.append(("L", j))
                if j < 7:
                    blocks.append(("N", j + 1))
                for t in range(4):
                    blocks.append(("C", t))
                for bi, (kind, kb) in enumerate(blocks):
                    items.append(dict(Tk=Tk, u=u, j=j, kind=kind, kb=kb, bi=bi, nb=len(blocks)))

    def emit_s(it):
        Tk, u, j, kind, kb = it["Tk"], it["u"], it["j"], it["kind"], it["kb"]
        pl, ph = u * 64, (u + 1) * 64
        qlo, qhi = j * 128, (j + 1) * 128
        sb = self.ps_s()
        if kind == "C":
            lhsT = self.kcT.t[pl:ph, Tk, kb * 128:(kb + 1) * 128]
            rl = acc3(self.kcT, [Tk], kb * 128, (kb + 1) * 128)
            it["vt"] = 8 + kb
            bias = self.ctxb
        else:
            lhsT = self.kT.t[pl:ph, Tk, kb * 128:(kb + 1) * 128]
            rl = acc3(self.kT, [Tk], kb * 128, (kb + 1) * 128)
            it["vt"] = kb
            bias = self.zero1
        rhs = self.qT.t[pl:ph, 4 * Tk:4 * Tk + 4, qlo:qhi]
        p.op("pe", lambda h: h.matmul(sb.t[:, :].rearrange("q (g e) -> q g e", g=4), lhsT, rhs, start=True, stop=True),
             reads=[rl, acc3(self.qT, range(4 * Tk, 4 * Tk + 4), qlo, qhi)], writes=[acc(sb)], skip_same=True)
        pt = self.PT[self.nxt("pt", self.NPT)]
        it["pt"] = pt
        p.op("act", lambda h: h.activation(pt.t[:, :], sb.t[:, :], AF.Exp, bias=bias.t[:, 0:1], scale=0.125),
             reads=[acc(sb), acc(bias)], writes=[acc(pt)])
        if kind in ("P", "N"):
            mk = self.maskP if kind == "P" else self.maskN
            p.op("pool", lambda h: h.tensor_tensor(
                pt.t[:, :].rearrange("q (g e) -> q g e", g=4), pt.t[:, :].rearrange("q (g e) -> q g e", g=4),
                mk.t[:, j, :].unsqueeze(1).broadcast_to([128, 4, 128]), ALU.mult),
                reads=[acc(pt), acc(mk)], writes=[acc(pt)])

    cur_po = [None]

    def emit_pv(it):
        Tk, u, j, bi, nb = it["Tk"], it["u"], it["j"], it["bi"], it["nb"]
        hkv = 2 * Tk + u
        pl, ph = u * 64, (u + 1) * 64
        qlo, qhi = j * 128, (j + 1) * 128
        if bi == 0:
            cur_po[0] = self.ps_o()
        po = cur_po[0]
        vt, pt = it["vt"], it["pt"]
        p.op("pe", lambda h: h.matmul(po.t[:, :], VB.t[:, vt, hkv * 128:(hkv + 1) * 128], pt.t[:, :], start=(bi == 0), stop=(bi == nb - 1)),
             reads=[acc3(VB, [vt], hkv * 128, (hkv + 1) * 128), acc(pt)], writes=[acc(po)], skip_same=True)
        if bi == nb - 1:
            tmp = self.TMP[self.nxt("tmp", self.NTMP)]
            p.op("dve", lambda h: h.tensor_tensor(
                tmp.t[64:128, :].rearrange("q (g e) -> q g e", g=4), po.t[64:128, :].rearrange("q (g e) -> q g e", g=4),
                self.esink.t[64:128, hkv * 4:hkv * 4 + 4].unsqueeze(2).broadcast_to([64, 4, 128]), ALU.add),
                reads=[acc(po), acc(self.esink)], writes=[acc(tmp)])
            p.op("dve", lambda h: h.reciprocal(tmp.t[64:128, :], tmp.t[64:128, :]), reads=[acc(tmp)], writes=[acc(tmp)])
            p.op("dve", lambda h: h.tensor_tensor(
                self.oT.t[pl:ph, 4 * Tk:4 * Tk + 4, qlo:qhi], po.t[0:64, :].rearrange("q (g e) -> q g e", g=4),
                tmp.t[64:128, :].rearrange("q (g e) -> q g e", g=4), ALU.mult),
                reads=[acc(po), acc(tmp)], writes=[acc3(self.oT, range(4 * Tk, 4 * Tk + 4), qlo, qhi)])

    LA = 2
    for i in range(len(items) + LA):
        if i < len(items):
            emit_s(items[i])
        if i >= LA:
            emit_pv(items[i - LA])
    for t in range(8):
        for c in range(2):
            lo, hi = c * 512, (c + 1) * 512
            p.op("pool", lambda h, t=t, lo=lo, hi=hi: h.tensor_tensor(self.oT.t[:, t, lo:hi], self.oT.t[:, t, lo:hi], self.zT.t[:, t, lo:hi], ALU.mult),
                 reads=[acc3(self.oT, [t], lo, hi), acc3(self.zT, [t], lo, hi)], writes=[acc3(self.oT, [t], lo, hi)])
    self.out_proj(li)


Builder.layer_a = layer_a


def build(layers=(0, 1, 2, 3), debug=False):
    b = Builder(layers, debug)
    b.plan_weights()
    b.setup()
    for li in layers:
        k = KIND[li]
        if k == "a":
            b.layer_a(li)
        elif k == "b":
            b.layer_b(li)
        else:
            b.layer_c(li)
        if debug:
            b.dump_x("dbg_x%d" % li)
    b.final()
    print("sbuf bytes remaining", b.nc.sbuf_bytes_remaining)
    b.p.finish()
    return b


def fm(v, n):
    return np.ascontiguousarray(np.asarray(v, np.float32).reshape(n, 128).T)


def to_fm3(a):
    a = np.asarray(a, np.float32)
    n = a.shape[1] // 128
    return np.ascontiguousarray(a.T.reshape(n, 128, a.shape[0]).transpose(1, 0, 2))


def from_fm3(a):
    return np.ascontiguousarray(a.transpose(2, 1, 0).reshape(a.shape[2], -1))


def perm_a():
    perm = np.zeros(1024, np.int64)
    for tq in range(8):
        Tk, g = tq // 4, tq % 4
        for u in range(2):
            hq = 4 * (2 * Tk + u) + g
            perm[tq * 128 + u * 64: tq * 128 + u * 64 + 64] = hq * 64 + np.arange(64)
    return perm


def rope_tables(sample):
    C = np.ones((128, 1024), np.float32)
    S = np.zeros((128, 1024), np.float32)
    if sample:
        tok = np.arange(1024)
        row = (tok // 64).astype(np.float32)
        col = (tok % 64).astype(np.float32)
        inv = (np.float32(10000.0) ** (-np.arange(16, dtype=np.float32) / np.float32(16))).astype(np.float32)
        for pp in range(128):
            d = pp % 64
            pos = row if d < 32 else col
            ang = (pos * inv[d % 16]).astype(np.float32)
            C[pp] = np.cos(ang)
            sgn = -1.0 if (d % 32) < 16 else 1.0
            S[pp] = sgn * np.sin(ang)
    return C, S


def rope_perm_matrix():
    Rm = np.zeros((128, 128), np.float32)
    for m in range(128):
        d = m % 64
        base = m - d
        partner = d + 16 if (d % 32) < 16 else d - 16
        Rm[base + partner, m] = 1.0
    return Rm


def win_masks(sample):
    mp = np.zeros((128, 8, 128), np.float32)
    mn = np.zeros((128, 8, 128), np.float32)
    k = np.arange(128)[:, None]
    q = np.arange(128)[None, :]
    for j in range(8):
        if sample:
            mp[:, j, :] = (k >= q)
            mn[:, j, :] = (k <= q)
        else:
            mp[:, j, :] = 1.0 if (j % 2 == 1) else 0.0
            mn[:, j, :] = 1.0 if (j % 2 == 0) else 0.0
    return mp, mn


_CACHE = {}
LAYERS = (0, 1, 2, 3)
DEBUG = False


def kernel(**inp):
    inp = {k: np.asarray(v) for k, v in inp.items()}
    layers = LAYERS
    key = (tuple(layers), DEBUG)
    if key not in _CACHE:
        _CACHE[key] = build(layers, DEBUG)
    b = _CACHE[key]
    pa = perm_a()
    shared = {}
    for li in layers:
        shared["l%d_mod_w" % li] = np.ascontiguousarray(inp["l%d_mod_w" % li], np.float32)
        w = inp["l%d_in_w" % li]
        ow = inp["l%d_out_w" % li]
        if KIND[li] == "a":
            w = np.concatenate([w[:, pa], w[:, 1024:1536], w[:, 1536 + pa]], axis=1)
            ow = ow[pa, :]
        shared["l%d_in_w" % li] = np.ascontiguousarray(w, np.float32)
        shared["l%d_out_w" % li] = np.ascontiguousarray(ow, np.float32)
        shared["l%d_norm_w" % li] = fm(inp["l%d_norm_w" % li], 8)
        shared["l%d_mod_b" % li] = fm(inp["l%d_mod_b" % li], 24)
        if KIND[li] == "a":
            shared["l%d_sink" % li] = np.ascontiguousarray(np.broadcast_to(inp["l%d_sink" % li].astype(np.float32)[None, :], (128, 16)))
    if 1 in layers:
        lamcat = np.concatenate([inp["l1_lambda_q1"], inp["l1_lambda_k1"], inp["l1_lambda_q2"], inp["l1_lambda_k2"]]).astype(np.float32)
        shared["l1_lam"] = np.ascontiguousarray(np.broadcast_to(lamcat[None, :], (128, 256)))
        shared["l1_subw"] = fm(inp["l1_subln_w"], 1)
    if 2 in layers:
        pp_ = np.arange(128)[:, None]
        ff_ = np.arange(128)[None, :]
        LOW, UPP, LOWI, UPPI = (pp_ > ff_), (pp_ < ff_), (pp_ >= ff_), (pp_ <= ff_)
        consts = [NEG * (1 - UPPI), NEG * (1 - LOWI), NEG * (1 - UPP), NEG * (1 - LOW),
                  -NEG * (1 - LOW), -NEG * (1 - UPP), 1.0 * UPPI, 1.0 * LOWI]
        shared["l2_consts"] = np.ascontiguousarray(np.concatenate([c_.astype(np.float32) for c_ in consts], axis=1))
        shared["l2_id32"] = np.eye(128, dtype=np.float32)
        cwv = inp["l2_conv_w"].astype(np.float32)
        shared["l2_cw"] = np.ascontiguousarray(cwv.reshape(3, 24, 128).transpose(2, 1, 0).reshape(128, 72))
    shared["finw"] = fm(inp["final_norm_w"], 8)
    shared["Rm"] = rope_perm_matrix()
    in_maps = []
    for core in range(8):
        sample = core < 4
        m = dict(shared)
        if sample:
            bb = core
            x = inp["x_sample"][bb]
            cond = inp["c"][bb]
        else:
            bb = core % 4
            s0 = 4 * (core - 4)
            x = inp["x_prompt"][s0:s0 + 4].reshape(1024, 1024)
            cond = inp["c_ctx"]
        m["xT"] = to_fm3(x)
        m["cond"] = fm(cond, 8)
        C, S = rope_tables(sample)
        m["ropeC"], m["ropeS"] = C, S
        mp, mn = win_masks(sample)
        m["maskP"], m["maskN"] = mp, mn
        m["ctxb"] = np.full((128, 1), 0.0 if sample else NEG, np.float32)
        for li in layers:
            if KIND[li] == "a":
                ck = inp["cache_l%d_k" % li][bb]
                cv = inp["cache_l%d_v" % li][bb]
                kcT = ck.transpose(1, 2, 0).reshape(2, 128, 512).transpose(1, 0, 2)
                m["l%d_kcT" % li] = np.ascontiguousarray(kcT, np.float32)
                m["l%d_vc" % li] = np.ascontiguousarray(cv.reshape(512, 256), np.float32)
        if 2 in layers:
            sc_ = np.zeros((128, 36), np.float32)
            sc_[:, 0] = 1.0 if sample else 0.0
            sc_[:, 1] = 0.0 if sample else 1.0
            sc_[:, 2] = inp["l2_onorm_w"].astype(np.float32)
            sc_[:, 3] = 1.0
            sc_[:, 4:20] = inp["l2_a_log"].astype(np.float32).reshape(1, 16)
            sc_[:, 20:36] = inp["l2_dt_bias"].astype(np.float32).reshape(1, 16)
            m["l2_scal"] = sc_
            m["l2_s0"] = np.ascontiguousarray(inp["state_l2"][bb], np.float32)
        if 1 in layers:
            m["l1_kcT"] = to_fm3(inp["cache_l1_k"][bb].reshape(512, 1024))
            m["l1_vc"] = np.ascontiguousarray(inp["cache_l1_v"][bb].reshape(512, 1024), np.float32)
            bt = np.zeros((128, 48), np.float32)
            if not sample:
                for qc in range(4):
                    for kb in range(12):
                        if not (kb < 8 and kb // 2 == qc):
                            bt[:, qc * 12 + kb] = NEG
            m["l1_bias"] = bt
        in_maps.append({k: m[k] for k in b.din})
    res = run_bass_kernel_spmd(b.nc, in_maps, core_ids=list(range(8)))
    R = res.results
    kernel.last = R
    y_sample = np.stack([from_fm3(R[c]["yT"]) for c in range(4)])
    y_prompt = np.concatenate([from_fm3(R[c]["yT"]).reshape(4, 256, 1024) for c in range(4, 8)])
    outs = {}
    for li in (0, 3):
        if li in layers:
            outs["nk%d" % li] = np.concatenate([from_fm3(R[c]["l%d_nkT" % li]).reshape(4, 256, 4, 64) for c in range(4, 8)])
            outs["nv%d" % li] = np.concatenate([R[c]["l%d_nv" % li].reshape(4, 256, 4, 64) for c in range(4, 8)])
        else:
            outs["nk%d" % li] = np.zeros((16, 256, 4, 64), np.float32)
            outs["nv%d" % li] = np.zeros((16, 256, 4, 64), np.float32)
    if 1 in layers:
        nk1 = np.concatenate([from_fm3(R[c]["l1_nkT"]).reshape(4, 256, 8, 2, 64) for c in range(4, 8)])
        nv1 = np.concatenate([R[c]["l1_nv"].reshape(4, 256, 8, 128) for c in range(4, 8)])
    else:
        nk1 = np.zeros((16, 256, 8, 2, 64), np.float32)
        nv1 = np.zeros((16, 256, 8, 128), np.float32)
    if 2 in layers:
        st = np.concatenate([R[c]["l2_nst"].transpose(1, 0, 2, 3, 4) for c in range(4, 8)])
    else:
        st = np.zeros((16, 2, 8, 128, 128), np.float32)
    return (y_prompt.astype(np.float32), y_sample.astype(np.float32), outs["nk0"], outs["nv0"], nk1, nv1, st,
            outs["nk3"], outs["nv3"])


def layer_b(self, li):
    p = self.p
    lam_init = lambda_init_for(li)
    if not hasattr(self, "lamv"):
        self.lamv = p.tensor("lamv", [128, 512], BF16)
        self.lsm = p.tensor("lsm", [128, 8], F32)
        self.l1bias = p.tensor("l1bias", [128, 48], F32)
    lamv, lsm, l1bias = self.lamv, self.lsm, self.l1bias
    lam32 = lamv.t[:, :].bitcast(F32)
    self.modulate(li)
    p.dma("sp", lam32, self.inp("l1_lam", [128, 256]), writes=[acc(lamv)])
    p.dma("sp", l1bias.t[:, :], self.inp("l1_bias", [128, 48]), writes=[acc(l1bias)])
    p.dma("sp", lsm.t[:, 7:8], self.inp("l1_subw", [128, 1]), writes=[acc(lsm)])
    p.op("dve", lambda h: h.tensor_tensor(lam32[:, 0:64], lam32[:, 0:64], lam32[:, 64:128], ALU.mult), reads=[acc(lamv)], writes=[acc(lamv)])
    p.op("dve", lambda h: h.tensor_tensor(lam32[:, 128:192], lam32[:, 128:192], lam32[:, 192:256], ALU.mult), reads=[acc(lamv)], writes=[acc(lamv)])
    p.op("dve", lambda h: h.reduce_sum(lsm.t[:, 0:1], lam32[:, 0:64], AX.X), reads=[acc(lamv), acc(lsm)], writes=[acc(lsm)])
    p.op("dve", lambda h: h.reduce_sum(lsm.t[:, 1:2], lam32[:, 128:192], AX.X), reads=[acc(lamv), acc(lsm)], writes=[acc(lsm)])
    p.op("act", lambda h: h.activation(lsm.t[:, 2:4], lsm.t[:, 0:2], AF.Exp), reads=[acc(lsm)], writes=[acc(lsm)])
    p.op("dve", lambda h: h.tensor_tensor(lsm.t[:, 4:5], lsm.t[:, 3:4], lsm.t[:, 2:3], ALU.subtract), reads=[acc(lsm)], writes=[acc(lsm)])
    p.op("dve", lambda h: h.tensor_scalar(lsm.t[:, 4:5], lsm.t[:, 4:5], -lam_init, None, ALU.add), reads=[acc(lsm)], writes=[acc(lsm)])
    p.op("dve", lambda h: h.tensor_scalar(lsm.t[:, 5:6], lsm.t[:, 7:8], 1.0 - lam_init, None, ALU.mult), reads=[acc(lsm)], writes=[acc(lsm)])
    kc = self.inp("l1_kcT", [128, 8, 512])
    p.dma("pool", self.kcT.t[:, :, :], kc, writes=[acc(self.kcT)])
    vc = self.inp("l1_vc", [512, 1024])
    VB = self.VB
    for t in range(4):
        p.dma("pool", VB.t[:, 8 + t, :], vc[t * 128:(t + 1) * 128, :], writes=[acc3(VB, [8 + t], 0, 1024)])
    wn = "l1_in_w"
    nkT = self.outp("l1_nkT", [128, 8, 1024])
    nv = self.outp("l1_nv", [1024, 1024])
    for b in range(4):
        buf = self.next_block(wn, b * 1024)
        if b == 0:
            for cl in range(8):
                for c in range(2):
                    ps = self.proj_tile(buf, cl, c)
                    self.rope_evac(ps, self.qT, cl, c)
        elif b == 1:
            for cl in range(8):
                for c in range(2):
                    lo, hi = c * 512, (c + 1) * 512
                    ps = self.proj_tile(buf, cl, c)
                    st = self.ST[self.nxt("st", self.NST)]
                    p.op("act", lambda h, ps=ps, st=st: h.copy(st.t[:, :], ps.t[:, :]), reads=[acc(ps)], writes=[acc(st)])
                    p.dma("sp", nkT[:, cl, lo:hi], st.t[:, :], reads=[acc(st)], is_output=True)
                    self.rope_evac(ps, self.kT, cl, c)
        elif b == 2:
            self.rope_flush()
            for tt in range(8):
                for g in range(2):
                    ps = self.ps_proj()
                    for kt in range(8):
                        p.op("pe", lambda h, kt=kt, tt=tt, ps=ps, buf=buf, g=g: h.matmul(
                            ps.t[:, :], self.hT.t[:, kt, tt * 128:(tt + 1) * 128], buf.t[:, kt, g * 512:(g + 1) * 512],
                            start=(kt == 0), stop=(kt == 7)),
                            reads=[acc(buf), acc3(self.hT, [kt], tt * 128, (tt + 1) * 128)], writes=[acc(ps)], skip_same=True)
                    st = self.ST[self.nxt("st", self.NST)]
                    p.op("dve", lambda h, ps=ps, st=st: h.tensor_copy(st.t[:, :], ps.t[:, :]), reads=[acc(ps)], writes=[acc(st)])
                    p.dma("sp", nv[tt * 128:(tt + 1) * 128, g * 512:(g + 1) * 512], st.t[:, :], reads=[acc(st)], is_output=True)
                    p.op("act", lambda h, ps=ps, tt=tt, g=g: h.copy(VB.t[:, tt, g * 512:(g + 1) * 512], ps.t[:, :]),
                         reads=[acc(ps)], writes=[acc3(VB, [tt], g * 512, (g + 1) * 512)])
        else:
            for cl in range(8):
                for c in range(2):
                    lo, hi = c * 512, (c + 1) * 512
                    ps = self.proj_tile(buf, cl, c)
                    p.op("act", lambda h, ps=ps, cl=cl, lo=lo, hi=hi: h.activation(self.zT.t[:, cl, lo:hi], ps.t[:, :], AF.Silu),
                         reads=[acc(ps)], writes=[acc3(self.zT, [cl], lo, hi)])
    items = []
    for hh in range(8):
        for qc in range(2):
            for c in range(2):
                for kb in range(12):
                    items.append(dict(hh=hh, qc=qc, c=c, kb=kb))

    def emit_s(it):
        hh, qc, c, kb = it["hh"], it["qc"], it["c"], it["kb"]
        qlo, qhi = qc * 512, (qc + 1) * 512
        pl, ph = c * 64, (c + 1) * 64
        sb = self.ps_s()
        if kb >= 8:
            lhsT = self.kcT.t[pl:ph, hh, (kb - 8) * 128:(kb - 7) * 128]
            rl = acc3(self.kcT, [hh], (kb - 8) * 128, (kb - 7) * 128)
        else:
            lhsT = self.kT.t[pl:ph, hh, kb * 128:(kb + 1) * 128]
            rl = acc3(self.kT, [hh], kb * 128, (kb + 1) * 128)
        rhs = self.qT.t[pl:ph, hh, qlo:qhi]
        p.op("pe", lambda h: h.matmul(sb.t[:, :], lhsT, rhs, start=True, stop=True),
             reads=[rl, acc3(self.qT, [hh], qlo, qhi)], writes=[acc(sb)], skip_same=True)
        pt = self.PT[self.nxt("pt", self.NPT)]
        it["pt"] = pt
        for hf in range(2):
            bi = (2 * qc + hf) * 12 + kb
            p.op("act", lambda h, hf=hf, bi=bi: h.activation(pt.t[:, hf * 256:(hf + 1) * 256], sb.t[:, hf * 256:(hf + 1) * 256], AF.Exp,
                                                         bias=l1bias.t[:, bi:bi + 1], scale=0.125),
                 reads=[acc(sb), acc(l1bias)], writes=[acc(pt, hf * 256, (hf + 1) * 256)])

    ocs = []

    def emit_pv(it):
        hh, qc, c, kb, pt = it["hh"], it["qc"], it["c"], it["kb"], it["pt"]
        qlo, qhi = qc * 512, (qc + 1) * 512
        po, pd = (self.PS[6], self.PS[7]) if c == 0 else (self.PS[0], self.PS[1])
        p.op("pe", lambda h: h.matmul(po.t[:, :], VB.t[:, kb, hh * 128:(hh + 1) * 128], pt.t[:, :], start=(kb == 0), stop=(kb == 11)),
             reads=[acc3(VB, [kb], hh * 128, (hh + 1) * 128), acc(pt)], writes=[acc(po)], skip_same=True)
        p.op("pe", lambda h: h.matmul(pd.t[:, :], self.ones.t[:, :], pt.t[:, :], start=(kb == 0), stop=(kb == 11)),
             reads=[acc(self.ones), acc(pt)], writes=[acc(pd)], skip_same=True)
        if kb < 11:
            return
        rd = self.TMP[self.nxt("tmp", self.NTMP)]
        p.op("dve", lambda h: h.reciprocal(rd.t[:, :], pd.t[:, :]), reads=[acc(pd)], writes=[acc(rd)])
        oc = self.TMP[self.nxt("tmp", self.NTMP)]
        p.op("dve", lambda h: h.tensor_tensor(oc.t[:, :], po.t[:, :], rd.t[:, :], ALU.mult),
             reads=[acc(po), acc(rd)], writes=[acc(oc)])
        ocs.append(oc)
        if c == 0:
            return
        o0, o1 = ocs[0], ocs[1]
        del ocs[:]
        o = self.TMP[self.nxt("tmp", self.NTMP)]
        p.op("dve", lambda h: h.scalar_tensor_tensor(o.t[:, :], o1.t[:, :], lsm.t[:, 4:5], o0.t[:, :], ALU.mult, ALU.add),
             reads=[acc(o0), acc(o1), acc(lsm)], writes=[acc(o)])
        sq = self.PT[self.nxt("pt", self.NPT)]
        p.op("act", lambda h: h.activation(sq.t[:, :], o.t[:, :], AF.Square), reads=[acc(o)], writes=[acc(sq)])
        pr = self.PS[2]
        p.op("pe", lambda h: h.matmul(pr.t[:, :], self.ones.t[:, :], sq.t[:, :], start=True, stop=True),
             reads=[acc(self.ones), acc(sq)], writes=[acc(pr)], skip_same=True)
        rs = self.TMP[self.nxt("tmp", self.NTMP)]
        p.op("act", lambda h: h.activation(rs.t[:, :], pr.t[:, :], AF.Sqrt, bias=self.eps1.t[:, 0:1], scale=1.0 / 128.0),
             reads=[acc(pr), acc(self.eps1)], writes=[acc(rs)])
        p.op("dve", lambda h: h.reciprocal(rs.t[:, :], rs.t[:, :]), reads=[acc(rs)], writes=[acc(rs)])
        p.op("dve", lambda h: h.scalar_tensor_tensor(o.t[:, :], o.t[:, :], lsm.t[:, 5:6], rs.t[:, :], ALU.mult, ALU.mult),
             reads=[acc(o), acc(rs), acc(lsm)], writes=[acc(o)])
        p.op("pool", lambda h: h.tensor_tensor(self.oT.t[:, hh, qlo:qhi], o.t[:, :], self.zT.t[:, hh, qlo:qhi], ALU.mult),
             reads=[acc(o), acc3(self.zT, [hh], qlo, qhi)], writes=[acc3(self.oT, [hh], qlo, qhi)])

    LA = 2
    for i in range(len(items) + LA):
        if i < len(items):
            emit_s(items[i])
        if i >= LA:
            emit_pv(items[i - LA])
    self.out_proj(li)


Builder.layer_b = layer_b


class Slot:
    def __init__(self, t, a, n=128):
        self.t = t
        self.a = a
        self.n = n
        if len(t.shape) == 3:
            self.ap = t.t[:, :, :].rearrange("q a b -> q (a b)")[:, a:a + n]
        else:
            self.ap = t.t[:, a:a + n]
        self.acc = acc(t, a, a + n)

    def cols(self, lo, hi):
        return Slot(self.t, self.a + lo, hi - lo)


def layer_c(self, li):
    p = self.p
    VB = self.VB
    if not hasattr(self, "S32"):
        self.S32 = p.tensor("S32", [128, 4, 128], F32, gran=128)
        self.S16 = p.tensor("S16", [128, 4, 128], BF16, gran=128)
        self.id32 = p.tensor("id32", [128, 128], F32)
        self.id16 = p.tensor("id16", [128, 128], BF16)
        self.ones32 = p.tensor("ones32", [128, 128], F32)
        self.cw = p.tensor("cw", [128, 160], F32)
        self.l2s = p.tensor("l2s", [128, 64], F32)
    S32, S16, id32, id16, ones32, cw, l2s = self.S32, self.S16, self.id32, self.id16, self.ones32, self.cw, self.l2s
    GS = self.ropeC
    MK = self.ropeS
    KEEP, BND, ONW, ONE = 0, 1, 2, 3
    G_BETA, G_GC, G_NGC, G_EGC, G_KDEC, G_BG, G_EGT = range(7)

    def gs(idx, t, c0, c1):
        a = idx * 128 + t * 16
        return Slot(GS, a + c0, c1 - c0)

    def mk(idx):
        return Slot(MK, idx * 128)

    self.modulate(li)
    if not hasattr(self, "lamv"):
        self.lamv = p.tensor("lamv", [128, 512], BF16)
    M16 = self.lamv
    p.dma("sp", MK.t[:, :], self.inp("l2_consts", [128, 1024]), writes=[acc(MK)])
    for k_, src_ in enumerate((0, 1, 4, 5)):
        p.op("dve", lambda h, k_=k_, src_=src_: h.tensor_copy(M16.t[:, k_ * 128:(k_ + 1) * 128], MK.t[:, src_ * 128:(src_ + 1) * 128]),
             reads=[acc(MK, src_ * 128, (src_ + 1) * 128)], writes=[acc(M16)])
    p.dma("sp", id32.t[:, :], self.inp("l2_id32", [128, 128]), writes=[acc(id32)])
    p.op("dve", lambda h: h.tensor_copy(id16.t[:, :], id32.t[:, :]), reads=[acc(id32)], writes=[acc(id16)])
    p.op("dve", lambda h: h.memset(ones32.t[:, :], 1.0), writes=[acc(ones32)])
    p.dma("sp", cw.t[:, 0:72], self.inp("l2_cw", [128, 72]), writes=[acc(cw)])
    p.dma("sp", l2s.t[:, 0:36], self.inp("l2_scal", [128, 36]), writes=[acc(l2s)])
    p.op("dve", lambda h: h.tensor_scalar(cw.t[:, 72:144], cw.t[:, 0:72], l2s.t[:, BND:BND + 1], -1.0, ALU.mult, ALU.mult),
         reads=[acc(cw), acc(l2s)], writes=[acc(cw)])
    p.op("act", lambda h: h.activation(l2s.t[:, 36:52], l2s.t[:, 4:20], AF.Exp), reads=[acc(l2s)], writes=[acc(l2s)])
    p.op("dve", lambda h: h.tensor_scalar(l2s.t[:, 36:52], l2s.t[:, 36:52], -1.0, None, ALU.mult), reads=[acc(l2s)], writes=[acc(l2s)])

    rr = {"s16": 0, "s32": 0, "ps": 0}

    def s16():
        i = rr["s16"]
        rr["s16"] = (i + 1) % (self.NPT * 4)
        return Slot(self.PT[i // 4], (i % 4) * 128)

    pool32 = self.TMP + self.ST

    def s32():
        i = rr["s32"]
        rr["s32"] = (i + 1) % (len(pool32) * 4)
        return Slot(pool32[i // 4], (i % 4) * 128)

    def pst(n=128):
        i = rr["ps"]
        rr["ps"] = (i + 1) % 8
        return Slot(self.PS[i], 0, n)

    def mm(out, lhsT, rhs, start=True, stop=True, extra_r=()):
        p.op("pe", lambda h: h.matmul(out[0], lhsT[0], rhs[0], start=start, stop=stop),
             reads=[lhsT[1], rhs[1]] + list(extra_r), writes=[out[1]], skip_same=True)

    def sl(s):
        return (s.ap, s.acc)

    wn = "l2_in_w"
    QSC = 128.0 ** -0.5

    def ktok(tt):
        if tt < 4:
            return VB, (8 + tt) * 1024
        return self.kcT, (tt - 4) * 1024

    kflat = self.kcT.t[:, :, :].rearrange("q a b -> q (a b)")
    vflat = VB.t[:, :, :].rearrange("q a b -> q (a b)")

    def ktok_slot(tt, hh):
        if tt < 4:
            a = (8 + tt) * 1024 + hh * 128
            s_ = Slot.__new__(Slot)
            s_.t, s_.a, s_.n = VB, a, 128
            s_.ap = vflat[:, a:a + 128]
            s_.acc = acc(VB, a, a + 128)
            return s_
        a = (tt - 4) * 1024 + hh * 128
        s_ = Slot.__new__(Slot)
        s_.t, s_.a, s_.n = self.kcT, a, 128
        s_.ap = kflat[:, a:a + 128]
        s_.acc = acc(self.kcT, a, a + 128)
        return s_

    def vtok_slot(tt, hh):
        a = tt * 1024 + hh * 128
        s_ = Slot.__new__(Slot)
        s_.t, s_.a, s_.n = VB, a, 128
        s_.ap = vflat[:, a:a + 128]
        s_.acc = acc(VB, a, a + 128)
        return s_

    for b in range(3):
        buf = self.next_block(wn, b * 1024)
        for cl in range(8):
            ft = b * 8 + cl
            xp = [None, None]
            xs = []
            for c in range(2):
                ps = self.proj_tile(buf, cl, c)
                xt = self.TMP[self.nxt("tmp", self.NTMP)]
                p.op("act", lambda h, ps=ps, xt=xt: h.copy(xt.t[:, :], ps.t[:, :]), reads=[acc(ps)], writes=[acc(xt)])
                xs.append(xt)
            w0 = cw.t[:, ft * 3 + 0:ft * 3 + 1]
            w1 = cw.t[:, ft * 3 + 1:ft * 3 + 2]
            w2 = cw.t[:, ft * 3 + 2:ft * 3 + 3]
            nb0 = cw.t[:, 72 + ft * 3 + 0:72 + ft * 3 + 1]
            nb2 = cw.t[:, 72 + ft * 3 + 2:72 + ft * 3 + 3]
            ys = []
            for c in range(2):
                y = self.ST[self.nxt("st", self.NST)]
                x = xs[c]
                xo = xs[1 - c]
                p.op("dve", lambda h, y=y, x=x, w1=w1: h.tensor_scalar(y.t[:, :], x.t[:, :], w1, None, ALU.mult),
                     reads=[acc(x), acc(cw)], writes=[acc(y)])
                p.op("dve", lambda h, y=y, x=x, w0=w0: h.scalar_tensor_tensor(y.t[:, 1:512], x.t[:, 0:511], w0, y.t[:, 1:512], ALU.mult, ALU.add),
                     reads=[acc(x), acc(cw), acc(y)], writes=[acc(y)])
                p.op("dve", lambda h, y=y, x=x, w2=w2: h.scalar_tensor_tensor(y.t[:, 0:511], x.t[:, 1:512], w2, y.t[:, 0:511], ALU.mult, ALU.add),
                     reads=[acc(x), acc(cw), acc(y)], writes=[acc(y)])
                if c == 1:
                    p.op("dve", lambda h, y=y, xo=xo, w0=w0: h.scalar_tensor_tensor(y.t[:, 0:1], xo.t[:, 511:512], w0, y.t[:, 0:1], ALU.mult, ALU.add),
                         reads=[acc(xo), acc(cw), acc(y)], writes=[acc(y)])
                    p.op("dve", lambda h, y=y, xo=xo, nb0=nb0: h.scalar_tensor_tensor(y.t[:, 0:1], xo.t[:, 511:512], nb0, y.t[:, 0:1], ALU.mult, ALU.add),
                         reads=[acc(xo), acc(cw), acc(y)], writes=[acc(y)])
                else:
                    p.op("dve", lambda h, y=y, xo=xo, w2=w2: h.scalar_tensor_tensor(y.t[:, 511:512], xo.t[:, 0:1], w2, y.t[:, 511:512], ALU.mult, ALU.add),
                         reads=[acc(xo), acc(cw), acc(y)], writes=[acc(y)])
                    p.op("dve", lambda h, y=y, xo=xo, nb2=nb2: h.scalar_tensor_tensor(y.t[:, 511:512], xo.t[:, 0:1], nb2, y.t[:, 511:512], ALU.mult, ALU.add),
                         reads=[acc(xo), acc(cw), acc(y)], writes=[acc(y)])
                p.op("dve", lambda h, y=y, x=x, nb0=nb0: h.scalar_tensor_tensor(y.t[:, 256:257], x.t[:, 255:256], nb0, y.t[:, 256:257], ALU.mult, ALU.add),
                     reads=[acc(x), acc(cw), acc(y)], writes=[acc(y)])
                p.op("dve", lambda h, y=y, x=x, nb2=nb2: h.scalar_tensor_tensor(y.t[:, 255:256], x.t[:, 256:257], nb2, y.t[:, 255:256], ALU.mult, ALU.add),
                     reads=[acc(x), acc(cw), acc(y)], writes=[acc(y)])
                ys.append(y)
            for c in range(2):
                y = ys[c]
                lo, hi = c * 512, (c + 1) * 512
                if b == 2:
                    hh = cl
                    v16 = self.PT[self.nxt("pt", self.NPT)]
                    p.op("act", lambda h, y=y, v16=v16: h.activation(v16.t[:, :], y.t[:, :], AF.Silu), reads=[acc(y)], writes=[acc(v16)])
                    for q4 in range(4):
                        tt = c * 4 + q4
                        pt_ = pst()
                        mm(sl(pt_), (v16.t[:, q4 * 128:(q4 + 1) * 128], acc(v16, q4 * 128, (q4 + 1) * 128)), (id16.t[:, :], acc(id16)))
                        vs = vtok_slot(tt, hh)
                        p.op("act", lambda h, vs=vs, pt_=pt_: h.copy(vs.ap, pt_.ap), reads=[pt_.acc], writes=[vs.acc])
                else:
                    hh = cl
                    p.op("act", lambda h, y=y: h.activation(y.t[:, :], y.t[:, :], AF.Silu), reads=[acc(y)], writes=[acc(y)])
                    sq = self.PT[self.nxt("pt", self.NPT)]
                    p.op("act", lambda h, y=y, sq=sq: h.activation(sq.t[:, :], y.t[:, :], AF.Square), reads=[acc(y)], writes=[acc(sq)])
                    pss = self.ps_proj()
                    p.op("pe", lambda h, pss=pss, sq=sq: h.matmul(pss.t[:, :], self.ones.t[:, :], sq.t[:, :], start=True, stop=True),
                         reads=[acc(self.ones), acc(sq)], writes=[acc(pss)], skip_same=True)
                    rs = self.TMP[self.nxt("tmp", self.NTMP)]
                    p.op("act", lambda h, rs=rs, pss=pss: h.activation(rs.t[:, :], pss.t[:, :], AF.Sqrt, bias=self.eps1.t[:, 0:1], scale=1.0),
                         reads=[acc(pss), acc(self.eps1)], writes=[acc(rs)])
                    p.op("dve", lambda h, rs=rs: h.reciprocal(rs.t[:, :], rs.t[:, :]), reads=[acc(rs)], writes=[acc(rs)])
                    dst = self.qT if b == 0 else self.kT
                    sc_ = QSC if b == 0 else 1.0
                    p.op("dve", lambda h, y=y, rs=rs, dst=dst, hh=hh, lo=lo, hi=hi, sc_=sc_: h.scalar_tensor_tensor(
                        dst.t[:, hh, lo:hi], y.t[:, :], sc_, rs.t[:, :], ALU.mult, ALU.mult),
                        reads=[acc(y), acc(rs)], writes=[acc3(dst, [hh], lo, hi)])
                    if b == 1:
                        for q4 in range(4):
                            tt = c * 4 + q4
                            pt_ = pst()
                            mm(sl(pt_), (self.kT.t[:, hh, tt * 128:(tt + 1) * 128], acc3(self.kT, [hh], tt * 128, (tt + 1) * 128)), (id16.t[:, :], acc(id16)))
                            ks = ktok_slot(tt, hh)
                            p.op("act", lambda h, ks=ks, pt_=pt_: h.copy(ks.ap, pt_.ap), reads=[pt_.acc], writes=[ks.acc])
    buf = self.next_block(wn, 3072)
    for cl in range(8):
        for c in range(2):
            lo, hi = c * 512, (c + 1) * 512
            ps = self.proj_tile(buf, cl, c)
            p.op("act", lambda h, ps=ps, cl=cl, lo=lo, hi=hi: h.activation(self.zT.t[:, cl, lo:hi], ps.t[:, :], AF.Silu),
                 reads=[acc(ps)], writes=[acc3(self.zT, [cl], lo, hi)])
    buf = self.next_block(wn, 4096)
    alog_nA = l2s.t[:, 36:52]
    dtb = l2s.t[:, 20:36]
    one1 = l2s.t[:, ONE:ONE + 1]
    for t in range(8):
        gp = pst(32)
        for kt in range(8):
            p.op("pe", lambda h, kt=kt, t=t, gp=gp, buf=buf: h.matmul(gp.ap, self.hT.t[:, kt, t * 128:(t + 1) * 128], buf.t[:, kt, 0:32],
                                                                  start=(kt == 0), stop=(kt == 7)),
                 reads=[acc(buf), acc3(self.hT, [kt], t * 128, (t + 1) * 128)], writes=[gp.acc], skip_same=True)
        beta = gs(G_BETA, t, 0, 16)
        p.op("act", lambda h, beta=beta, gp=gp: h.activation(beta.ap, gp.ap[:, 0:16], AF.Sigmoid), reads=[gp.acc], writes=[beta.acc])
        sc = s32()
        p.op("dve", lambda h, sc=sc, gp=gp: h.tensor_tensor(sc.ap[:, 0:16], gp.ap[:, 16:32], dtb, ALU.add), reads=[gp.acc, acc(l2s)], writes=[sc.acc])
        p.op("act", lambda h, sc=sc: h.activation(sc.ap[:, 16:32], sc.ap[:, 0:16], AF.Exp), reads=[sc.acc], writes=[sc.acc])
        p.op("act", lambda h, sc=sc: h.activation(sc.ap[:, 32:48], sc.ap[:, 16:32], AF.Ln, bias=one1, scale=1.0), reads=[sc.acc, acc(l2s)], writes=[sc.acc])
        p.op("dve", lambda h, sc=sc: h.tensor_tensor(sc.ap[:, 48:64], sc.ap[:, 32:48], alog_nA, ALU.mult), reads=[sc.acc, acc(l2s)], writes=[sc.acc])
        g32 = (sc.ap[:, 48:64], sc.acc)
        pg = pst(32)
        UT, LT = mk(6), mk(7)
        p.op("pe", lambda h, pg=pg, sc=sc, UT=UT: h.matmul(pg.ap[:, 0:8], UT.ap, sc.ap[:, 48:56], start=True, stop=True),
             reads=[UT.acc, sc.acc], writes=[pg.acc], skip_same=True)
        p.op("pe", lambda h, pg=pg, sc=sc, LT=LT: h.matmul(pg.ap[:, 8:16], LT.ap, sc.ap[:, 56:64], start=True, stop=True),
             reads=[LT.acc, sc.acc], writes=[pg.acc], skip_same=True)
        p.op("pe", lambda h, pg=pg, sc=sc: h.matmul(pg.ap[:, 16:32], ones32.t[:, :], sc.ap[:, 48:64], start=True, stop=True),
             reads=[acc(ones32), sc.acc], writes=[pg.acc], skip_same=True)
        gc, ngc, egc, kdec, bg, egt = (gs(i, t, 0, 16) for i in (G_GC, G_NGC, G_EGC, G_KDEC, G_BG, G_EGT))
        p.op("dve", lambda h, gc=gc, pg=pg: h.tensor_copy(gc.ap, pg.ap[:, 0:16]), reads=[pg.acc], writes=[gc.acc])
        p.op("dve", lambda h, gc=gc, ngc=ngc: h.tensor_scalar(ngc.ap, gc.ap, -1.0, None, ALU.mult), reads=[gc.acc], writes=[ngc.acc])
        p.op("act", lambda h, gc=gc, egc=egc: h.activation(egc.ap, gc.ap, AF.Exp), reads=[gc.acc], writes=[egc.acc])
        p.op("act", lambda h, egt=egt, pg=pg: h.activation(egt.ap, pg.ap[:, 16:32], AF.Exp), reads=[pg.acc], writes=[egt.acc])
        p.op("dve", lambda h, kdec=kdec, pg=pg, ngc=ngc: h.tensor_tensor(kdec.ap, pg.ap[:, 16:32], ngc.ap, ALU.add), reads=[pg.acc, ngc.acc], writes=[kdec.acc])
        p.op("act", lambda h, kdec=kdec: h.activation(kdec.ap, kdec.ap, AF.Exp), reads=[kdec.acc], writes=[kdec.acc])
        p.op("dve", lambda h, bg=bg, beta=beta, egc=egc: h.tensor_tensor(bg.ap, beta.ap, egc.ap, ALU.mult), reads=[beta.acc, egc.acc], writes=[bg.acc])

    if self.debug:
        dq = self.outp("dbg_qT", [128, 8, 1024], BF16)
        dk = self.outp("dbg_kT", [128, 8, 1024], BF16)
        dv = self.outp("dbg_VB", [128, 12, 1024], BF16)
        dkc = self.outp("dbg_kcT", [128, 8, 512], BF16)
        dg = self.outp("dbg_GS", [128, 1024], F32)
        dz = self.outp("dbg_zT", [128, 8, 1024], BF16)
        p.dma("sp", dq, self.qT.t[:, :, :], reads=[acc(self.qT)], is_output=True)
        p.dma("sp", dk, self.kT.t[:, :, :], reads=[acc(self.kT)], is_output=True)
        p.dma("sp", dv, VB.t[:, :, :], reads=[acc(VB)], is_output=True)
        p.dma("sp", dkc, self.kcT.t[:, :, :], reads=[acc(self.kcT)], is_output=True)
        p.dma("sp", dg, GS.t[:, :], reads=[acc(GS)], is_output=True)
        p.dma("sp", dz, self.zT.t[:, :, :], reads=[acc(self.zT)], is_output=True)
    s0 = self.inp("l2_s0", [2, 8, 128, 128])
    nst = self.outp("l2_nst", [2, 4, 8, 128, 128])
    keep = l2s.t[:, KEEP:KEEP + 1]
    onw = l2s.t[:, ONW:ONW + 1]

    def col(idx, t, c):
        s_ = gs(idx, t, c, c + 1)
        return s_

    if not hasattr(self, "XF"):
        self.XF = p.tensor("XF", [128, 512], F32, gran=128)
        self.XB = [p.tensor("XB%d" % i, [128, 512], BF16, gran=128) for i in range(4)]
    f_t = self.TMP + self.ST + [self.XF]
    b_t = self.PT + self.XB
    FS = [[Slot(f_t[2 * ci + j // 4], (j % 4) * 128) for j in range(8)] for ci in range(4)]
    BS = [[Slot(b_t[2 * ci + j // 4], (j % 4) * 128) for j in range(8)] for ci in range(4)]
    psrr = [0, 0, 0, 0]

    def cps(ci):
        i = psrr[ci]
        psrr[ci] = 1 - i
        return Slot(self.PS[2 * ci + i], 0, 128)

    def inst_gen(t, d, hh, ci, first):
        c = d * 8 + hh
        tl, th = t * 128, (t + 1) * 128
        F, B = FS[ci], BS[ci]
        KT_ = (self.kT.t[:, hh, tl:th], acc3(self.kT, [hh], tl, th))
        QT_ = (self.qT.t[:, hh, tl:th], acc3(self.qT, [hh], tl, th))
        ktk = ktok_slot(t, hh)
        vtk = vtok_slot(t, hh)
        gcC, ngcC, egcC, kdecC, bgC, egtC, betaC = (col(i, t, c) for i in (G_GC, G_NGC, G_EGC, G_KDEC, G_BG, G_EGT, G_BETA))
        MTi = (M16.t[:, d * 128:(d + 1) * 128], acc(M16))
        MAs = (M16.t[:, (2 + d) * 128:(3 + d) * 128], acc(M16))

        def diag(colslot, dg):
            p.op("dve", lambda h: h.tensor_scalar(dg.ap, id32.t[:, :], colslot.ap, None, ALU.mult),
                 reads=[acc(id32), colslot.acc], writes=[dg.acc])

        def decay(mask, scale, biascol, o_):
            ps_ = cps(ci)
            mm(sl(ps_), (ones32.t[:, :], acc(ones32)), sl(F[0]), start=True, stop=False)
            mm(sl(ps_), (id16.t[:, :], acc(id16)), mask, start=False, stop=True)
            p.op("act", lambda h: h.activation(o_.ap, ps_.ap, AF.Exp, bias=biascol.ap, scale=scale),
                 reads=[ps_.acc, biascol.acc], writes=[o_.acc])

        def rowscaled(colslot, src, dg, o_):
            diag(colslot, dg)
            ps_ = cps(ci)
            mm(sl(ps_), (ones32.t[:, :], acc(ones32)), sl(dg))
            p.op("dve", lambda h: h.tensor_tensor(o_.ap, src[0], ps_.ap, ALU.mult), reads=[src[1], ps_.acc], writes=[o_.acc])

        def masked(lhsT, rhs, dm, o_):
            ps_ = cps(ci)
            mm(sl(ps_), lhsT, rhs)
            p.op("dve", lambda h: h.tensor_tensor(o_.ap, ps_.ap, dm.ap, ALU.mult), reads=[ps_.acc, dm.acc], writes=[o_.acc])

        diag(gcC, F[0])
        yield
        decay(MAs, -1.0, gcC, B[1])
        yield
        psG = cps(ci)
        mm(sl(psG), KT_, KT_)
        p.op("dve", lambda h: h.scalar_tensor_tensor(F[2].ap, psG.ap, betaC.ap, B[1].ap, ALU.mult, ALU.mult),
             reads=[psG.acc, betaC.acc, B[1].acc], writes=[F[2].acc])
        yield
        psPt = cps(ci)
        mm(sl(psPt), sl(F[2]), (id32.t[:, :], acc(id32)))
        p.op("act", lambda h: h.copy(F[3].ap, psPt.ap), reads=[psPt.acc], writes=[F[3].acc])
        yield
        Tt32 = F[6]
        p.op("dve", lambda h: h.tensor_tensor(Tt32.ap, id32.t[:, :], F[3].ap, ALU.subtract), reads=[acc(id32), F[3].acc], writes=[Tt32.acc])
        yield
        cur, nxt_ = (F[2], F[3]), (F[4], F[5])
        for m in (1, 2, 4, 8, 16, 32):
            Am, Pm = cur
            A2, P2 = nxt_
            psA = cps(ci)
            mm(sl(psA), sl(Pm), sl(Am))
            p.op("act", lambda h, A2=A2, psA=psA: h.copy(A2.ap, psA.ap), reads=[psA.acc], writes=[A2.acc])
            yield
            if m < 32:
                psP = cps(ci)
                mm(sl(psP), sl(Am), sl(Pm))
                p.op("dve", lambda h, P2=P2, psP=psP: h.tensor_copy(P2.ap, psP.ap), reads=[psP.acc], writes=[P2.acc])
                yield
            psT = cps(ci)
            mm(sl(psT), sl(A2), sl(Tt32))
            p.op("dve", lambda h, psT=psT: h.tensor_tensor(Tt32.ap, Tt32.ap, psT.ap, ALU.add), reads=[Tt32.acc, psT.acc], writes=[Tt32.acc])
            yield
            cur, nxt_ = nxt_, cur
        Tt16 = B[1]
        p.op("act", lambda h: h.copy(Tt16.ap, Tt32.ap), reads=[Tt32.acc], writes=[Tt16.acc])
        yield
        decay(MTi, 1.0, ngcC, B[0])
        yield
        p.op("dve", lambda h: h.tensor_scalar(B[3].ap, id16.t[:, :], egcC.ap, None, ALU.mult), reads=[acc(id16), egcC.acc], writes=[B[3].acc])
        psR = cps(ci)
        mm(sl(psR), (self.ones.t[:, :], acc(self.ones)), sl(B[3]))
        p.op("dve", lambda h: h.tensor_tensor(B[2].ap, QT_[0], psR.ap, ALU.mult), reads=[QT_[1], psR.acc], writes=[B[2].acc])
        yield
        masked(KT_, QT_, B[0], B[3])
        yield
        Vb, Kbg, Kdec, WT = B[4], B[5], B[6], B[7]
        p.op("pool", lambda h: h.tensor_scalar(Vb.ap, vtk.ap, betaC.ap, None, ALU.mult), reads=[vtk.acc, betaC.acc], writes=[Vb.acc])
        p.op("pool", lambda h: h.tensor_scalar(Kbg.ap, ktk.ap, bgC.ap, None, ALU.mult), reads=[ktk.acc, bgC.acc], writes=[Kbg.acc])
        p.op("pool", lambda h: h.tensor_scalar(Kdec.ap, ktk.ap, kdecC.ap, None, ALU.mult), reads=[ktk.acc, kdecC.acc], writes=[Kdec.acc])
        yield
        psU = cps(ci)
        mm(sl(psU), sl(Tt16), sl(Vb))
        U = F[7]
        p.op("act", lambda h: h.copy(U.ap, psU.ap), reads=[psU.acc], writes=[U.acc])
        yield
        psW = cps(ci)
        mm(sl(psW), sl(Kbg), sl(Tt16))
        p.op("dve", lambda h: h.tensor_copy(WT.ap, psW.ap), reads=[psW.acc], writes=[WT.acc])
        yield
        S16s = Slot(S16, ci * 128)
        S32s = Slot(S32, ci * 128)
        S16ap = (S16.t[:, ci, :], S16s.acc)
        S32ap = S32.t[:, ci, :]
        psWS = cps(ci)
        mm(sl(psWS), sl(WT), S16ap)
        Vn = B[4]
        p.op("dve", lambda h: h.tensor_tensor(Vn.ap, U.ap, psWS.ap, ALU.subtract), reads=[U.acc, psWS.acc], writes=[Vn.acc])
        yield
        psS = cps(ci)
        mm(sl(psS), sl(Kdec), sl(Vn))
        p.op("dve", lambda h: h.scalar_tensor_tensor(S32ap, S32ap, egtC.ap, psS.ap, ALU.mult, ALU.add),
             reads=[S32s.acc, egtC.acc, psS.acc], writes=[S32s.acc])
        yield
        psO = cps(ci)
        mm(sl(psO), S16ap, sl(B[2]), start=True, stop=False)
        mm(sl(psO), sl(Vn), sl(B[3]), start=False, stop=True)
        oslot_ap = self.oT.t[:, hh, tl:th]
        oacc = acc3(self.oT, [hh], tl, th)
        if first:
            p.op("act", lambda h: h.copy(oslot_ap, psO.ap), reads=[psO.acc], writes=[oacc])
            yield
        else:
            ot, rs, sq = F[2], F[3], B[5]
            p.op("dve", lambda h: h.tensor_tensor(ot.ap, psO.ap, oslot_ap, ALU.add), reads=[psO.acc, oacc], writes=[ot.acc])
            p.op("act", lambda h: h.activation(sq.ap, ot.ap, AF.Square), reads=[ot.acc], writes=[sq.acc])
            yield
            pss = cps(ci)
            mm(sl(pss), (self.ones.t[:, :], acc(self.ones)), sl(sq))
            p.op("act", lambda h: h.activation(rs.ap, pss.ap, AF.Sqrt, bias=self.eps1.t[:, 0:1], scale=1.0 / 128.0),
                 reads=[pss.acc, acc(self.eps1)], writes=[rs.acc])
            yield
            p.op("dve", lambda h: h.reciprocal(rs.ap, rs.ap), reads=[rs.acc], writes=[rs.acc])
            p.op("dve", lambda h: h.scalar_tensor_tensor(ot.ap, ot.ap, onw, rs.ap, ALU.mult, ALU.mult),
                 reads=[ot.acc, rs.acc, acc(l2s)], writes=[ot.acc])
            zacc = acc3(self.zT, [hh], tl, th)
            p.op("pool", lambda h: h.tensor_tensor(oslot_ap, ot.ap, self.zT.t[:, hh, tl:th], ALU.mult),
                 reads=[ot.acc, zacc], writes=[oacc])
            yield

    def chain_gen(ci, d, hh):
        S32s = Slot(S32, ci * 128)
        S16s = Slot(S16, ci * 128)
        p.dma("sp", S32.t[:, ci, :], s0[d, hh, :, :], writes=[S32s.acc])
        p.op("dve", lambda h: h.tensor_scalar(S32.t[:, ci, :], S32.t[:, ci, :], keep, None, ALU.mult),
             reads=[S32s.acc, acc(l2s)], writes=[S32s.acc])
        p.op("act", lambda h: h.copy(S16.t[:, ci, :], S32.t[:, ci, :]), reads=[S32s.acc], writes=[S16s.acc])
        yield
        for n in range(8):
            t = n if d == 0 else 7 - n
            yield from inst_gen(t, d, hh, ci, n < 4)
            if n % 2 == 1:
                p.dma("sp", nst[d, t // 2, hh, :, :], S32.t[:, ci, :], reads=[S32s.acc], is_output=True)
                if n < 7:
                    p.op("dve", lambda h: h.tensor_scalar(S32.t[:, ci, :], S32.t[:, ci, :], keep, None, ALU.mult),
                         reads=[S32s.acc, acc(l2s)], writes=[S32s.acc])
            if n < 7:
                p.op("act", lambda h: h.copy(S16.t[:, ci, :], S32.t[:, ci, :]), reads=[S32s.acc], writes=[S16s.acc])
            yield

    for hp in range(4):
        chains = [(d, 2 * hp + e) for e in range(2) for d in range(2)]
        gens = [chain_gen(ci, d, hh) for ci, (d, hh) in enumerate(chains)]
        alive = list(gens)
        while alive:
            for g_ in list(alive):
                try:
                    next(g_)
                except StopIteration:
                    alive.remove(g_)
    if self.debug:
        do = self.outp("dbg_oT", [128, 8, 1024], BF16)
        p.dma("sp", do, self.oT.t[:, :, :], reads=[acc(self.oT)], is_output=True)
    p.dma("sp", self.ropeC.t[:, :], self.inp("ropeC", [128, 1024]), writes=[acc(self.ropeC)])
    p.dma("sp", self.ropeS.t[:, :], self.inp("ropeS", [128, 1024]), writes=[acc(self.ropeS)])
    self.out_proj(li)


Builder.layer_c = layer_c
```
